# Optimizing a Trainium2 kernel written in Bass

```python
import math
import jax, jax.numpy as jnp
from jax import lax
import numpy as np

D_MODEL = 1024
BATCH = 16
SEQ = 4096
DEPTH = 4

PLE_DIM = 256
D_FF = 2816
MIX_WIDTH = D_MODEL
SSM_WIDTH = MIX_WIDTH // 2
POOL_WIDTH = MIX_WIDTH - SSM_WIDTH
SSM_GROUP_CH = 16
SSM_GROUPS = SSM_WIDTH // SSM_GROUP_CH
SSM_STATE = 64
POOL_WINDOWS = (2, 4, 8, 16)
POOL_GROUP_CH = POOL_WIDTH // len(POOL_WINDOWS)
EPS = 1e-6
DT_MIN = 1e-3
DT_MAX = 1e-1

kernel_name = "hybrid_s5_pool_macaron_ple"


def rms_norm(x, g):
    x32 = x.astype(jnp.float32)
    y = x32 * lax.rsqrt(jnp.mean(x32 * x32, axis=-1, keepdims=True) + EPS)
    return (y * g.astype(jnp.float32)).astype(x.dtype)


def swiglu(x, wi, wo):
    gu = x @ wi
    g, u = jnp.split(gu, 2, axis=-1)
    return (jax.nn.silu(g) * u) @ wo


def _ssm_combine(e1, e2):
    a1, b1 = e1
    a2, b2 = e2
    return a1 * a2, a2 * b1 + b2


def s5_mixer(u, lam_re, lam_im, log_dt, b_re, b_im, c_re, c_im, d_skip, w_glu):
    bsz, seq, _ = u.shape
    f32 = jnp.float32
    u32 = u.astype(f32)
    ug = u32.reshape(bsz, seq, SSM_GROUPS, SSM_GROUP_CH)
    lam = lax.complex(lam_re.astype(f32), lam_im.astype(f32))
    dt = jnp.exp(log_dt.astype(f32))[:, None]
    lam_bar = jnp.exp(lam * dt)
    b = lax.complex(b_re.astype(f32), b_im.astype(f32))
    b_bar = ((lam_bar - 1.0) / lam)[..., None] * b
    bu = jnp.einsum('blgh,gph->blgp', ug.astype(jnp.complex64), b_bar)
    a = jnp.broadcast_to(lam_bar[None, None], (1, seq, SSM_GROUPS, SSM_STATE))
    _, states = lax.associative_scan(_ssm_combine, (a, bu), axis=1)
    c = lax.complex(c_re.astype(f32), c_im.astype(f32))
    y = jnp.real(jnp.einsum('blgp,ghp->blgh', states, c)).reshape(bsz, seq, SSM_WIDTH)
    y = y + d_skip.astype(f32) * u32
    y = jax.nn.gelu(y)
    y = y * jax.nn.sigmoid(y @ w_glu.astype(f32))
    return y.astype(u.dtype)


def pool_mixer(u, w_pool, scale):
    bsz, seq, _ = u.shape
    u32 = u.astype(jnp.float32)
    cs = lax.cumsum(u32, axis=1)
    count = jnp.arange(1, seq + 1, dtype=jnp.float32)[:, None]
    outs = []
    for gi, win in enumerate(POOL_WINDOWS):
        sl = slice(gi * POOL_GROUP_CH, (gi + 1) * POOL_GROUP_CH)
        cg = cs[..., sl]
        prev = jnp.pad(cg, ((0, 0), (win, 0), (0, 0)))[:, :seq]
        mean = (cg - prev) / jnp.minimum(count, float(win))
        outs.append((mean - u32[..., sl]) @ w_pool[gi].astype(jnp.float32))
    y = jnp.concatenate(outs, axis=-1) * scale.astype(jnp.float32)
    return y.astype(u.dtype)


def setup_inputs(seed: int = 0) -> dict:
    key = jax.random.key(seed)
    ks = jax.random.split(key, 32)
    f32 = jnp.float32
    nrm = lambda k, shape, s: jax.random.normal(k, shape, f32) * s
    n_idx = jnp.arange(SSM_STATE, dtype=f32)
    lam_re = -0.5 + nrm(ks[0], (DEPTH, SSM_GROUPS, SSM_STATE), 0.01)
    lam_im = math.pi * n_idx[None, None, :] + nrm(ks[1], (DEPTH, SSM_GROUPS, SSM_STATE), 0.01)
    log_dt = jax.random.uniform(ks[2], (DEPTH, SSM_GROUPS), f32, math.log(DT_MIN), math.log(DT_MAX))
    return {
        "x": nrm(ks[3], (BATCH, SEQ, D_MODEL), 1.0),
        "p": nrm(ks[4], (DEPTH, BATCH, SEQ, PLE_DIM), 1.0),
        "ffn1_norm": 1.0 + nrm(ks[5], (DEPTH, D_MODEL), 0.02),
        "ffn1_wi": nrm(ks[6], (DEPTH, D_MODEL, 2 * D_FF), D_MODEL ** -0.5),
        "ffn1_wo": nrm(ks[7], (DEPTH, D_FF, D_MODEL), D_FF ** -0.5),
        "mix_norm": 1.0 + nrm(ks[8], (DEPTH, D_MODEL), 0.02),
        "w_in": nrm(ks[9], (DEPTH, D_MODEL, MIX_WIDTH), D_MODEL ** -0.5),
        "ssm_lambda_re": lam_re,
        "ssm_lambda_im": lam_im,
        "ssm_log_dt": log_dt,
        "ssm_b_re": nrm(ks[10], (DEPTH, SSM_GROUPS, SSM_STATE, SSM_GROUP_CH), (2.0 * SSM_GROUP_CH) ** -0.5),
        "ssm_b_im": nrm(ks[11], (DEPTH, SSM_GROUPS, SSM_STATE, SSM_GROUP_CH), (2.0 * SSM_GROUP_CH) ** -0.5),
        "ssm_c_re": nrm(ks[12], (DEPTH, SSM_GROUPS, SSM_GROUP_CH, SSM_STATE), (2.0 * SSM_STATE) ** -0.5),
        "ssm_c_im": nrm(ks[13], (DEPTH, SSM_GROUPS, SSM_GROUP_CH, SSM_STATE), (2.0 * SSM_STATE) ** -0.5),
        "ssm_d": nrm(ks[14], (DEPTH, SSM_WIDTH), 1.0),
        "ssm_w_glu": nrm(ks[15], (DEPTH, SSM_WIDTH, SSM_WIDTH), SSM_WIDTH ** -0.5),
        "pool_w": nrm(ks[16], (DEPTH, len(POOL_WINDOWS), POOL_GROUP_CH, POOL_GROUP_CH), POOL_GROUP_CH ** -0.5),
        "pool_scale": 1.0 + nrm(ks[17], (DEPTH, POOL_WIDTH), 0.02),
        "w_out": nrm(ks[18], (DEPTH, MIX_WIDTH, D_MODEL), MIX_WIDTH ** -0.5),
        "ffn2_norm": 1.0 + nrm(ks[19], (DEPTH, D_MODEL), 0.02),
        "ffn2_wi": nrm(ks[20], (DEPTH, D_MODEL, 2 * D_FF), D_MODEL ** -0.5),
        "ffn2_wo": nrm(ks[21], (DEPTH, D_FF, D_MODEL), D_FF ** -0.5),
        "ple_norm": 1.0 + nrm(ks[22], (DEPTH, D_MODEL), 0.02),
        "ple_w_gate": nrm(ks[23], (DEPTH, D_MODEL, D_MODEL), D_MODEL ** -0.5),
        "ple_w_proj": nrm(ks[24], (DEPTH, PLE_DIM, D_MODEL), PLE_DIM ** -0.5),
        "final_norm": 1.0 + nrm(ks[25], (D_MODEL,), 0.02),
    }


def reference(x, p, ffn1_norm, ffn1_wi, ffn1_wo, mix_norm, w_in,
              ssm_lambda_re, ssm_lambda_im, ssm_log_dt, ssm_b_re, ssm_b_im, ssm_c_re, ssm_c_im,
              ssm_d, ssm_w_glu, pool_w, pool_scale, w_out,
              ffn2_norm, ffn2_wi, ffn2_wo, ple_norm, ple_w_gate, ple_w_proj, final_norm):
    h = x
    for i in range(DEPTH):
        h = h + 0.5 * swiglu(rms_norm(h, ffn1_norm[i]), ffn1_wi[i], ffn1_wo[i])
        z = rms_norm(h, mix_norm[i]) @ w_in[i]
        y_ssm = s5_mixer(z[..., :SSM_WIDTH], ssm_lambda_re[i], ssm_lambda_im[i], ssm_log_dt[i],
                         ssm_b_re[i], ssm_b_im[i], ssm_c_re[i], ssm_c_im[i], ssm_d[i], ssm_w_glu[i])
        y_pool = pool_mixer(z[..., SSM_WIDTH:], pool_w[i], pool_scale[i])
        h = h + jnp.concatenate([y_ssm, y_pool], axis=-1) @ w_out[i]
        h = h + 0.5 * swiglu(rms_norm(h, ffn2_norm[i]), ffn2_wi[i], ffn2_wo[i])
        gate = jax.nn.sigmoid((rms_norm(h, ple_norm[i]) @ ple_w_gate[i]).astype(jnp.float32))
        h = h + (gate * (p[i] @ ple_w_proj[i]).astype(jnp.float32)).astype(h.dtype)
    return rms_norm(h, final_norm)
```

```python
import contextlib
import math
import numpy as np
import concourse.bass as bass
import concourse.mybir as mybir
from concourse.bass_utils import run_bass_kernel_spmd

F32 = mybir.dt.float32
BF16 = mybir.dt.bfloat16
AF = mybir.ActivationFunctionType
ALU = mybir.AluOpType

ENGS = ("pe", "act", "dve", "pool", "sp")


class Buf:
    __slots__ = ("w", "rs")

    def __init__(self):
        self.w = None
        self.rs = []


class Op:
    __slots__ = ("eng", "fn", "deps", "sig", "val", "dma", "semkey")

    def __init__(self, eng, fn, dma):
        self.eng = eng
        self.fn = fn
        self.deps = []
        self.sig = False
        self.val = 0
        self.dma = dma
        self.semkey = None


class Prog:
    def __init__(self, nc):
        self.nc = nc
        self.ops = {e: [] for e in ENGS}
        self.last_dma = {}

    def add(self, eng, fn, reads=(), writes=(), dma=False, semkey=None, extra=()):
        op = Op(eng, fn, dma)
        op.semkey = semkey
        deps = op.deps
        for b in reads:
            if b.w is not None:
                deps.append(b.w)
        for b in writes:
            if b.w is not None:
                deps.append(b.w)
            deps.extend(b.rs)
        deps.extend(extra)
        for b in reads:
            if not dma and eng != "pool":
                b.rs = [r for r in b.rs if r.eng != eng or r.dma]
            b.rs.append(op)
        for b in writes:
            b.w = op
            b.rs = []
        self.ops[eng].append(op)
        if dma:
            self.last_dma[semkey] = op
        return op

    def barrier(self, dummies):
        lasts = [self.ops[e][-1] for e in ENGS if self.ops[e]] + list(self.last_dma.values())
        for e in ("act", "dve", "pool"):
            d = dummies[e]
            if e == "act":
                self.add(e, (lambda d: lambda h: h.activation(out=d, in_=d, func=AF.Copy))(d), extra=lasts)
            else:
                self.add(e, (lambda d: lambda h: h.memset(d, 0.0))(d), extra=lasts)
        self.add("sp", lambda h: h.dma_start(out=dummies["spo"], in_=dummies["spi"]), extra=lasts, dma=True, semkey="barrier")

    def emit(self, final_waits=()):
        nc = self.nc
        for e in ENGS:
            for op in self.ops[e]:
                seen = set()
                nd = []
                for d in op.deps:
                    if d is op or id(d) in seen:
                        continue
                    seen.add(id(d))
                    if d.eng == "pe" and op.eng == "pe" and not d.dma and not op.dma:
                        continue
                    nd.append(d)
                    d.sig = True
                op.deps = nd
        for op in final_waits:
            op.sig = True
        cnt = {e: 0 for e in ENGS}
        dma_cnt = {}
        for e in ENGS:
            for op in self.ops[e]:
                if op.dma:
                    k = op.semkey
                    dma_cnt[k] = dma_cnt.get(k, 0) + 16
                    op.val = dma_cnt[k]
                elif op.sig:
                    cnt[e] += 1
                    op.val = cnt[e]
        with contextlib.ExitStack() as st:
            sems = {e: st.enter_context(nc.semaphore("s_" + e)) for e in ENGS}
            dsems = {k: st.enter_context(nc.semaphore("d_%s" % str(k))) for k in dma_cnt}
            block = st.enter_context(nc.Block())

            def sem_of(d):
                return dsems[d.semkey] if d.dma else sems[d.eng]

            def run(e, handle):
                waited = {}
                for op in self.ops[e]:
                    for d in op.deps:
                        s = sem_of(d)
                        key = id(s)
                        if waited.get(key, 0) >= d.val:
                            continue
                        handle.wait_ge(s, d.val)
                        waited[key] = d.val
                    ins = op.fn(handle)
                    if op.dma:
                        ins.then_inc(dsems[op.semkey], 16)
                    elif op.sig:
                        ins.then_inc(sems[e], 1)
                if e == "sp":
                    for d in final_waits:
                        handle.wait_ge(sem_of(d), d.val)

            @block.tensor
            def _(h):
                run("pe", h)

            @block.scalar
            def _(h):
                run("act", h)

            @block.vector
            def _(h):
                run("dve", h)

            @block.gpsimd
            def _(h):
                run("pool", h)

            @block.sync
            def _(h):
                run("sp", h)


D = 1024
KD = 8
DFF = 2816
NJ = 22
TT = 1024
SUB = 512
NSUB = 2
TC = 8
NCS = SUB // TC
DEPTH = 4
EPS = 1e-6
NRING = 5
RINGW = 2816
POOL_WINS = (2, 4, 8, 16)

WSHAPES = [
    ("ffn1_norm", [DEPTH, D]), ("ffn1_wi", [DEPTH, D, 2 * DFF]), ("ffn1_wo", [DEPTH, DFF, D]),
    ("mix_norm", [DEPTH, D]), ("w_in", [DEPTH, D, D]),
    ("ssm_lambda_re", [DEPTH, 32, 64]), ("ssm_lambda_im", [DEPTH, 32, 64]), ("ssm_log_dt", [DEPTH, 32]),
    ("ssm_b_re", [DEPTH, 32, 64, 16]), ("ssm_b_im", [DEPTH, 32, 64, 16]),
    ("ssm_c_re", [DEPTH, 32, 16, 64]), ("ssm_c_im", [DEPTH, 32, 16, 64]),
    ("ssm_d", [DEPTH, 512]), ("ssm_w_glu", [DEPTH, 512, 512]),
    ("pool_w", [DEPTH, 4, 128, 128]), ("pool_scale", [DEPTH, 512]), ("w_out", [DEPTH, D, D]),
    ("ffn2_norm", [DEPTH, D]), ("ffn2_wi", [DEPTH, D, 2 * DFF]), ("ffn2_wo", [DEPTH, DFF, D]),
    ("ple_norm", [DEPTH, D]), ("ple_w_gate", [DEPTH, D, D]), ("ple_w_proj", [DEPTH, 256, D]),
    ("final_norm", [D]),
]
NCONST = 128 + 64 + 64 + 64


def host_consts():
    c = np.zeros((128, NCONST), np.float32)
    c[:, 0:128] = np.eye(128, dtype=np.float32)
    for g2 in range(2):
        for q4 in range(4):
            for h in range(16):
                c[(2 * q4 + g2) * 16 + h, 128 + 64 * g2 + q4 * 16 + h] = 1.0
    for gi, w in enumerate(POOL_WINS):
        for t in range(16):
            c[:, 256 + gi * 16 + t] = 1.0 / min(t + 1, w)
    return c


def build(n_seq, seq_len, depth, flags=frozenset()):
    NTOK = n_seq * seq_len
    NTILE = seq_len // TT
    nc = bass.Bass("TRN2", target_bir_lowering=False)
    dram = {}
    dram["xT"] = nc.dram_tensor("xT", [D, NTOK], F32, kind="ExternalInput").ap()
    dram["pT"] = nc.dram_tensor("pT", [DEPTH, 256, NTOK], F32, kind="ExternalInput").ap()
    dram["consts"] = nc.dram_tensor("consts", [128, NCONST], F32, kind="ExternalInput").ap()
    for name, shp in WSHAPES:
        dram[name] = nc.dram_tensor(name, shp, F32, kind="ExternalInput").ap()
    outT = nc.dram_tensor("outT", [D, NTOK], F32, kind="ExternalOutput").ap()
    kfir_s = nc.dram_tensor("kfir_s", [DEPTH, 128, 4 * 8 * 128], BF16, kind="Internal").ap()
    bst_s = nc.dram_tensor("bst_s", [DEPTH, 128, 4 * 8 * 2 * 128], BF16, kind="Internal").ap()
    cst_s = nc.dram_tensor("cst_s", [DEPTH, 128, 16 * 8 * 2 * 32], BF16, kind="Internal").ap()
    rot_s = nc.dram_tensor("rot_s", [DEPTH, 128, 2 * 16 * 64], F32, kind="Internal").ap()

    dbg = None
    if "dbg" in flags:
        dbg = nc.dram_tensor("dbg", [128, 40960], F32, kind="ExternalOutput").ap()
    dbg_dmas = []
    dbg_i = [0]
    P = Prog(nc)
    NW = 53100

    with contextlib.ExitStack() as st:
        big = st.enter_context(nc.sbuf_tensor("big", [128, NW], F32))[:]
        banks = [st.enter_context(nc.psum_tensor("bank%d" % i, [128, 512], F32))[:] for i in range(8)]
        bank_bufs = [Buf() for _ in range(8)]
        bank_i = [0]

        def next_bank():
            i = bank_i[0] % 8
            bank_i[0] += 1
            return banks[i], bank_bufs[i]

        off = [0]

        def alloc(nelem, dt=F32):
            words = nelem if dt == F32 else (nelem + 1) // 2
            a = off[0]
            off[0] += words
            assert off[0] <= NW, ("SBUF arena overflow", off[0])
            v = big[:, a:a + words]
            return v if dt == F32 else v.bitcast(dt)

        def op(eng, method, reads, writes, *args, **kw):
            return P.add(eng, lambda h: getattr(h, method)(*args, **kw), reads, writes)

        def dma(q, out, in_, reads, writes, semkey):
            return P.add(q, lambda h: h.dma_start(out=out, in_=in_), reads, writes, dma=True, semkey=semkey)

        def dump(ap2d, col, reads, q="sp"):
            if dbg is None:
                return
            n_ = ap2d.shape[1]
            dbg_i[0] += 1
            dbg_dmas.append(dma(q, dbg[:, col:col + n_], ap2d, reads, [Buf()], "dbg%d" % dbg_i[0]))

        def mm(out, lhsT, rhs, start, stop, reads, writes, tp=None):
            kw = dict(start=start, stop=stop, skip_group_check=True)
            if tp is not None:
                kw["tile_position"] = tp
            return P.add("pe", lambda h: h.matmul(out, lhsT=lhsT, rhs=rhs, **kw), reads, writes)

        cst = alloc(NCONST)
        B_cst = Buf()
        ident = cst[:, 0:128]
        sel = [cst[:, 128:192], cst[:, 192:256]]
        invc = cst[:, 256:320].rearrange("p (g t) -> p g t", t=16)
        dma("sp", cst, dram["consts"], [], [B_cst], "cst")
        misc = alloc(16)
        B_misc = Buf()
        op("dve", "memset", [], [B_misc], misc[:, 0:1], math.pi / 2)
        op("dve", "memset", [], [B_misc], misc[:, 1:2], EPS)
        op("dve", "memset", [], [B_misc], misc[:, 2:3], 0.0)
        halfpi = misc[:, 0:1]
        epsc = misc[:, 1:2]
        ones_bf = alloc(128, BF16)
        B_ones = Buf()
        op("dve", "memset", [], [B_ones], ones_bf, 1.0)
        gains = alloc(4 * DEPTH * 8)
        B_gains = Buf()
        gfin = alloc(8)
        dsk = alloc(DEPTH * 4)
        psc = alloc(DEPTH * 4)
        RR = alloc(DEPTH * 16)
        B_RR = Buf()
        carry = alloc(DEPTH * 16 * 2)
        carry_v = carry.rearrange("p (l q r) -> p l q r", l=DEPTH, r=2)
        B_carry = [[Buf() for _ in range(4)] for _ in range(DEPTH)]
        hist = alloc(DEPTH * 4 * 16).rearrange("p (l g t) -> p l g t", l=DEPTH, g=4)
        B_hist = [[Buf() for _ in range(4)] for _ in range(DEPTH)]
        dummies = {e: alloc(2) for e in ("act", "dve", "pool", "spo", "spi")}
        op("dve", "memset", [], [], dummies["spi"], 0.0)
        op("dve", "memset", [], [], dummies["act"], 0.0)
        persist_end = off[0]

        stage = alloc(128)
        B_stage = Buf()

        def load_T(rows_ap, R, dst, B_dst, evac="dve"):
            dma("sp", stage[0:R, :], rows_ap, [], [B_stage], "stage")
            ps, bps = next_bank()
            mm(ps[:, 0:R], stage[0:R, :], ident[0:R, 0:R], True, True, [B_stage, B_cst], [bps])
            op(evac, "tensor_copy", [bps], [B_dst], out=dst, in_=ps[:, 0:R])

        for ki, nm in enumerate(("ffn1_norm", "mix_norm", "ffn2_norm", "ple_norm")):
            load_T(dram[nm].rearrange("l (k p) -> (l k) p", p=128), DEPTH * 8,
                   gains[:, ki * DEPTH * 8:(ki + 1) * DEPTH * 8], B_gains)
        load_T(dram["final_norm"].rearrange("(k p) -> k p", p=128), 8, gfin, B_gains)
        load_T(dram["ssm_d"].rearrange("l (b p) -> (l b) p", p=128), DEPTH * 4, dsk, B_gains)
        load_T(dram["pool_scale"].rearrange("l (b p) -> (l b) p", p=128), DEPTH * 4, psc, B_gains)
        gains_v = gains.rearrange("p (n l k) -> p n l k", n=4, l=DEPTH)

        LQ = DEPTH * 16

        def t64():
            return alloc(LQ), Buf()

        lr, B_lr = t64()
        li, B_li = t64()
        ldt, B_ldt = t64()
        load_T(dram["ssm_lambda_re"].rearrange("l (q g) p -> (l q) (g p)", g=2), LQ, lr, B_lr)
        load_T(dram["ssm_lambda_im"].rearrange("l (q g) p -> (l q) (g p)", g=2), LQ, li, B_li)
        ld2 = alloc(2)
        B_ld2 = Buf()
        dma("sp", ld2[0:LQ, :], dram["ssm_log_dt"].rearrange("l (q g) -> (l q) g", g=2), [], [B_ld2], "ld2")
        stage2 = alloc(128)
        B_stage2 = Buf()
        for g2 in range(2):
            op("dve", "tensor_copy", [B_ld2], [B_stage2], out=stage2[0:LQ, g2 * 64:(g2 + 1) * 64],
               in_=ld2[0:LQ, g2:g2 + 1].to_broadcast([LQ, 64]))
        ps, bps = next_bank()
        mm(ps[:, 0:LQ], stage2[0:LQ, :], ident[0:LQ, 0:LQ], True, True, [B_stage2, B_cst], [bps])
        op("dve", "tensor_copy", [bps], [B_ldt], out=ldt, in_=ps[:, 0:LQ])

        def tt(eng, out, a, b, o, reads, writes):
            return op(eng, "tensor_tensor", reads, writes, out=out, in0=a, in1=b, op=o)

        dt_, B_dt = t64()
        op("act", "activation", [B_ldt], [B_dt], out=dt_, in_=ldt, func=AF.Exp)
        mr, B_mr = t64()
        mi, B_mi = t64()
        tt("dve", mr, lr, dt_, ALU.mult, [B_lr, B_dt], [B_mr])
        tt("dve", mi, li, dt_, ALU.mult, [B_li, B_dt], [B_mi])
        em, B_em = t64()
        op("act", "activation", [B_mr], [B_em], out=em, in_=mr, func=AF.Exp)
        op("act", "activation", [B_mr], [B_RR], out=RR, in_=mr, func=AF.Exp, scale=float(TC))
        cu, B_cu = t64()
        su, B_su = t64()
        op("act", "activation", [B_mi], [B_su], out=su, in_=mi, func=AF.Sin, scale=1.0 / 64)
        op("act", "activation", [B_mi, B_misc], [B_cu], out=cu, in_=mi, func=AF.Sin, scale=1.0 / 64, bias=halfpi)
        ta, B_ta = t64()
        tb, B_tb = t64()

        def csquare(c, Bc, s, Bs):
            tt("dve", ta, c, c, ALU.mult, [Bc], [B_ta])
            tt("dve", tb, s, s, ALU.mult, [Bs], [B_tb])
            op("dve", "scalar_tensor_tensor", [Bc, Bs], [Bs], out=s, in0=c, scalar=2.0, in1=s,
               op0=ALU.mult, op1=ALU.mult)
            tt("dve", c, ta, tb, ALU.subtract, [B_ta, B_tb], [Bc])

        for _ in range(6):
            csquare(cu, B_cu, su, B_su)
        lbr, B_lbr = t64()
        lbi, B_lbi = t64()
        tt("dve", lbr, em, cu, ALU.mult, [B_em, B_cu], [B_lbr])
        tt("dve", lbi, em, su, ALU.mult, [B_em, B_su], [B_lbi])
        a1, B_a1 = t64()
        op("dve", "tensor_scalar", [B_lbr], [B_a1], out=a1, in0=lbr, scalar1=-1.0, scalar2=None, op0=ALU.add)
        inv, B_inv = t64()
        tt("dve", ta, lr, lr, ALU.mult, [B_lr], [B_ta])
        tt("dve", tb, li, li, ALU.mult, [B_li], [B_tb])
        tt("dve", ta, ta, tb, ALU.add, [B_ta, B_tb], [B_ta])
        op("dve", "reciprocal", [B_ta], [B_inv], out=inv, in_=ta)
        cr, B_cr = t64()
        ci, B_ci = t64()
        tt("dve", ta, a1, lr, ALU.mult, [B_a1, B_lr], [B_ta])
        tt("dve", tb, lbi, li, ALU.mult, [B_lbi, B_li], [B_tb])
        tt("dve", ta, ta, tb, ALU.add, [B_ta, B_tb], [B_ta])
        tt("dve", cr, ta, inv, ALU.mult, [B_ta, B_inv], [B_cr])
        tt("dve", ta, lbi, lr, ALU.mult, [B_lbi, B_lr], [B_ta])
        tt("dve", tb, a1, li, ALU.mult, [B_a1, B_li], [B_tb])
        tt("dve", ta, ta, tb, ALU.subtract, [B_ta, B_tb], [B_ta])
        tt("dve", ci, ta, inv, ALU.mult, [B_ta, B_inv], [B_ci])
        Er = alloc(9 * LQ).rearrange("p (k q) -> p k q", k=9)
        Ei = alloc(9 * LQ).rearrange("p (k q) -> p k q", k=9)
        B_E = Buf()
        op("dve", "memset", [], [B_E], Er[:, 0, :], 1.0)
        op("dve", "memset", [], [B_E], Ei[:, 0, :], 0.0)
        op("dve", "tensor_copy", [B_lbr], [B_E], out=Er[:, 1, :], in_=lbr)
        op("dve", "tensor_copy", [B_lbi], [B_E], out=Ei[:, 1, :], in_=lbi)
        for k in range(2, 9):
            tt("dve", ta, Er[:, k - 1, :], lbr, ALU.mult, [B_E, B_lbr], [B_ta])
            tt("dve", tb, Ei[:, k - 1, :], lbi, ALU.mult, [B_E, B_lbi], [B_tb])
            tt("dve", Er[:, k, :], ta, tb, ALU.subtract, [B_ta, B_tb], [B_E])
            tt("dve", ta, Er[:, k - 1, :], lbi, ALU.mult, [B_E, B_lbi], [B_ta])
            tt("dve", tb, Ei[:, k - 1, :], lbr, ALU.mult, [B_E, B_lbr], [B_tb])
            tt("dve", Ei[:, k, :], ta, tb, ALU.add, [B_ta, B_tb], [B_E])
        for _ in range(3):
            csquare(cu, B_cu, su, B_su)
        tabc = alloc(LQ * NCS).rearrange("p (q c) -> p q c", c=NCS)
        tabs = alloc(LQ * NCS).rearrange("p (q c) -> p q c", c=NCS)
        B_tab = Buf()
        op("dve", "tensor_copy", [B_cu], [B_tab], out=tabc[:, :, 0], in_=cu)
        op("dve", "tensor_copy", [B_su], [B_tab], out=tabs[:, :, 0], in_=su)
        tw1 = alloc(LQ * 32).rearrange("p (q c) -> p q c", c=32)
        tw2 = alloc(LQ * 32).rearrange("p (q c) -> p q c", c=32)
        B_tw1, B_tw2 = Buf(), Buf()
        n = 1
        while n < NCS:
            Ac, As = tabc[:, :, 0:n], tabs[:, :, 0:n]
            Bc = tabc[:, :, n - 1:n].to_broadcast([128, LQ, n])
            Bs = tabs[:, :, n - 1:n].to_broadcast([128, LQ, n])
            tt("dve", tw1[:, :, 0:n], Ac, Bc, ALU.mult, [B_tab], [B_tw1])
            tt("dve", tw2[:, :, 0:n], As, Bs, ALU.mult, [B_tab], [B_tw2])
            tt("dve", tabc[:, :, n:2 * n], tw1[:, :, 0:n], tw2[:, :, 0:n], ALU.subtract, [B_tw1, B_tw2], [B_tab])
            tt("dve", tw1[:, :, 0:n], Ac, Bs, ALU.mult, [B_tab], [B_tw1])
            tt("dve", tw2[:, :, 0:n], As, Bc, ALU.mult, [B_tab], [B_tw2])
            tt("dve", tabs[:, :, n:2 * n], tw1[:, :, 0:n], tw2[:, :, 0:n], ALU.add, [B_tw1, B_tw2], [B_tab])
            n *= 2
        for l in range(depth):
            rv = rot_s[l].rearrange("p (r q c) -> p r q c", r=2, q=16)
            dma("sp", rv[:, 0], tabc[:, l * 16:(l + 1) * 16, :], [B_tab], [Buf()], "rot%d" % l)
            dma("sp", rv[:, 1], tabs[:, l * 16:(l + 1) * 16, :], [B_tab], [Buf()], "rot%d" % l)

        def t3(n_, dt=F32):
            return alloc(n_, dt), Buf()

        braw = [t3(256), t3(256)]
        Bm = [t3(512), t3(512)]
        Cm = [t3(512), t3(512)]
        X2 = [t3(256), t3(256)]
        tmpA, B_tmpA = t3(512)
        tmpB, B_tmpB = t3(512)
        EB = [t3(8 * 512), t3(8 * 512)]
        CSt, B_CSt = t3(16 * 8 * 2 * 32, BF16)
        BSt, B_BSt = t3(4 * 8 * 2 * 128, BF16)
        KFt, B_KFt = t3(4 * 8 * 128, BF16)
        Lexp = [t3(16 * 128), t3(16 * 128)]
        Cexp = [t3(16 * 128), t3(16 * 128)]
        for (tl, bl) in Bm + Cm + Lexp + Cexp:
            op("pool", "memset", [], [bl], tl, 0.0)
        CSv = CSt.rearrange("p (q j r c) -> p q j r c", q=16, j=8, r=2)
        BSv = BSt.rearrange("p (b j r c) -> p b j r c", b=4, j=8, r=2)
        KFv = KFt.rearrange("p (b k c) -> p b k c", b=4, k=8)

        def bc32(ap16):
            return ap16.unsqueeze(2).to_broadcast([128, 16, 32])

        for l in range(depth):
            qs = slice(l * 16, (l + 1) * 16)
            for ri, nm in enumerate(("ssm_b_re", "ssm_b_im")):
                dma("sp", braw[ri][0].rearrange("p (q h) -> p q h", h=16),
                    dram[nm][l].rearrange("(q g) p h -> (g p) q h", g=2), [], [braw[ri][1]], "braw%d" % ri)
            crb = cr[:, qs].unsqueeze(2).to_broadcast([128, 16, 16])
            cib = ci[:, qs].unsqueeze(2).to_broadcast([128, 16, 16])
            bre = braw[0][0].rearrange("p (q h) -> p q h", h=16)
            bim = braw[1][0].rearrange("p (q h) -> p q h", h=16)
            tA = tmpA[:, 0:256].rearrange("p (q h) -> p q h", h=16)
            tB = tmpB[:, 0:256].rearrange("p (q h) -> p q h", h=16)
            tC = tmpA[:, 256:512].rearrange("p (q h) -> p q h", h=16)
            for ri in range(2):
                if ri == 0:
                    tt("dve", tA, bre, crb, ALU.mult, [braw[0][1], B_cr], [B_tmpA])
                    tt("dve", tB, bim, cib, ALU.mult, [braw[1][1], B_ci], [B_tmpB])
                    tt("dve", tC, tA, tB, ALU.subtract, [B_tmpA, B_tmpB], [B_tmpA])
                else:
                    tt("dve", tA, bim, crb, ALU.mult, [braw[1][1], B_cr], [B_tmpA])
                    tt("dve", tB, bre, cib, ALU.mult, [braw[0][1], B_ci], [B_tmpB])
                    tt("dve", tC, tA, tB, ALU.add, [B_tmpA, B_tmpB], [B_tmpA])
                bmv = Bm[ri][0].rearrange("p (q g h) -> p q g h", g=2, h=16)
                op("dve", "tensor_copy", [B_tmpA], [Bm[ri][1]], out=bmv[0:64, :, 0, :], in_=tC[0:64])
                op("dve", "tensor_copy", [B_tmpA], [Bm[ri][1]], out=bmv[64:128, :, 1, :], in_=tC[64:128])
            for ri, nm in enumerate(("ssm_c_re", "ssm_c_im")):
                x2v = X2[ri][0].rearrange("p (i c) -> p i c", c=64)
                dma("sp", x2v, dram[nm][l].rearrange("(i g) h p -> (g h) i p", g=8), [], [X2[ri][1]], "x2%d" % ri)
                cmv = Cm[ri][0].rearrange("p (q g h) -> p q g h", g=2, h=16)
                for i in range(4):
                    ps, bps = next_bank()
                    mm(ps[0:64, 0:64], x2v[:, i, :], sel[0], True, True, [X2[ri][1], B_cst], [bps])
                    mm(ps[64:128, 0:64], x2v[:, i, :], sel[1], True, True, [X2[ri][1], B_cst], [bps], tp=(0, 64))
                    pv = ps[:, 0:64].rearrange("p (q h) -> p q h", h=16)
                    op("dve", "tensor_copy", [bps], [Cm[ri][1]], out=cmv[0:64, 4 * i:4 * i + 4, 0, :], in_=pv[0:64])
                    op("dve", "tensor_copy", [bps], [Cm[ri][1]], out=cmv[64:128, 4 * i:4 * i + 4, 1, :], in_=pv[64:128])
            Bmr, Bmi = [Bm[r][0].rearrange("p (q c) -> p q c", c=32) for r in range(2)]
            Cmr, Cmi = [Cm[r][0].rearrange("p (q c) -> p q c", c=32) for r in range(2)]
            tAv = tmpA.rearrange("p (q c) -> p q c", c=32)
            tBv = tmpB.rearrange("p (q c) -> p q c", c=32)
            EBr = EB[0][0].rearrange("p (k q c) -> p k q c", k=8, c=32)
            EBi = EB[1][0].rearrange("p (k q c) -> p k q c", k=8, c=32)
            for k in range(8):
                e = "dve" if k % 2 == 0 else "pool"
                er, ei = bc32(Er[:, k, qs]), bc32(Ei[:, k, qs])
                tt(e, tAv, Bmr, er, ALU.mult, [Bm[0][1], B_E], [B_tmpA])
                tt(e, tBv, Bmi, ei, ALU.mult, [Bm[1][1], B_E], [B_tmpB])
                tt(e, EBr[:, k], tAv, tBv, ALU.subtract, [B_tmpA, B_tmpB], [EB[0][1]])
                tt(e, tAv, Bmi, er, ALU.mult, [Bm[1][1], B_E], [B_tmpA])
                tt(e, tBv, Bmr, ei, ALU.mult, [Bm[0][1], B_E], [B_tmpB])
                tt(e, EBi[:, k], tAv, tBv, ALU.add, [B_tmpA, B_tmpB], [EB[1][1]])
            for j in range(8):
                e = "dve"
                er, ei = bc32(Er[:, j + 1, qs]), bc32(Ei[:, j + 1, qs])
                tt(e, tAv, Cmr, er, ALU.mult, [Cm[0][1], B_E], [B_tmpA])
                tt(e, tBv, Cmi, ei, ALU.mult, [Cm[1][1], B_E], [B_tmpB])
                tt(e, CSv[:, :, j, 0, :], tAv, tBv, ALU.subtract, [B_tmpA, B_tmpB], [B_CSt])
                tt(e, tAv, Cmr, ei, ALU.mult, [Cm[0][1], B_E], [B_tmpA])
                tt(e, tBv, Cmi, er, ALU.mult, [Cm[1][1], B_E], [B_tmpB])
                op(e, "scalar_tensor_tensor", [B_tmpA, B_tmpB], [B_CSt], out=CSv[:, :, j, 1, :], in0=tAv, scalar=-1.0,
                   in1=tBv, op0=ALU.mult, op1=ALU.subtract)
            dma("sp", cst_s[l], CSt, [B_CSt], [Buf()], "cst_s")
            for b in range(4):
                for j0 in range(0, 8, 2):
                    ps, bps = next_bank()
                    for jj in range(2):
                        for ri in range(2):
                            src = EB[ri][0].rearrange("p (k c) -> p k c", k=8)[:, 7 - (j0 + jj), b * 128:(b + 1) * 128]
                            sl = (jj * 2 + ri) * 128
                            mm(ps[:, sl:sl + 128], src, ident, jj == 0 and ri == 0, False, [EB[ri][1], B_cst], [bps])
                    op("act", "activation", [bps], [B_BSt], out=BSv[:, b, j0:j0 + 2].rearrange("p j r c -> p (j r c)"),
                       in_=ps, func=AF.Copy)
            dma("sp", bst_s[l], BSt, [B_BSt], [Buf()], "bst_s")
            for ri in range(2):
                cev = Cexp[ri][0].rearrange("p (b q c) -> p b q c", b=4, q=4)
                cmv4 = Cm[ri][0].rearrange("p (b q c) -> p b q c", b=4, q=4)
                for q4 in range(4):
                    if ri == 0:
                        op("pool", "tensor_copy", [Cm[ri][1]], [Cexp[ri][1]], out=cev[:, :, q4, 32 * q4:32 * q4 + 32],
                           in_=cmv4[:, :, q4, :])
                    else:
                        op("pool", "tensor_scalar", [Cm[ri][1]], [Cexp[ri][1]], out=cev[:, :, q4, 32 * q4:32 * q4 + 32],
                           in0=cmv4[:, :, q4, :], scalar1=-1.0, scalar2=None, op0=ALU.mult)
            for k in range(8):
                for ri in range(2):
                    lev = Lexp[ri][0].rearrange("p (b q c) -> p b q c", b=4, q=4)
                    ebv = EB[ri][0].rearrange("p (k b q c) -> p k b q c", k=8, b=4, q=4)
                    for q4 in range(4):
                        op("pool" if q4 % 2 else "dve", "tensor_copy", [EB[ri][1]], [Lexp[ri][1]],
                           out=lev[:, :, q4, 32 * q4:32 * q4 + 32], in_=ebv[:, k, :, q4, :])
                ps, bps = next_bank()
                first = True
                for b in range(4):
                    for q4 in range(4):
                        for ri in range(2):
                            lq = Lexp[ri][0].rearrange("p (q c) -> p q c", c=128)[:, 4 * b + q4, :]
                            cq = Cexp[ri][0].rearrange("p (q c) -> p q c", c=128)[:, 4 * b + q4, :]
                            mm(ps[:, b * 128:(b + 1) * 128], lq, cq, first, False, [Lexp[ri][1], Cexp[ri][1]], [bps])
                            first = False
                op("act", "activation", [bps], [B_KFt], out=KFv[:, :, k, :], in_=ps.rearrange("p (b c) -> p b c", b=4),
                   func=AF.Copy)
            dma("sp", kfir_s[l], KFt, [B_KFt], [Buf()], "kfir_s")

        dump(RR, 0, [B_RR])
        dump(cr, 64, [B_cr])
        dump(ci, 128, [B_ci])
        dump(Er.rearrange("p k q -> p (k q)"), 192, [B_E])
        dump(Ei.rearrange("p k q -> p (k q)"), 768, [B_E])
        dump(tabc[:, 0:16, :].rearrange("p q c -> p (q c)"), 1344, [B_tab])
        dump(tabs[:, 0:16, :].rearrange("p q c -> p (q c)"), 2368, [B_tab])
        P.barrier(dummies)

        off[0] = persist_end
        h = alloc(KD * TT).rearrange("p (k t) -> p k t", k=KD)
        B_h = [[Buf() for _ in range(NSUB)] for _ in range(KD)]
        xn = alloc(KD * TT, BF16).rearrange("p (k t) -> p k t", k=KD)
        B_xn = [Buf() for _ in range(NSUB)]
        ring = [alloc(RINGW, BF16) for _ in range(NRING)]
        B_ring = [[Buf(), Buf()] for _ in range(NRING)]
        KF = alloc(4 * 8 * 128, BF16)
        BS = alloc(4 * 8 * 2 * 128, BF16)
        CS = alloc(16 * 8 * 2 * 32, BF16)
        ROT = alloc(2 * 16 * 64)
        B_sc = Buf()
        KFm = KF.rearrange("p (b k c) -> p b k c", b=4, k=8)
        BSm = BS.rearrange("p (b j r c) -> p b j r c", b=4, j=8, r=2)
        CSm = CS.rearrange("p (q j r c) -> p q j r c", q=16, j=8, r=2)
        ROTm = ROT.rearrange("p (r q c) -> p r q c", r=2, q=16)
        sq = [alloc(SUB, BF16) for _ in range(4)]
        B_sq = [Buf() for _ in range(4)]
        sq_i = [0]
        rstd = [alloc(SUB) for _ in range(2)]
        B_rstd = [Buf() for _ in range(2)]
        sgt = [alloc(SUB) for _ in range(3)]
        B_sgt = [Buf() for _ in range(3)]
        sgt_i = [0]
        pb = alloc(2 * TT, BF16).rearrange("p (k t) -> p k t", k=2)
        B_pb = Buf()
        u_start = off[0]
        act = alloc(NJ * TT, BF16).rearrange("p (j t) -> p j t", j=NJ)
        B_act = [[Buf() for _ in range(NSUB)] for _ in range(NJ)]
        u_end = off[0]
        off[0] = u_start
        zs_f = alloc(4 * SUB).rearrange("p (b t) -> p b t", b=4)
        B_zsf = [Buf() for _ in range(4)]
        zs_bf = alloc(4 * SUB, BF16).rearrange("p (b t) -> p b t", b=4)
        B_zsb = [Buf() for _ in range(4)]
        zp_f = alloc(4 * (SUB + 16)).rearrange("p (g t) -> p g t", g=4)
        B_zp = [Buf() for _ in range(4)]
        ptmp = [alloc(SUB + 16) for _ in range(3)]
        B_ptmp = [Buf() for _ in range(3)]
        d_bf = alloc(4 * SUB, BF16).rearrange("p (g t) -> p g t", g=4)
        B_dbf = [Buf() for _ in range(4)]
        Vt = [alloc(4 * 2 * NCS).rearrange("p (q r c) -> p q r c", q=4, r=2) for _ in range(2)]
        Wt = [alloc(4 * 2 * NCS).rearrange("p (q r c) -> p q r c", q=4, r=2) for _ in range(2)]
        SFt = [alloc(4 * 2 * (NCS + 1)).rearrange("p (q r c) -> p q r c", q=4, r=2) for _ in range(2)]
        rtmp = [alloc(4 * NCS).rearrange("p (q c) -> p q c", q=4) for _ in range(2)]
        sbf_all = alloc(16 * 2 * NCS, BF16).rearrange("p (q r c) -> p q r c", q=16, r=2)
        B_sbq = [Buf() for _ in range(4)]
        B_V = [Buf() for _ in range(2)]
        B_W = [Buf() for _ in range(2)]
        B_SF = [Buf() for _ in range(2)]
        B_rt = [Buf() for _ in range(2)]
        ys = [alloc(SUB) for _ in range(2)]
        B_ys = [Buf() for _ in range(2)]
        gt = [alloc(SUB) for _ in range(2)]
        B_gt = [Buf() for _ in range(2)]
        gy_bf = alloc(4 * SUB, BF16).rearrange("p (b t) -> p b t", b=4)
        B_gyb = [Buf() for _ in range(4)]
        assert off[0] <= NW
        off[0] = max(off[0], u_end)
        ostage = big[:, u_start:u_start + KD * SUB].rearrange("p (k t) -> p k t", k=KD)
        B_ost = Buf()

        def act_region_bufs():
            r = []
            for row in B_act:
                r.extend(row)
            return r

        mixer_tmp_bufs = (B_zsf + B_zsb + B_zp + B_ptmp + B_dbf + B_V + B_W + B_SF + B_rt + B_sbq + B_ys + B_gt
                          + B_gyb + [B_ost])

        plan = []
        for s in range(n_seq):
            for ti in range(NTILE):
                for l in range(depth):
                    for j in range(NJ):
                        plan.append(("wi", 0, l, j))
                    for m in range(KD):
                        plan.append(("wo", 0, l, m))
                    for sub in range(NSUB):
                        for m2 in range(4):
                            plan.append(("sq", "w_in", l, m2))
                        plan.append(("pool", l))
                        plan.append(("glu", l))
                        for m2 in range(4):
                            plan.append(("sq", "w_out", l, m2))
                    for j in range(NJ):
                        plan.append(("wi", 1, l, j))
                    for m in range(KD):
                        plan.append(("wo", 1, l, m))
                    for m2 in range(4):
                        plan.append(("projm", l, m2))
                        plan.append(("sq", "ple_w_gate", l, m2))
        issued = [0]
        consumed = [0]

        def issue_piece(idx):
            d = plan[idx]
            slot = idx % NRING
            t, B = ring[slot], B_ring[slot]
            key = "ring%d" % slot
            if d[0] == "wi":
                _, f, l, j = d
                w = dram["ffn1_wi" if f == 0 else "ffn2_wi"][l].rearrange("(k p) n -> p k n", p=128)
                tv = t[:, 0:2048].rearrange("p (t k c) -> p t k c", t=2, k=8)
                for tq in range(2):
                    c0 = tq * DFF + j * 128
                    dma("pool", tv[:, tq], w[:, :, c0:c0 + 128], [], [B[tq]], key)
            elif d[0] == "wo":
                _, f, l, m = d
                w = dram["ffn1_wo" if f == 0 else "ffn2_wo"][l].rearrange("(j p) n -> p j n", p=128)
                dma("pool", t[:, 0:NJ * 128].rearrange("p (j c) -> p j c", j=NJ), w[:, :, m * 128:(m + 1) * 128], [], B, key)
            elif d[0] == "sq":
                _, nm, l, m2 = d
                w = dram[nm][l].rearrange("(k p) n -> p k n", p=128)
                dma("pool", t[:, 0:2048].rearrange("p (k c) -> p k c", k=8), w[:, :, m2 * 256:(m2 + 1) * 256], [], B, key)
            elif d[0] == "projm":
                _, l, m2 = d
                w = dram["ple_w_proj"][l].rearrange("(k p) n -> p k n", p=128)
                dma("pool", t[:, 0:512].rearrange("p (k c) -> p k c", k=2), w[:, :, m2 * 256:(m2 + 1) * 256], [], B, key)
            elif d[0] == "glu":
                l = d[1]
                w = dram["ssm_w_glu"][l].rearrange("(k p) n -> p k n", p=128)
                dma("pool", t[:, 0:2048].rearrange("p (k c) -> p k c", k=4), w, [], B, key)
            elif d[0] == "pool":
                l = d[1]
                w = dram["pool_w"][l].rearrange("g p n -> p g n")
                dma("pool", t[:, 0:512].rearrange("p (g c) -> p g c", g=4), w, [], B, key)

        def get_piece(desc):
            idx = consumed[0]
            assert plan[idx] == desc, (plan[idx], desc)
            consumed[0] += 1
            while issued[0] < min(len(plan), idx + NRING - 1):
                issue_piece(issued[0])
                issued[0] += 1
            return ring[idx % NRING], B_ring[idx % NRING]

        def rmsnorm(gain_cols, out_fn):
            for sub in range(NSUB):
                ts = slice(sub * SUB, (sub + 1) * SUB)
                ps, bps = next_bank()
                for k in range(KD):
                    si = sq_i[0] % 4
                    sq_i[0] += 1
                    op("act", "activation", [B_h[k][sub]], [B_sq[si]], out=sq[si], in_=h[:, k, ts], func=AF.Square)
                    mm(ps, ones_bf, sq[si], k == 0, k == KD - 1, [B_ones, B_sq[si]], [bps])
                ri = sub
                op("act", "activation", [bps, B_misc], [B_rstd[ri]], out=rstd[ri], in_=ps, func=AF.Sqrt, scale=1.0 / D,
                   bias=epsc)
                op("dve", "reciprocal", [B_rstd[ri]], [B_rstd[ri]], out=rstd[ri], in_=rstd[ri])
                for k in range(KD):
                    out_fn(sub, k, h[:, k, ts], gain_cols[:, k:k + 1], rstd[ri], [B_h[k][sub], B_rstd[ri], B_gains])

        def norm_to_xn(gain_cols):
            def f(sub, k, hk, gcol, rs, reads):
                ts = slice(sub * SUB, (sub + 1) * SUB)
                op("dve", "scalar_tensor_tensor", reads, [B_xn[sub]], out=xn[:, k, ts], in0=hk, scalar=gcol, in1=rs,
                   op0=ALU.mult, op1=ALU.mult)
            rmsnorm(gain_cols, f)

        def ffn(f, l):
            norm_to_xn(gains_v[:, 0 if f == 0 else 2, l, :])
            for j in range(NJ):
                t, B = get_piece(("wi", f, l, j))
                tv = t[:, 0:2048].rearrange("p (t k c) -> p t k c", t=2, k=8)
                for sub in range(NSUB):
                    ts = slice(sub * SUB, (sub + 1) * SUB)
                    gps, bg = next_bank()
                    for k in range(KD):
                        mm(gps, tv[:, 0, k, :], xn[:, k, ts], k == 0, k == KD - 1, B + [B_xn[sub]], [bg])
                    ups, bu = next_bank()
                    for k in range(KD):
                        mm(ups, tv[:, 1, k, :], xn[:, k, ts], k == 0, k == KD - 1, B + [B_xn[sub]], [bu])
                    si = sgt_i[0] % 3
                    sgt_i[0] += 1
                    op("act", "activation", [bg], [B_sgt[si]], out=sgt[si], in_=gps, func=AF.Silu)
                    op("dve", "tensor_tensor", [B_sgt[si], bu], [B_act[j][sub]] , out=act[:, j, ts], in0=ups, in1=sgt[si],
                       op=ALU.mult)
            for m in range(KD):
                t, B = get_piece(("wo", f, l, m))
                for sub in range(NSUB):
                    ts = slice(sub * SUB, (sub + 1) * SUB)
                    yps, by = next_bank()
                    for j in range(NJ):
                        mm(yps, t[:, j * 128:(j + 1) * 128], act[:, j, ts], j == 0, j == NJ - 1, B + [B_act[j][sub]], [by])
                    op("dve", "scalar_tensor_tensor", [by, B_h[m][sub]], [B_h[m][sub]], out=h[:, m, ts], in0=yps, scalar=0.5,
                       in1=h[:, m, ts], op0=ALU.mult, op1=ALU.add)

        def load_ssm_consts(l):
            dma("sp", KF, kfir_s[l], [], [B_sc], "sc")
            dma("sp", BS, bst_s[l], [], [B_sc], "sc")
            dma("sp", CS, cst_s[l], [], [B_sc], "sc")
            dma("sp", ROT, rot_s[l], [], [B_sc], "sc")

        def mixer(l, first_of_seq, sub):
            ts = slice(sub * SUB, (sub + 1) * SUB)
            seq_start = first_of_seq and sub == 0
            for m2 in range(4):
                t, B = get_piece(("sq", "w_in", l, m2))
                tv = t[:, 0:2048].rearrange("p (k c) -> p k c", k=8)
                for mmi in range(2):
                    m = 2 * m2 + mmi
                    zps, bz = next_bank()
                    for k in range(KD):
                        mm(zps, tv[:, k, mmi * 128:(mmi + 1) * 128], xn[:, k, ts], k == 0, k == KD - 1, B + [B_xn[sub]], [bz])
                    if m < 4:
                        op("act", "activation", [bz], [B_zsf[m]], out=zs_f[:, m, :].rearrange("p (j c) -> p c j", c=NCS),
                           in_=zps.rearrange("p (c j) -> p c j", j=TC), func=AF.Copy)
                        op("pool", "tensor_copy", [B_zsf[m]], [B_zsb[m]], out=zs_bf[:, m, :], in_=zs_f[:, m, :])
                    else:
                        g = m - 4
                        if seq_start:
                            op("pool", "memset", [], [B_zp[g]], zp_f[:, g, 0:16], 0.0)
                        else:
                            op("pool", "tensor_copy", [B_hist[l][g]], [B_zp[g]], out=zp_f[:, g, 0:16], in_=hist[:, l, g, :])
                        op("act", "activation", [bz], [B_zp[g]], out=zp_f[:, g, 16:16 + SUB], in_=zps, func=AF.Copy)
            tp_, Bp = get_piece(("pool", l))
            tpv = tp_[:, 0:512].rearrange("p (g c) -> p g c", g=4)
            for g, wdw in enumerate(POOL_WINS):
                zz = zp_f[:, g, :]
                cur, Bcur = zz, B_zp[g]
                lo = 0
                sh = 1
                ti_ = 0
                while sh < wdw:
                    nxt, Bn = ptmp[ti_ % 3], B_ptmp[ti_ % 3]
                    ti_ += 1
                    nlo = lo + sh
                    tt("pool", nxt[:, nlo:16 + SUB], cur[:, nlo:16 + SUB], cur[:, nlo - sh:16 + SUB - sh], ALU.add, [Bcur], [Bn])
                    cur, Bcur, lo = nxt, Bn, nlo
                    sh *= 2
                nx2, Bn2 = ptmp[ti_ % 3], B_ptmp[ti_ % 3]
                ti_ += 1
                op("pool", "tensor_scalar", [Bcur], [Bn2], out=nx2[:, 16:16 + SUB], in0=cur[:, 16:16 + SUB], scalar1=1.0 / wdw,
                   scalar2=None, op0=ALU.mult)
                tt("pool", d_bf[:, g, :], nx2[:, 16:16 + SUB], zz[:, 16:16 + SUB], ALU.subtract, [Bn2, B_zp[g]], [B_dbf[g]])
                if seq_start:
                    nxt, Bn = ptmp[ti_ % 3], B_ptmp[ti_ % 3]
                    tt("pool", nxt[:, 0:16], cur[:, 16:32], invc[:, g, :], ALU.mult, [Bcur, B_cst], [Bn])
                    tt("pool", d_bf[:, g, 0:16], nxt[:, 0:16], zz[:, 16:32], ALU.subtract, [Bn, B_zp[g]], [B_dbf[g]])
                op("pool", "tensor_copy", [B_zp[g], Bcur, B_dbf[g]], [B_hist[l][g]], out=hist[:, l, g, :], in_=zp_f[:, g, SUB:SUB + 16])
                pps, bp = next_bank()
                mm(pps, tpv[:, g, :], d_bf[:, g, :], True, True, Bp + [B_dbf[g]], [bp])
                op("dve", "tensor_scalar", [bp, B_gains], [B_xn[sub]], out=xn[:, 4 + g, ts], in0=pps,
                   scalar1=psc[:, l * 4 + g:l * 4 + g + 1], scalar2=None, op0=ALU.mult)
            zj = zs_bf.rearrange("p b (j c) -> p b j c", c=NCS)
            do_ssm = "nossm" not in flags
            stop_at = 99
            for f_ in flags:
                if f_.startswith("stop"):
                    stop_at = int(f_[4:])
            dd = dbg is not None and l == 0 and sub == 0 and seq_start
            if dd:
                dump(KF, 3392, [B_sc], q="pool")
                dump(BS, 7488, [B_sc], q="pool")
                dump(CS, 15680, [B_sc], q="pool")
                dump(zs_f.rearrange("p b t -> p (b t)"), 23872, B_zsf)
                dump(ROT, 25920, [B_sc])
            ROTq = ROTm.rearrange("p r (b q) c -> p r b q c", q=4)
            carry_q = carry_v.rearrange("p l (b q) r -> p l b q r", q=4)
            sbq = sbf_all.rearrange("p (b q) r c -> p b q r c", q=4)
            if do_ssm:
                Sb = [next_bank() for _ in range(4)]
                firsts = [True] * 4
                for b in range(4):
                    for ri in range(2):
                        for j in range(TC):
                            for q4 in range(4):
                                sps, bs = Sb[q4]
                                Sv = sps.rearrange("p (b r c) -> p b r c", b=4, r=2)
                                mm(Sv[:, b, ri, :], BSm[32 * q4:32 * q4 + 32, b, j, ri, :], zj[32 * q4:32 * q4 + 32, b, j, :],
                                   firsts[q4], False, [B_sc, B_zsb[b]], [bs], tp=(32 * q4, 0))
                                firsts[q4] = False
            for q4 in range(4 if (do_ssm and stop_at > 1) else 0):
                par = q4 % 2
                V, W, SF, RT = Vt[par], Wt[par], SFt[par], rtmp[par]
                sps, bs = Sb[q4]
                Sv = sps.rearrange("p (b r c) -> p b r c", b=4, r=2)
                rc = ROTq[:, 0, :, q4, :]
                rs_ = ROTq[:, 1, :, q4, :]
                Bc = B_carry[l][q4]
                tt("dve", RT, Sv[:, :, 1, :], rs_, ALU.mult, [bs, B_sc], [B_rt[par]])
                tt("dve", V[:, :, 0, :], Sv[:, :, 0, :], rc, ALU.mult, [bs, B_sc], [B_V[par]])
                tt("dve", V[:, :, 0, :], V[:, :, 0, :], RT, ALU.add, [B_V[par], B_rt[par]], [B_V[par]])
                tt("dve", RT, Sv[:, :, 0, :], rs_, ALU.mult, [bs, B_sc], [B_rt[par]])
                tt("dve", V[:, :, 1, :], Sv[:, :, 1, :], rc, ALU.mult, [bs, B_sc], [B_V[par]])
                tt("dve", V[:, :, 1, :], V[:, :, 1, :], RT, ALU.subtract, [B_V[par], B_rt[par]], [B_V[par]])
                if stop_at <= 2:
                    continue
                if seq_start:
                    op("dve", "memset", [], [Bc], carry_q[:, l, :, q4, :], 0.0)
                for b in range(4):
                    for ri in range(2):
                        q = 4 * b + q4
                        op("dve", "tensor_tensor_scan", [B_V[par], B_RR, Bc], [B_W[par]], out=W[:, b, ri, :],
                           data0=RR[:, l * 16 + q:l * 16 + q + 1].to_broadcast([128, NCS]), data1=V[:, b, ri, :],
                           initial=carry_v[:, l, q, ri:ri + 1], op0=ALU.mult, op1=ALU.add)
                if stop_at <= 3:
                    continue
                tt("pool", RT, W[:, :, 1, :], rs_, ALU.mult, [B_W[par], B_sc], [B_rt[par]])
                tt("pool", SF[:, :, 0, 1:NCS + 1], W[:, :, 0, :], rc, ALU.mult, [B_W[par], B_sc], [B_SF[par]])
                tt("pool", SF[:, :, 0, 1:NCS + 1], SF[:, :, 0, 1:NCS + 1], RT, ALU.subtract, [B_SF[par], B_rt[par]], [B_SF[par]])
                tt("pool", RT, W[:, :, 0, :], rs_, ALU.mult, [B_W[par], B_sc], [B_rt[par]])
                tt("pool", SF[:, :, 1, 1:NCS + 1], W[:, :, 1, :], rc, ALU.mult, [B_W[par], B_sc], [B_SF[par]])
                tt("pool", SF[:, :, 1, 1:NCS + 1], SF[:, :, 1, 1:NCS + 1], RT, ALU.add, [B_SF[par], B_rt[par]], [B_SF[par]])
                op("pool", "tensor_copy", [Bc], [B_SF[par]], out=SF[:, :, :, 0], in_=carry_q[:, l, :, q4, :])
                op("pool", "tensor_copy", [B_SF[par]], [Bc], out=carry_q[:, l, :, q4, :], in_=SF[:, :, :, NCS])
                op("act", "activation", [B_SF[par]], [B_sbq[q4]], out=sbq[:, :, q4, :, :], in_=SF[:, :, :, 0:NCS], func=AF.Copy)
                if dd and q4 == 0:
                    dump(V.rearrange("p b r c -> p (b r c)"), 27968, [B_V[par]])
                    dump(W.rearrange("p b r c -> p (b r c)"), 28480, [B_W[par]])
                    dump(SF.rearrange("p b r c -> p (b r c)"), 28992, [B_SF[par]])
            for b in range(4 if (do_ssm and stop_at > 4) else 0):
                par = b % 2
                yps, by = next_bank()
                Yj = yps.rearrange("p (j c) -> p j c", c=NCS)
                mm(yps, KFm[:, b, 0, :], zs_bf[:, b, :], True, False, [B_sc, B_zsb[b]], [by])
                for k in range(1, TC):
                    mm(yps[:, k * NCS:SUB], KFm[:, b, k, :], zs_bf[:, b, 0:(TC - k) * NCS], False, False, [B_sc, B_zsb[b]], [by])
                for j in range(TC if stop_at > 5 else 0):
                    for q4 in range(4):
                        for ri in range(2):
                            mm(Yj[32 * q4:32 * q4 + 32, j, :], CSm[:, 4 * b + q4, j, ri, :], sbf_all[:, 4 * b + q4, ri, :], False,
                               (j == TC - 1 and q4 == 3 and ri == 1), [B_sc] + B_sbq, [by], tp=(0, 32 * q4))
                if stop_at <= 6:
                    continue
                Y, G = ys[par], gt[par]
                op("dve", "tensor_scalar", [B_zsf[b], B_gains], [B_gt[par]], out=G, in0=zs_f[:, b, :],
                   scalar1=dsk[:, l * 4 + b:l * 4 + b + 1], scalar2=None, op0=ALU.mult)
                if stop_at <= 7:
                    continue
                tt("dve", Y, yps, G, ALU.add, [by, B_gt[par]], [B_ys[par]])
                if dd and b == 0:
                    dump(Y, 29512, [B_ys[par]])
                if stop_at <= 8:
                    continue
                op("act", "activation", [B_ys[par]], [B_gt[par]], out=G, in_=Y, func=AF.Square)
                if stop_at <= 9:
                    continue
                op("pool", "tensor_scalar", [B_gt[par]], [B_gt[par]], out=G, in0=G, scalar1=0.044715, scalar2=1.0,
                   op0=ALU.mult, op1=ALU.add)
                op("pool", "tensor_tensor", [B_gt[par], B_ys[par]], [B_gt[par]], out=G, in0=G, in1=Y, op=ALU.mult)
                if stop_at <= 10:
                    continue
                op("act", "activation", [B_gt[par]], [B_gt[par]], out=G, in_=G, func=AF.Sigmoid, scale=1.5957691216057308)
                if stop_at <= 11:
                    continue
                op("dve", "tensor_tensor", [B_gt[par], B_ys[par]], [B_zsf[b]], out=zs_f[:, b, :], in0=G, in1=Y, op=ALU.mult)
                if stop_at <= 12:
                    continue
                op("act", "activation", [B_zsf[b]], [B_gyb[b]], out=gy_bf[:, b, :], in_=zs_f[:, b, :], func=AF.Copy)
            tg, Bg = get_piece(("glu", l))
            tgv = tg[:, 0:2048].rearrange("p (k c) -> p k c", k=4)
            for n_ in range(4 if "nossm" not in flags else 0):
                gps, bg = next_bank()
                for b in range(4):
                    mm(gps, tgv[:, b, n_ * 128:(n_ + 1) * 128], gy_bf[:, b, :], b == 0, b == 3, Bg + [B_gyb[b]], [bg])
                si = sgt_i[0] % 3
                sgt_i[0] += 1
                op("act", "activation", [bg], [B_sgt[si]], out=sgt[si], in_=gps, func=AF.Sigmoid)
                op("dve", "tensor_tensor", [B_sgt[si], B_zsf[n_]], [B_xn[sub]], out=xn[:, n_, ts].rearrange("p (c j) -> p c j", j=TC),
                   in0=sgt[si].rearrange("p (j c) -> p c j", c=NCS), in1=zs_f[:, n_, :].rearrange("p (j c) -> p c j", c=NCS),
                   op=ALU.mult)
            if "nossm" in flags:
                for n_ in range(4):
                    op("dve", "tensor_copy", [B_zsb[n_]], [B_xn[sub]], out=xn[:, n_, ts], in_=zs_bf[:, n_, :])
            if dbg is not None and l == 0 and sub == 0 and seq_start:
                for k_ in range(8):
                    dump(xn[:, k_, ts], 30024 + 512 * k_, [B_xn[sub]], q="pool")
            for m2 in range(4):
                t, B = get_piece(("sq", "w_out", l, m2))
                tv = t[:, 0:2048].rearrange("p (k c) -> p k c", k=8)
                for mmi in range(2):
                    m = 2 * m2 + mmi
                    ops_, bo = next_bank()
                    for k in range(KD):
                        mm(ops_, tv[:, k, mmi * 128:(mmi + 1) * 128], xn[:, k, ts], k == 0, k == KD - 1, B + [B_xn[sub]], [bo])
                    tt("dve", h[:, m, ts], ops_, h[:, m, ts], ALU.add, [bo, B_h[m][sub]], [B_h[m][sub]])

        def ple(l, tok0):
            norm_to_xn(gains_v[:, 3, l, :])
            dma("pool", pb, dram["pT"][l].rearrange("(k p) n -> p k n", p=128)[:, :, tok0:tok0 + TT], [], [B_pb], "pb")
            for m2 in range(4):
                tpj, Bpj = get_piece(("projm", l, m2))
                tpjv = tpj[:, 0:512].rearrange("p (k c) -> p k c", k=2)
                t, B = get_piece(("sq", "ple_w_gate", l, m2))
                tv = t[:, 0:2048].rearrange("p (k c) -> p k c", k=8)
                for mmi in range(2):
                    m = 2 * m2 + mmi
                    for sub in range(NSUB):
                        ts = slice(sub * SUB, (sub + 1) * SUB)
                        gps, bg = next_bank()
                        for k in range(KD):
                            mm(gps, tv[:, k, mmi * 128:(mmi + 1) * 128], xn[:, k, ts], k == 0, k == KD - 1, B + [B_xn[sub]], [bg])
                        pps, bp = next_bank()
                        for k in range(2):
                            mm(pps, tpjv[:, k, mmi * 128:(mmi + 1) * 128], pb[:, k, ts], k == 0, k == 1, Bpj + [B_pb], [bp])
                        si = sgt_i[0] % 3
                        sgt_i[0] += 1
                        if "ple_mm" in flags:
                            op("dve", "tensor_copy", [bg], [B_sgt[si]], out=sgt[si], in_=gps)
                            op("dve", "tensor_copy", [bp], [B_sgt[si]], out=sgt[si], in_=pps)
                            continue
                        op("act", "activation", [bg], [B_sgt[si]], out=sgt[si], in_=gps, func=AF.Sigmoid)
                        tt("dve", sgt[si], pps, sgt[si], ALU.mult, [B_sgt[si], bp], [B_sgt[si]])
                        tt("dve", h[:, m, ts], sgt[si], h[:, m, ts], ALU.add, [B_sgt[si], B_h[m][sub]], [B_h[m][sub]])

        out_dmas = []
        if "prolog" in flags:
            plan = []
            op("dve", "memset", [], [B_ost], ostage, 0.0)
            out_dmas.append(dma("sp", outT.rearrange("(k p) n -> p k n", p=128)[:, :, 0:SUB], ostage, [B_ost], [B_ost], "ost"))
        for s in range(n_seq if "prolog" not in flags else 0):
            for ti in range(NTILE):
                tok0 = s * seq_len + ti * TT
                for k in range(KD):
                    dma("sp", h[:, k, :], dram["xT"][k * 128:(k + 1) * 128, tok0:tok0 + TT], [], B_h[k], "hx%d" % k)
                for l in range(depth):
                    load_ssm_consts(l)
                    if "noffn" not in flags:
                        ffn(0, l)
                    else:
                        for _ in range(NJ + KD):
                            get_piece(plan[consumed[0]])
                    norm_to_xn(gains_v[:, 1, l, :])
                    guard = act_region_bufs()
                    op("pool", "memset", [], guard + mixer_tmp_bufs, dummies["pool"], 0.0)
                    for sub in range(NSUB):
                        if "nomix" in flags:
                            for _ in range(10):
                                get_piece(plan[consumed[0]])
                        else:
                            mixer(l, ti == 0, sub)
                    op("pool", "memset", [], guard + mixer_tmp_bufs, dummies["pool"], 0.0)
                    if "noffn" not in flags:
                        ffn(1, l)
                    else:
                        for _ in range(NJ + KD):
                            get_piece(plan[consumed[0]])
                    if "nople" in flags:
                        for _ in range(8):
                            get_piece(plan[consumed[0]])
                    else:
                        ple(l, tok0)
                op("pool", "memset", [], act_region_bufs() + mixer_tmp_bufs, dummies["pool"], 0.0)

                def fo(sub, k, hk, gcol, rs, reads):
                    op("dve", "scalar_tensor_tensor", reads + [B_ost], [B_ost], out=ostage[:, k, :], in0=hk, scalar=gcol, in1=rs,
                       op0=ALU.mult, op1=ALU.mult)
                    if k == KD - 1:
                        t0 = tok0 + sub * SUB
                        d_ = dma("sp", outT.rearrange("(k p) n -> p k n", p=128)[:, :, t0:t0 + SUB], ostage, [B_ost], [B_ost], "ost")
                        out_dmas.append(d_)
                rmsnorm(gfin, fo)
        assert consumed[0] == len(plan), (consumed[0], len(plan))
        P.emit(final_waits=out_dmas + dbg_dmas)
    return nc


_CACHE = {}


def _get_nc(n_seq, seq_len, depth, flags=frozenset()):
    key = (n_seq, seq_len, depth, flags)
    if key not in _CACHE:
        _CACHE[key] = build(n_seq, seq_len, depth, flags)
    return _CACHE[key]


def kernel(**inputs):
    x = np.asarray(inputs["x"], dtype=np.float32)
    p = np.asarray(inputs["p"], dtype=np.float32)
    B, L, _ = x.shape
    ncores = 8
    n_seq = B // ncores
    nc = _get_nc(n_seq, L, DEPTH)
    consts = host_consts()
    wmap = {name: np.ascontiguousarray(np.asarray(inputs[name], dtype=np.float32)) for name, _ in WSHAPES}
    in_maps = []
    for c in range(ncores):
        xs = x[c * n_seq:(c + 1) * n_seq].reshape(n_seq * L, D)
        ps_ = p[:, c * n_seq:(c + 1) * n_seq].reshape(DEPTH, n_seq * L, 256)
        m = {"xT": np.ascontiguousarray(xs.T), "pT": np.ascontiguousarray(ps_.transpose(0, 2, 1)), "consts": consts}
        m.update(wmap)
        in_maps.append(m)
    res = run_bass_kernel_spmd(nc, in_maps, core_ids=list(range(ncores)))
    out = np.empty((B, L, D), np.float32)
    for c in range(ncores):
        o = np.asarray(res.results[c]["outT"])
        out[c * n_seq:(c + 1) * n_seq] = o.T.reshape(n_seq, L, D)
    return out
```

```python
import contextlib
import math
import numpy as np
import concourse.bass as bass
import concourse.mybir as mybir
from concourse.bass_utils import run_bass_kernel_spmd

F32 = mybir.dt.float32
BF16 = mybir.dt.bfloat16
AF = mybir.ActivationFunctionType
ALU = mybir.AluOpType

ENGS = ("pe", "act", "dve", "pool", "sp")


class Buf:
    __slots__ = ("w", "rs")

    def __init__(self):
        self.w = None
        self.rs = []


class Op:
    __slots__ = ("eng", "fn", "deps", "sig", "val", "dma", "semkey")

    def __init__(self, eng, fn, dma):
        self.eng = eng
        self.fn = fn
        self.deps = []
        self.sig = False
        self.val = 0
        self.dma = dma
        self.semkey = None


class Prog:
    def __init__(self, nc):
        self.nc = nc
        self.ops = {e: [] for e in ENGS}
        self.last_dma = {}

    def add(self, eng, fn, reads=(), writes=(), dma=False, semkey=None, extra=()):
        op = Op(eng, fn, dma)
        op.semkey = semkey
        deps = op.deps
        for b in reads:
            if b.w is not None:
                deps.append(b.w)
        for b in writes:
            if b.w is not None:
                deps.append(b.w)
            deps.extend(b.rs)
        deps.extend(extra)
        for b in reads:
            if not dma and eng != "pool":
                b.rs = [r for r in b.rs if r.eng != eng or r.dma]
            b.rs.append(op)
        for b in writes:
            b.w = op
            b.rs = []
        self.ops[eng].append(op)
        if dma:
            self.last_dma[semkey] = op
        return op

    def barrier(self, dummies):
        lasts = [self.ops[e][-1] for e in ENGS if self.ops[e]] + list(self.last_dma.values())
        for e in ("act", "dve", "pool"):
            d = dummies[e]
            if e == "act":
                self.add(e, (lambda d: lambda h: h.activation(out=d, in_=d, func=AF.Copy))(d), extra=lasts)
            else:
                self.add(e, (lambda d: lambda h: h.memset(d, 0.0))(d), extra=lasts)
        self.add("sp", lambda h: h.dma_start(out=dummies["spo"], in_=dummies["spi"]), extra=lasts, dma=True, semkey="barrier")

    def emit(self, final_waits=()):
        nc = self.nc
        for e in ENGS:
            for op in self.ops[e]:
                seen = set()
                nd = []
                for d in op.deps:
                    if d is op or id(d) in seen:
                        continue
                    seen.add(id(d))
                    if d.eng == "pe" and op.eng == "pe" and not d.dma and not op.dma:
                        continue
                    nd.append(d)
                    d.sig = True
                op.deps = nd
        for op in final_waits:
            op.sig = True
        cnt = {e: 0 for e in ENGS}
        dma_cnt = {}
        for e in ENGS:
            for op in self.ops[e]:
                if op.dma:
                    k = op.semkey
                    dma_cnt[k] = dma_cnt.get(k, 0) + 16
                    op.val = dma_cnt[k]
                elif op.sig:
                    cnt[e] += 1
                    op.val = cnt[e]
        with contextlib.ExitStack() as st:
            sems = {e: st.enter_context(nc.semaphore("s_" + e)) for e in ENGS}
            dsems = {k: st.enter_context(nc.semaphore("d_%s" % str(k))) for k in dma_cnt}
            block = st.enter_context(nc.Block())

            def sem_of(d):
                return dsems[d.semkey] if d.dma else sems[d.eng]

            def run(e, handle):
                waited = {}
                for op in self.ops[e]:
                    for d in op.deps:
                        s = sem_of(d)
                        key = id(s)
                        if waited.get(key, 0) >= d.val:
                            continue
                        handle.wait_ge(s, d.val)
                        waited[key] = d.val
                    ins = op.fn(handle)
                    if op.dma:
                        ins.then_inc(dsems[op.semkey], 16)
                    elif op.sig:
                        ins.then_inc(sems[e], 1)
                if e == "sp":
                    for d in final_waits:
                        handle.wait_ge(sem_of(d), d.val)

            @block.tensor
            def _(h):
                run("pe", h)

            @block.scalar
            def _(h):
                run("act", h)

            @block.vector
            def _(h):
                run("dve", h)

            @block.gpsimd
            def _(h):
                run("pool", h)

            @block.sync
            def _(h):
                run("sp", h)


D = 1024
KD = 8
DFF = 2816
NJ = 22
TT = 1024
SUB = 512
NSUB = 2
TC = 8
NCS = SUB // TC
DEPTH = 4
EPS = 1e-6
NRING = 5
RINGW = 2816
POOL_WINS = (2, 4, 8, 16)

WSHAPES = [
    ("ffn1_norm", [DEPTH, D]), ("ffn1_wi", [DEPTH, D, 2 * DFF]), ("ffn1_wo", [DEPTH, DFF, D]),
    ("mix_norm", [DEPTH, D]), ("w_in", [DEPTH, D, D]),
    ("ssm_lambda_re", [DEPTH, 32, 64]), ("ssm_lambda_im", [DEPTH, 32, 64]), ("ssm_log_dt", [DEPTH, 32]),
    ("ssm_b_re", [DEPTH, 32, 64, 16]), ("ssm_b_im", [DEPTH, 32, 64, 16]),
    ("ssm_c_re", [DEPTH, 32, 16, 64]), ("ssm_c_im", [DEPTH, 32, 16, 64]),
    ("ssm_d", [DEPTH, 512]), ("ssm_w_glu", [DEPTH, 512, 512]),
    ("pool_w", [DEPTH, 4, 128, 128]), ("pool_scale", [DEPTH, 512]), ("w_out", [DEPTH, D, D]),
    ("ffn2_norm", [DEPTH, D]), ("ffn2_wi", [DEPTH, D, 2 * DFF]), ("ffn2_wo", [DEPTH, DFF, D]),
    ("ple_norm", [DEPTH, D]), ("ple_w_gate", [DEPTH, D, D]), ("ple_w_proj", [DEPTH, 256, D]),
    ("final_norm", [D]),
]
NCONST = 128 + 64 + 64 + 64


def host_consts():
    c = np.zeros((128, NCONST), np.float32)
    c[:, 0:128] = np.eye(128, dtype=np.float32)
    for g2 in range(2):
        for q4 in range(4):
            for h in range(16):
                c[(2 * q4 + g2) * 16 + h, 128 + 64 * g2 + q4 * 16 + h] = 1.0
    for gi, w in enumerate(POOL_WINS):
        for t in range(16):
            c[:, 256 + gi * 16 + t] = 1.0 / min(t + 1, w)
    return c


def build(n_seq, seq_len, depth, flags=frozenset()):
    NTOK = n_seq * seq_len
    NTILE = seq_len // TT
    nc = bass.Bass("TRN2", target_bir_lowering=False)
    dram = {}
    dram["xT"] = nc.dram_tensor("xT", [D, NTOK], F32, kind="ExternalInput").ap()
    dram["pT"] = nc.dram_tensor("pT", [DEPTH, 256, NTOK], F32, kind="ExternalInput").ap()
    dram["consts"] = nc.dram_tensor("consts", [128, NCONST], F32, kind="ExternalInput").ap()
    for name, shp in WSHAPES:
        dram[name] = nc.dram_tensor(name, shp, F32, kind="ExternalInput").ap()
    outT = nc.dram_tensor("outT", [D, NTOK], F32, kind="ExternalOutput").ap()
    kfir_s = nc.dram_tensor("kfir_s", [DEPTH, 128, 4 * 8 * 128], BF16, kind="Internal").ap()
    bst_s = nc.dram_tensor("bst_s", [DEPTH, 128, 4 * 8 * 2 * 128], BF16, kind="Internal").ap()
    cst_s = nc.dram_tensor("cst_s", [DEPTH, 128, 16 * 8 * 2 * 32], BF16, kind="Internal").ap()
    rot_s = nc.dram_tensor("rot_s", [DEPTH, 128, 2 * 16 * 64], F32, kind="Internal").ap()

    dbg = None
    if "dbg" in flags:
        dbg = nc.dram_tensor("dbg", [128, 40960], F32, kind="ExternalOutput").ap()
    dbg_dmas = []
    dbg_i = [0]
    P = Prog(nc)
    NW = 53100

    with contextlib.ExitStack() as st:
        big = st.enter_context(nc.sbuf_tensor("big", [128, NW], F32))[:]
        banks = [st.enter_context(nc.psum_tensor("bank%d" % i, [128, 512], F32))[:] for i in range(8)]
        bank_bufs = [Buf() for _ in range(8)]
        bank_i = [0]

        def next_bank():
            i = bank_i[0] % 8
            bank_i[0] += 1
            return banks[i], bank_bufs[i]

        off = [0]

        def alloc(nelem, dt=F32):
            words = nelem if dt == F32 else (nelem + 1) // 2
            a = off[0]
            off[0] += words
            assert off[0] <= NW, ("SBUF arena overflow", off[0])
            v = big[:, a:a + words]
            return v if dt == F32 else v.bitcast(dt)

        def op(eng, method, reads, writes, *args, **kw):
            return P.add(eng, lambda h: getattr(h, method)(*args, **kw), reads, writes)

        def dma(q, out, in_, reads, writes, semkey):
            return P.add(q, lambda h: h.dma_start(out=out, in_=in_), reads, writes, dma=True, semkey=semkey)

        def dump(ap2d, col, reads, q="sp"):
            if dbg is None:
                return
            n_ = ap2d.shape[1]
            dbg_i[0] += 1
            dbg_dmas.append(dma(q, dbg[:, col:col + n_], ap2d, reads, [Buf()], "dbg%d" % dbg_i[0]))

        def mm(out, lhsT, rhs, start, stop, reads, writes, tp=None):
            kw = dict(start=start, stop=stop, skip_group_check=True)
            if tp is not None:
                kw["tile_position"] = tp
            return P.add("pe", lambda h: h.matmul(out, lhsT=lhsT, rhs=rhs, **kw), reads, writes)

        cst = alloc(NCONST)
        B_cst = Buf()
        ident = cst[:, 0:128]
        sel = [cst[:, 128:192], cst[:, 192:256]]
        invc = cst[:, 256:320].rearrange("p (g t) -> p g t", t=16)
        dma("sp", cst, dram["consts"], [], [B_cst], "cst")
        misc = alloc(16)
        B_misc = Buf()
        op("dve", "memset", [], [B_misc], misc[:, 0:1], math.pi / 2)
        op("dve", "memset", [], [B_misc], misc[:, 1:2], EPS)
        op("dve", "memset", [], [B_misc], misc[:, 2:3], 0.0)
        halfpi = misc[:, 0:1]
        epsc = misc[:, 1:2]
        ones_bf = alloc(128, BF16)
        B_ones = Buf()
        op("dve", "memset", [], [B_ones], ones_bf, 1.0)
        gains = alloc(4 * DEPTH * 8)
        B_gains = Buf()
        gfin = alloc(8)
        dsk = alloc(DEPTH * 4)
        psc = alloc(DEPTH * 4)
        RR = alloc(DEPTH * 16)
        B_RR = Buf()
        carry = alloc(DEPTH * 16 * 2)
        carry_v = carry.rearrange("p (l q r) -> p l q r", l=DEPTH, r=2)
        B_carry = [[Buf() for _ in range(4)] for _ in range(DEPTH)]
        hist = alloc(DEPTH * 4 * 16).rearrange("p (l g t) -> p l g t", l=DEPTH, g=4)
        B_hist = [[Buf() for _ in range(4)] for _ in range(DEPTH)]
        dummies = {e: alloc(2) for e in ("act", "dve", "pool", "spo", "spi")}
        op("dve", "memset", [], [], dummies["spi"], 0.0)
        op("dve", "memset", [], [], dummies["act"], 0.0)
        persist_end = off[0]

        stage = alloc(128)
        B_stage = Buf()

        def load_T(rows_ap, R, dst, B_dst, evac="dve"):
            dma("sp", stage[0:R, :], rows_ap, [], [B_stage], "stage")
            ps, bps = next_bank()
            mm(ps[:, 0:R], stage[0:R, :], ident[0:R, 0:R], True, True, [B_stage, B_cst], [bps])
            op(evac, "tensor_copy", [bps], [B_dst], out=dst, in_=ps[:, 0:R])

        for ki, nm in enumerate(("ffn1_norm", "mix_norm", "ffn2_norm", "ple_norm")):
            load_T(dram[nm].rearrange("l (k p) -> (l k) p", p=128), DEPTH * 8,
                   gains[:, ki * DEPTH * 8:(ki + 1) * DEPTH * 8], B_gains)
        load_T(dram["final_norm"].rearrange("(k p) -> k p", p=128), 8, gfin, B_gains)
        load_T(dram["ssm_d"].rearrange("l (b p) -> (l b) p", p=128), DEPTH * 4, dsk, B_gains)
        load_T(dram["pool_scale"].rearrange("l (b p) -> (l b) p", p=128), DEPTH * 4, psc, B_gains)
        gains_v = gains.rearrange("p (n l k) -> p n l k", n=4, l=DEPTH)

        LQ = DEPTH * 16

        def t64():
            return alloc(LQ), Buf()

        lr, B_lr = t64()
        li, B_li = t64()
        ldt, B_ldt = t64()
        load_T(dram["ssm_lambda_re"].rearrange("l (q g) p -> (l q) (g p)", g=2), LQ, lr, B_lr)
        load_T(dram["ssm_lambda_im"].rearrange("l (q g) p -> (l q) (g p)", g=2), LQ, li, B_li)
        ld2 = alloc(2)
        B_ld2 = Buf()
        dma("sp", ld2[0:LQ, :], dram["ssm_log_dt"].rearrange("l (q g) -> (l q) g", g=2), [], [B_ld2], "ld2")
        stage2 = alloc(128)
        B_stage2 = Buf()
        for g2 in range(2):
            op("dve", "tensor_copy", [B_ld2], [B_stage2], out=stage2[0:LQ, g2 * 64:(g2 + 1) * 64],
               in_=ld2[0:LQ, g2:g2 + 1].to_broadcast([LQ, 64]))
        ps, bps = next_bank()
        mm(ps[:, 0:LQ], stage2[0:LQ, :], ident[0:LQ, 0:LQ], True, True, [B_stage2, B_cst], [bps])
        op("dve", "tensor_copy", [bps], [B_ldt], out=ldt, in_=ps[:, 0:LQ])

        def tt(eng, out, a, b, o, reads, writes):
            return op(eng, "tensor_tensor", reads, writes, out=out, in0=a, in1=b, op=o)

        dt_, B_dt = t64()
        op("act", "activation", [B_ldt], [B_dt], out=dt_, in_=ldt, func=AF.Exp)
        mr, B_mr = t64()
        mi, B_mi = t64()
        tt("dve", mr, lr, dt_, ALU.mult, [B_lr, B_dt], [B_mr])
        tt("dve", mi, li, dt_, ALU.mult, [B_li, B_dt], [B_mi])
        em, B_em = t64()
        op("act", "activation", [B_mr], [B_em], out=em, in_=mr, func=AF.Exp)
        op("act", "activation", [B_mr], [B_RR], out=RR, in_=mr, func=AF.Exp, scale=float(TC))
        cu, B_cu = t64()
        su, B_su = t64()
        op("act", "activation", [B_mi], [B_su], out=su, in_=mi, func=AF.Sin, scale=1.0 / 64)
        op("act", "activation", [B_mi, B_misc], [B_cu], out=cu, in_=mi, func=AF.Sin, scale=1.0 / 64, bias=halfpi)
        ta, B_ta = t64()
        tb, B_tb = t64()

        def csquare(c, Bc, s, Bs):
            tt("dve", ta, c, c, ALU.mult, [Bc], [B_ta])
            tt("dve", tb, s, s, ALU.mult, [Bs], [B_tb])
            op("dve", "scalar_tensor_tensor", [Bc, Bs], [Bs], out=s, in0=c, scalar=2.0, in1=s,
               op0=ALU.mult, op1=ALU.mult)
            tt("dve", c, ta, tb, ALU.subtract, [B_ta, B_tb], [Bc])

        for _ in range(6):
            csquare(cu, B_cu, su, B_su)
        lbr, B_lbr = t64()
        lbi, B_lbi = t64()
        tt("dve", lbr, em, cu, ALU.mult, [B_em, B_cu], [B_lbr])
        tt("dve", lbi, em, su, ALU.mult, [B_em, B_su], [B_lbi])
        a1, B_a1 = t64()
        op("dve", "tensor_scalar", [B_lbr], [B_a1], out=a1, in0=lbr, scalar1=-1.0, scalar2=None, op0=ALU.add)
        inv, B_inv = t64()
        tt("dve", ta, lr, lr, ALU.mult, [B_lr], [B_ta])
        tt("dve", tb, li, li, ALU.mult, [B_li], [B_tb])
        tt("dve", ta, ta, tb, ALU.add, [B_ta, B_tb], [B_ta])
        op("dve", "reciprocal", [B_ta], [B_inv], out=inv, in_=ta)
        cr, B_cr = t64()
        ci, B_ci = t64()
        tt("dve", ta, a1, lr, ALU.mult, [B_a1, B_lr], [B_ta])
        tt("dve", tb, lbi, li, ALU.mult, [B_lbi, B_li], [B_tb])
        tt("dve", ta, ta, tb, ALU.add, [B_ta, B_tb], [B_ta])
        tt("dve", cr, ta, inv, ALU.mult, [B_ta, B_inv], [B_cr])
        tt("dve", ta, lbi, lr, ALU.mult, [B_lbi, B_lr], [B_ta])
        tt("dve", tb, a1, li, ALU.mult, [B_a1, B_li], [B_tb])
        tt("dve", ta, ta, tb, ALU.subtract, [B_ta, B_tb], [B_ta])
        tt("dve", ci, ta, inv, ALU.mult, [B_ta, B_inv], [B_ci])
        Er = alloc(9 * LQ).rearrange("p (k q) -> p k q", k=9)
        Ei = alloc(9 * LQ).rearrange("p (k q) -> p k q", k=9)
        B_E = Buf()
        op("dve", "memset", [], [B_E], Er[:, 0, :], 1.0)
        op("dve", "memset", [], [B_E], Ei[:, 0, :], 0.0)
        op("dve", "tensor_copy", [B_lbr], [B_E], out=Er[:, 1, :], in_=lbr)
        op("dve", "tensor_copy", [B_lbi], [B_E], out=Ei[:, 1, :], in_=lbi)
        for k in range(2, 9):
            tt("dve", ta, Er[:, k - 1, :], lbr, ALU.mult, [B_E, B_lbr], [B_ta])
            tt("dve", tb, Ei[:, k - 1, :], lbi, ALU.mult, [B_E, B_lbi], [B_tb])
            tt("dve", Er[:, k, :], ta, tb, ALU.subtract, [B_ta, B_tb], [B_E])
            tt("dve", ta, Er[:, k - 1, :], lbi, ALU.mult, [B_E, B_lbi], [B_ta])
            tt("dve", tb, Ei[:, k - 1, :], lbr, ALU.mult, [B_E, B_lbr], [B_tb])
            tt("dve", Ei[:, k, :], ta, tb, ALU.add, [B_ta, B_tb], [B_E])
        for _ in range(3):
            csquare(cu, B_cu, su, B_su)
        tabc = alloc(LQ * NCS).rearrange("p (q c) -> p q c", c=NCS)
        tabs = alloc(LQ * NCS).rearrange("p (q c) -> p q c", c=NCS)
        B_tab = Buf()
        op("dve", "tensor_copy", [B_cu], [B_tab], out=tabc[:, :, 0], in_=cu)
        op("dve", "tensor_copy", [B_su], [B_tab], out=tabs[:, :, 0], in_=su)
        tw1 = alloc(LQ * 32).rearrange("p (q c) -> p q c", c=32)
        tw2 = alloc(LQ * 32).rearrange("p (q c) -> p q c", c=32)
        B_tw1, B_tw2 = Buf(), Buf()
        n = 1
        while n < NCS:
            Ac, As = tabc[:, :, 0:n], tabs[:, :, 0:n]
            Bc = tabc[:, :, n - 1:n].to_broadcast([128, LQ, n])
            Bs = tabs[:, :, n - 1:n].to_broadcast([128, LQ, n])
            tt("dve", tw1[:, :, 0:n], Ac, Bc, ALU.mult, [B_tab], [B_tw1])
            tt("dve", tw2[:, :, 0:n], As, Bs, ALU.mult, [B_tab], [B_tw2])
            tt("dve", tabc[:, :, n:2 * n], tw1[:, :, 0:n], tw2[:, :, 0:n], ALU.subtract, [B_tw1, B_tw2], [B_tab])
            tt("dve", tw1[:, :, 0:n], Ac, Bs, ALU.mult, [B_tab], [B_tw1])
            tt("dve", tw2[:, :, 0:n], As, Bc, ALU.mult, [B_tab], [B_tw2])
            tt("dve", tabs[:, :, n:2 * n], tw1[:, :, 0:n], tw2[:, :, 0:n], ALU.add, [B_tw1, B_tw2], [B_tab])
            n *= 2
        for l in range(depth):
            rv = rot_s[l].rearrange("p (r q c) -> p r q c", r=2, q=16)
            dma("sp", rv[:, 0], tabc[:, l * 16:(l + 1) * 16, :], [B_tab], [Buf()], "rot%d" % l)
            dma("sp", rv[:, 1], tabs[:, l * 16:(l + 1) * 16, :], [B_tab], [Buf()], "rot%d" % l)

        def t3(n_, dt=F32):
            return alloc(n_, dt), Buf()

        braw = [t3(256), t3(256)]
        Bm = [t3(512), t3(512)]
        Cm = [t3(512), t3(512)]
        X2 = [t3(256), t3(256)]
        tmpA, B_tmpA = t3(512)
        tmpB, B_tmpB = t3(512)
        EB = [t3(8 * 512), t3(8 * 512)]
        CSt, B_CSt = t3(16 * 8 * 2 * 32, BF16)
        BSt, B_BSt = t3(4 * 8 * 2 * 128, BF16)
        KFt, B_KFt = t3(4 * 8 * 128, BF16)
        Lexp = [t3(16 * 128), t3(16 * 128)]
        Cexp = [t3(16 * 128), t3(16 * 128)]
        for (tl, bl) in Bm + Cm + Lexp + Cexp:
            op("pool", "memset", [], [bl], tl, 0.0)
        CSv = CSt.rearrange("p (q j r c) -> p q j r c", q=16, j=8, r=2)
        BSv = BSt.rearrange("p (b j r c) -> p b j r c", b=4, j=8, r=2)
        KFv = KFt.rearrange("p (b k c) -> p b k c", b=4, k=8)

        def bc32(ap16):
            return ap16.unsqueeze(2).to_broadcast([128, 16, 32])

        for l in range(depth):
            qs = slice(l * 16, (l + 1) * 16)
            for ri, nm in enumerate(("ssm_b_re", "ssm_b_im")):
                dma("sp", braw[ri][0].rearrange("p (q h) -> p q h", h=16),
                    dram[nm][l].rearrange("(q g) p h -> (g p) q h", g=2), [], [braw[ri][1]], "braw%d" % ri)
            crb = cr[:, qs].unsqueeze(2).to_broadcast([128, 16, 16])
            cib = ci[:, qs].unsqueeze(2).to_broadcast([128, 16, 16])
            bre = braw[0][0].rearrange("p (q h) -> p q h", h=16)
            bim = braw[1][0].rearrange("p (q h) -> p q h", h=16)
            tA = tmpA[:, 0:256].rearrange("p (q h) -> p q h", h=16)
            tB = tmpB[:, 0:256].rearrange("p (q h) -> p q h", h=16)
            tC = tmpA[:, 256:512].rearrange("p (q h) -> p q h", h=16)
            for ri in range(2):
                if ri == 0:
                    tt("dve", tA, bre, crb, ALU.mult, [braw[0][1], B_cr], [B_tmpA])
                    tt("dve", tB, bim, cib, ALU.mult, [braw[1][1], B_ci], [B_tmpB])
                    tt("dve", tC, tA, tB, ALU.subtract, [B_tmpA, B_tmpB], [B_tmpA])
                else:
                    tt("dve", tA, bim, crb, ALU.mult, [braw[1][1], B_cr], [B_tmpA])
                    tt("dve", tB, bre, cib, ALU.mult, [braw[0][1], B_ci], [B_tmpB])
                    tt("dve", tC, tA, tB, ALU.add, [B_tmpA, B_tmpB], [B_tmpA])
                bmv = Bm[ri][0].rearrange("p (q g h) -> p q g h", g=2, h=16)
                op("dve", "tensor_copy", [B_tmpA], [Bm[ri][1]], out=bmv[0:64, :, 0, :], in_=tC[0:64])
                op("dve", "tensor_copy", [B_tmpA], [Bm[ri][1]], out=bmv[64:128, :, 1, :], in_=tC[64:128])
            for ri, nm in enumerate(("ssm_c_re", "ssm_c_im")):
                x2v = X2[ri][0].rearrange("p (i c) -> p i c", c=64)
                dma("sp", x2v, dram[nm][l].rearrange("(i g) h p -> (g h) i p", g=8), [], [X2[ri][1]], "x2%d" % ri)
                cmv = Cm[ri][0].rearrange("p (q g h) -> p q g h", g=2, h=16)
                for i in range(4):
                    ps, bps = next_bank()
                    mm(ps[0:64, 0:64], x2v[:, i, :], sel[0], True, True, [X2[ri][1], B_cst], [bps])
                    mm(ps[64:128, 0:64], x2v[:, i, :], sel[1], True, True, [X2[ri][1], B_cst], [bps], tp=(0, 64))
                    pv = ps[:, 0:64].rearrange("p (q h) -> p q h", h=16)
                    op("dve", "tensor_copy", [bps], [Cm[ri][1]], out=cmv[0:64, 4 * i:4 * i + 4, 0, :], in_=pv[0:64])
                    op("dve", "tensor_copy", [bps], [Cm[ri][1]], out=cmv[64:128, 4 * i:4 * i + 4, 1, :], in_=pv[64:128])
            Bmr, Bmi = [Bm[r][0].rearrange("p (q c) -> p q c", c=32) for r in range(2)]
            Cmr, Cmi = [Cm[r][0].rearrange("p (q c) -> p q c", c=32) for r in range(2)]
            tAv = tmpA.rearrange("p (q c) -> p q c", c=32)
            tBv = tmpB.rearrange("p (q c) -> p q c", c=32)
            EBr = EB[0][0].rearrange("p (k q c) -> p k q c", k=8, c=32)
            EBi = EB[1][0].rearrange("p (k q c) -> p k q c", k=8, c=32)
            for k in range(8):
                e = "dve" if k % 2 == 0 else "pool"
                er, ei = bc32(Er[:, k, qs]), bc32(Ei[:, k, qs])
                tt(e, tAv, Bmr, er, ALU.mult, [Bm[0][1], B_E], [B_tmpA])
                tt(e, tBv, Bmi, ei, ALU.mult, [Bm[1][1], B_E], [B_tmpB])
                tt(e, EBr[:, k], tAv, tBv, ALU.subtract, [B_tmpA, B_tmpB], [EB[0][1]])
                tt(e, tAv, Bmi, er, ALU.mult, [Bm[1][1], B_E], [B_tmpA])
                tt(e, tBv, Bmr, ei, ALU.mult, [Bm[0][1], B_E], [B_tmpB])
                tt(e, EBi[:, k], tAv, tBv, ALU.add, [B_tmpA, B_tmpB], [EB[1][1]])
            for j in range(8):
                e = "dve"
                er, ei = bc32(Er[:, j + 1, qs]), bc32(Ei[:, j + 1, qs])
                tt(e, tAv, Cmr, er, ALU.mult, [Cm[0][1], B_E], [B_tmpA])
                tt(e, tBv, Cmi, ei, ALU.mult, [Cm[1][1], B_E], [B_tmpB])
                tt(e, CSv[:, :, j, 0, :], tAv, tBv, ALU.subtract, [B_tmpA, B_tmpB], [B_CSt])
                tt(e, tAv, Cmr, ei, ALU.mult, [Cm[0][1], B_E], [B_tmpA])
                tt(e, tBv, Cmi, er, ALU.mult, [Cm[1][1], B_E], [B_tmpB])
                op(e, "scalar_tensor_tensor", [B_tmpA, B_tmpB], [B_CSt], out=CSv[:, :, j, 1, :], in0=tAv, scalar=-1.0,
                   in1=tBv, op0=ALU.mult, op1=ALU.subtract)
            dma("sp", cst_s[l], CSt, [B_CSt], [Buf()], "cst_s")
            for b in range(4):
                for j0 in range(0, 8, 2):
                    ps, bps = next_bank()
                    for jj in range(2):
                        for ri in range(2):
                            src = EB[ri][0].rearrange("p (k c) -> p k c", k=8)[:, 7 - (j0 + jj), b * 128:(b + 1) * 128]
                            sl = (jj * 2 + ri) * 128
                            mm(ps[:, sl:sl + 128], src, ident, jj == 0 and ri == 0, False, [EB[ri][1], B_cst], [bps])
                    op("act", "activation", [bps], [B_BSt], out=BSv[:, b, j0:j0 + 2].rearrange("p j r c -> p (j r c)"),
                       in_=ps, func=AF.Copy)
            dma("sp", bst_s[l], BSt, [B_BSt], [Buf()], "bst_s")
            for ri in range(2):
                cev = Cexp[ri][0].rearrange("p (b q c) -> p b q c", b=4, q=4)
                cmv4 = Cm[ri][0].rearrange("p (b q c) -> p b q c", b=4, q=4)
                for q4 in range(4):
                    if ri == 0:
                        op("pool", "tensor_copy", [Cm[ri][1]], [Cexp[ri][1]], out=cev[:, :, q4, 32 * q4:32 * q4 + 32],
                           in_=cmv4[:, :, q4, :])
                    else:
                        op("pool", "tensor_scalar", [Cm[ri][1]], [Cexp[ri][1]], out=cev[:, :, q4, 32 * q4:32 * q4 + 32],
                           in0=cmv4[:, :, q4, :], scalar1=-1.0, scalar2=None, op0=ALU.mult)
            for k in range(8):
                for ri in range(2):
                    lev = Lexp[ri][0].rearrange("p (b q c) -> p b q c", b=4, q=4)
                    ebv = EB[ri][0].rearrange("p (k b q c) -> p k b q c", k=8, b=4, q=4)
                    for q4 in range(4):
                        op("pool" if q4 % 2 else "dve", "tensor_copy", [EB[ri][1]], [Lexp[ri][1]],
                           out=lev[:, :, q4, 32 * q4:32 * q4 + 32], in_=ebv[:, k, :, q4, :])
                ps, bps = next_bank()
                first = True
                for b in range(4):
                    for q4 in range(4):
                        for ri in range(2):
                            lq = Lexp[ri][0].rearrange("p (q c) -> p q c", c=128)[:, 4 * b + q4, :]
                            cq = Cexp[ri][0].rearrange("p (q c) -> p q c", c=128)[:, 4 * b + q4, :]
                            mm(ps[:, b * 128:(b + 1) * 128], lq, cq, first, False, [Lexp[ri][1], Cexp[ri][1]], [bps])
                            first = False
                op("act", "activation", [bps], [B_KFt], out=KFv[:, :, k, :], in_=ps.rearrange("p (b c) -> p b c", b=4),
                   func=AF.Copy)
            dma("sp", kfir_s[l], KFt, [B_KFt], [Buf()], "kfir_s")

        dump(RR, 0, [B_RR])
        dump(cr, 64, [B_cr])
        dump(ci, 128, [B_ci])
        dump(Er.rearrange("p k q -> p (k q)"), 192, [B_E])
        dump(Ei.rearrange("p k q -> p (k q)"), 768, [B_E])
        dump(tabc[:, 0:16, :].rearrange("p q c -> p (q c)"), 1344, [B_tab])
        dump(tabs[:, 0:16, :].rearrange("p q c -> p (q c)"), 2368, [B_tab])
        P.barrier(dummies)

        off[0] = persist_end
        h = alloc(KD * TT).rearrange("p (k t) -> p k t", k=KD)
        B_h = [[Buf() for _ in range(NSUB)] for _ in range(KD)]
        xn = alloc(KD * TT, BF16).rearrange("p (k t) -> p k t", k=KD)
        B_xn = [Buf() for _ in range(NSUB)]
        ring = [alloc(RINGW, BF16) for _ in range(NRING)]
        B_ring = [[Buf(), Buf()] for _ in range(NRING)]
        KF = alloc(4 * 8 * 128, BF16)
        BS = alloc(4 * 8 * 2 * 128, BF16)
        CS = alloc(16 * 8 * 2 * 32, BF16)
        ROT = alloc(2 * 16 * 64)
        B_sc = Buf()
        KFm = KF.rearrange("p (b k c) -> p b k c", b=4, k=8)
        BSm = BS.rearrange("p (b j r c) -> p b j r c", b=4, j=8, r=2)
        CSm = CS.rearrange("p (q j r c) -> p q j r c", q=16, j=8, r=2)
        ROTm = ROT.rearrange("p (r q c) -> p r q c", r=2, q=16)
        sq = [alloc(SUB, BF16) for _ in range(4)]
        B_sq = [Buf() for _ in range(4)]
        sq_i = [0]
        rstd = [alloc(SUB) for _ in range(2)]
        B_rstd = [Buf() for _ in range(2)]
        sgt = [alloc(SUB) for _ in range(3)]
        B_sgt = [Buf() for _ in range(3)]
        sgt_i = [0]
        pb = alloc(2 * TT, BF16).rearrange("p (k t) -> p k t", k=2)
        B_pb = Buf()
        u_start = off[0]
        act = alloc(NJ * TT, BF16).rearrange("p (j t) -> p j t", j=NJ)
        B_act = [[Buf() for _ in range(NSUB)] for _ in range(NJ)]
        u_end = off[0]
        off[0] = u_start
        zs_f = alloc(4 * SUB).rearrange("p (b t) -> p b t", b=4)
        B_zsf = [Buf() for _ in range(4)]
        zs_bf = alloc(4 * SUB, BF16).rearrange("p (b t) -> p b t", b=4)
        B_zsb = [Buf() for _ in range(4)]
        zp_f = alloc(4 * (SUB + 16)).rearrange("p (g t) -> p g t", g=4)
        B_zp = [Buf() for _ in range(4)]
        ptmp = [alloc(SUB + 16) for _ in range(3)]
        B_ptmp = [Buf() for _ in range(3)]
        d_bf = alloc(4 * SUB, BF16).rearrange("p (g t) -> p g t", g=4)
        B_dbf = [Buf() for _ in range(4)]
        Vt = [alloc(4 * 2 * NCS).rearrange("p (q r c) -> p q r c", q=4, r=2) for _ in range(2)]
        Wt = [alloc(4 * 2 * NCS).rearrange("p (q r c) -> p q r c", q=4, r=2) for _ in range(2)]
        SFt = [alloc(4 * 2 * (NCS + 1)).rearrange("p (q r c) -> p q r c", q=4, r=2) for _ in range(2)]
        rtmp = [alloc(4 * NCS).rearrange("p (q c) -> p q c", q=4) for _ in range(2)]
        sbf_all = alloc(16 * 2 * NCS, BF16).rearrange("p (q r c) -> p q r c", q=16, r=2)
        B_sbq = [Buf() for _ in range(4)]
        B_V = [Buf() for _ in range(2)]
        B_W = [Buf() for _ in range(2)]
        B_SF = [Buf() for _ in range(2)]
        B_rt = [Buf() for _ in range(2)]
        ys = [alloc(SUB) for _ in range(2)]
        B_ys = [Buf() for _ in range(2)]
        gt = [alloc(SUB) for _ in range(2)]
        B_gt = [Buf() for _ in range(2)]
        gy_bf = alloc(4 * SUB, BF16).rearrange("p (b t) -> p b t", b=4)
        B_gyb = [Buf() for _ in range(4)]
        assert off[0] <= NW
        off[0] = max(off[0], u_end)
        ostage = big[:, u_start:u_start + KD * SUB].rearrange("p (k t) -> p k t", k=KD)
        B_ost = Buf()

        def act_region_bufs():
            r = []
            for row in B_act:
                r.extend(row)
            return r

        mixer_tmp_bufs = (B_zsf + B_zsb + B_zp + B_ptmp + B_dbf + B_V + B_W + B_SF + B_rt + B_sbq + B_ys + B_gt
                          + B_gyb + [B_ost])

        plan = []
        for s in range(n_seq):
            for ti in range(NTILE):
                for l in range(depth):
                    for j in range(NJ):
                        plan.append(("wi", 0, l, j))
                    for m in range(KD):
                        plan.append(("wo", 0, l, m))
                    for sub in range(NSUB):
                        for m2 in range(4):
                            plan.append(("sq", "w_in", l, m2))
                        plan.append(("glu", l))
                        plan.append(("pool", l))
                        for m2 in range(4):
                            plan.append(("sq", "w_out", l, m2))
                    for j in range(NJ):
                        plan.append(("wi", 1, l, j))
                    for m in range(KD):
                        plan.append(("wo", 1, l, m))
                    for m2 in range(4):
                        plan.append(("projm", l, m2))
                        plan.append(("sq", "ple_w_gate", l, m2))
        issued = [0]
        consumed = [0]

        def issue_piece(idx):
            d = plan[idx]
            slot = idx % NRING
            t, B = ring[slot], B_ring[slot]
            key = "ring%d" % slot
            if d[0] == "wi":
                _, f, l, j = d
                w = dram["ffn1_wi" if f == 0 else "ffn2_wi"][l].rearrange("(k p) n -> p k n", p=128)
                tv = t[:, 0:2048].rearrange("p (t k c) -> p t k c", t=2, k=8)
                for tq in range(2):
                    c0 = tq * DFF + j * 128
                    dma("pool", tv[:, tq], w[:, :, c0:c0 + 128], [], [B[tq]], key)
            elif d[0] == "wo":
                _, f, l, m = d
                w = dram["ffn1_wo" if f == 0 else "ffn2_wo"][l].rearrange("(j p) n -> p j n", p=128)
                dma("pool", t[:, 0:NJ * 128].rearrange("p (j c) -> p j c", j=NJ), w[:, :, m * 128:(m + 1) * 128], [], B, key)
            elif d[0] == "sq":
                _, nm, l, m2 = d
                w = dram[nm][l].rearrange("(k p) n -> p k n", p=128)
                dma("pool", t[:, 0:2048].rearrange("p (k c) -> p k c", k=8), w[:, :, m2 * 256:(m2 + 1) * 256], [], B, key)
            elif d[0] == "projm":
                _, l, m2 = d
                w = dram["ple_w_proj"][l].rearrange("(k p) n -> p k n", p=128)
                dma("pool", t[:, 0:512].rearrange("p (k c) -> p k c", k=2), w[:, :, m2 * 256:(m2 + 1) * 256], [], B, key)
            elif d[0] == "glu":
                l = d[1]
                w = dram["ssm_w_glu"][l].rearrange("(k p) n -> p k n", p=128)
                dma("pool", t[:, 0:2048].rearrange("p (k c) -> p k c", k=4), w, [], B, key)
            elif d[0] == "pool":
                l = d[1]
                w = dram["pool_w"][l].rearrange("g p n -> p g n")
                dma("pool", t[:, 0:512].rearrange("p (g c) -> p g c", g=4), w, [], B, key)

        def get_piece(desc):
            idx = consumed[0]
            assert plan[idx] == desc, (plan[idx], desc)
            consumed[0] += 1
            while issued[0] < min(len(plan), idx + NRING - 1):
                issue_piece(issued[0])
                issued[0] += 1
            return ring[idx % NRING], B_ring[idx % NRING]

        def rmsnorm(gain_cols, out_fn):
            for sub in range(NSUB):
                ts = slice(sub * SUB, (sub + 1) * SUB)
                ps, bps = next_bank()
                for k in range(KD):
                    si = sq_i[0] % 4
                    sq_i[0] += 1
                    op("act", "activation", [B_h[k][sub]], [B_sq[si]], out=sq[si], in_=h[:, k, ts], func=AF.Square)
                    mm(ps, ones_bf, sq[si], k == 0, k == KD - 1, [B_ones, B_sq[si]], [bps])
                ri = sub
                op("act", "activation", [bps, B_misc], [B_rstd[ri]], out=rstd[ri], in_=ps, func=AF.Ln, scale=1.0 / D,
                   bias=epsc)
                op("act", "activation", [B_rstd[ri]], [B_rstd[ri]], out=rstd[ri], in_=rstd[ri], func=AF.Exp, scale=-0.5)
                for k in range(KD):
                    out_fn(sub, k, h[:, k, ts], gain_cols[:, k:k + 1], rstd[ri], [B_h[k][sub], B_rstd[ri], B_gains])

        def norm_to_xn(gain_cols):
            def f(sub, k, hk, gcol, rs, reads):
                ts = slice(sub * SUB, (sub + 1) * SUB)
                op("dve", "scalar_tensor_tensor", reads, [B_xn[sub]], out=xn[:, k, ts], in0=hk, scalar=gcol, in1=rs,
                   op0=ALU.mult, op1=ALU.mult)
            rmsnorm(gain_cols, f)

        def ffn(f, l):
            norm_to_xn(gains_v[:, 0 if f == 0 else 2, l, :])
            for j in range(NJ):
                t, B = get_piece(("wi", f, l, j))
                tv = t[:, 0:2048].rearrange("p (t k c) -> p t k c", t=2, k=8)
                for sub in range(NSUB):
                    ts = slice(sub * SUB, (sub + 1) * SUB)
                    gps, bg = next_bank()
                    for k in range(KD):
                        mm(gps, tv[:, 0, k, :], xn[:, k, ts], k == 0, k == KD - 1, B + [B_xn[sub]], [bg])
                    ups, bu = next_bank()
                    for k in range(KD):
                        mm(ups, tv[:, 1, k, :], xn[:, k, ts], k == 0, k == KD - 1, B + [B_xn[sub]], [bu])
                    si = sgt_i[0] % 3
                    sgt_i[0] += 1
                    op("act", "activation", [bg], [B_sgt[si]], out=sgt[si], in_=gps, func=AF.Silu)
                    op("dve", "tensor_tensor", [B_sgt[si], bu], [B_act[j][sub]] , out=act[:, j, ts], in0=ups, in1=sgt[si],
                       op=ALU.mult)
            for m in range(KD):
                t, B = get_piece(("wo", f, l, m))
                for sub in range(NSUB):
                    ts = slice(sub * SUB, (sub + 1) * SUB)
                    yps, by = next_bank()
                    for j in range(NJ):
                        mm(yps, t[:, j * 128:(j + 1) * 128], act[:, j, ts], j == 0, j == NJ - 1, B + [B_act[j][sub]], [by])
                    op("dve", "scalar_tensor_tensor", [by, B_h[m][sub]], [B_h[m][sub]], out=h[:, m, ts], in0=yps, scalar=0.5,
                       in1=h[:, m, ts], op0=ALU.mult, op1=ALU.add)

        def load_ssm_consts(l):
            dma("sp", KF, kfir_s[l], [], [B_sc], "sc")
            dma("sp", BS, bst_s[l], [], [B_sc], "sc")
            dma("sp", CS, cst_s[l], [], [B_sc], "sc")
            dma("sp", ROT, rot_s[l], [], [B_sc], "sc")

        def mixer(l, first_of_seq, sub):
            ts = slice(sub * SUB, (sub + 1) * SUB)
            seq_start = first_of_seq and sub == 0
            for m2 in range(4):
                t, B = get_piece(("sq", "w_in", l, m2))
                tv = t[:, 0:2048].rearrange("p (k c) -> p k c", k=8)
                for mmi in range(2):
                    m = 2 * m2 + mmi
                    zps, bz = next_bank()
                    for k in range(KD):
                        mm(zps, tv[:, k, mmi * 128:(mmi + 1) * 128], xn[:, k, ts], k == 0, k == KD - 1, B + [B_xn[sub]], [bz])
                    if m < 4:
                        op("act", "activation", [bz], [B_zsf[m]], out=zs_f[:, m, :].rearrange("p (j c) -> p c j", c=NCS),
                           in_=zps.rearrange("p (c j) -> p c j", j=TC), func=AF.Copy)
                        op("pool", "tensor_copy", [B_zsf[m]], [B_zsb[m]], out=zs_bf[:, m, :], in_=zs_f[:, m, :])
                    else:
                        g = m - 4
                        if seq_start:
                            op("pool", "memset", [], [B_zp[g]], zp_f[:, g, 0:16], 0.0)
                        else:
                            op("pool", "tensor_copy", [B_hist[l][g]], [B_zp[g]], out=zp_f[:, g, 0:16], in_=hist[:, l, g, :])
                        op("act", "activation", [bz], [B_zp[g]], out=zp_f[:, g, 16:16 + SUB], in_=zps, func=AF.Copy)
            for g, wdw in enumerate(POOL_WINS):
                zz = zp_f[:, g, :]
                cur, Bcur = zz, B_zp[g]
                lo = 0
                sh = 1
                ti_ = 0
                while sh < wdw:
                    nxt, Bn = ptmp[ti_ % 3], B_ptmp[ti_ % 3]
                    ti_ += 1
                    nlo = lo + sh
                    tt("pool", nxt[:, nlo:16 + SUB], cur[:, nlo:16 + SUB], cur[:, nlo - sh:16 + SUB - sh], ALU.add, [Bcur], [Bn])
                    cur, Bcur, lo = nxt, Bn, nlo
                    sh *= 2
                op("dve", "scalar_tensor_tensor", [Bcur, B_zp[g]], [B_dbf[g]], out=d_bf[:, g, :], in0=cur[:, 16:16 + SUB],
                   scalar=1.0 / wdw, in1=zz[:, 16:16 + SUB], op0=ALU.mult, op1=ALU.subtract)
                if seq_start:
                    nxt, Bn = ptmp[ti_ % 3], B_ptmp[ti_ % 3]
                    tt("pool", nxt[:, 0:16], cur[:, 16:32], invc[:, g, :], ALU.mult, [Bcur, B_cst], [Bn])
                    tt("pool", d_bf[:, g, 0:16], nxt[:, 0:16], zz[:, 16:32], ALU.subtract, [Bn, B_zp[g]], [B_dbf[g]])
                op("pool", "tensor_copy", [B_zp[g], Bcur, B_dbf[g]], [B_hist[l][g]], out=hist[:, l, g, :], in_=zp_f[:, g, SUB:SUB + 16])
            zj = zs_bf.rearrange("p b (j c) -> p b j c", c=NCS)
            do_ssm = "nossm" not in flags
            stop_at = 99
            for f_ in flags:
                if f_.startswith("stop"):
                    stop_at = int(f_[4:])
            dd = dbg is not None and l == 0 and sub == 0 and seq_start
            if dd:
                dump(KF, 3392, [B_sc], q="pool")
                dump(BS, 7488, [B_sc], q="pool")
                dump(CS, 15680, [B_sc], q="pool")
                dump(zs_f.rearrange("p b t -> p (b t)"), 23872, B_zsf)
                dump(ROT, 25920, [B_sc])
            ROTq = ROTm.rearrange("p r (b q) c -> p r b q c", q=4)
            carry_q = carry_v.rearrange("p l (b q) r -> p l b q r", q=4)
            sbq = sbf_all.rearrange("p (b q) r c -> p b q r c", q=4)
            if do_ssm:
                Sb = [next_bank() for _ in range(4)]
                firsts = [True] * 4
                for b in range(4):
                    for ri in range(2):
                        for j in range(TC):
                            for q4 in range(4):
                                sps, bs = Sb[q4]
                                Sv = sps.rearrange("p (b r c) -> p b r c", b=4, r=2)
                                mm(Sv[:, b, ri, :], BSm[32 * q4:32 * q4 + 32, b, j, ri, :], zj[32 * q4:32 * q4 + 32, b, j, :],
                                   firsts[q4], False, [B_sc, B_zsb[b]], [bs], tp=(32 * q4, 0))
                                firsts[q4] = False
            Yb = []
            for b in range(4 if do_ssm else 0):
                yps, by = next_bank()
                Yb.append((yps, by))
                mm(yps, KFm[:, b, 0, :], zs_bf[:, b, :], True, False, [B_sc, B_zsb[b]], [by])
                for k in range(1, TC):
                    mm(yps[:, k * NCS:SUB], KFm[:, b, k, :], zs_bf[:, b, 0:(TC - k) * NCS], False, False, [B_sc, B_zsb[b]], [by])
            for q4 in range(4 if (do_ssm and stop_at > 1) else 0):
                par = q4 % 2
                V, W, SF, RT = Vt[par], Wt[par], SFt[par], rtmp[par]
                sps, bs = Sb[q4]
                Sv = sps.rearrange("p (b r c) -> p b r c", b=4, r=2)
                rc = ROTq[:, 0, :, q4, :]
                rs_ = ROTq[:, 1, :, q4, :]
                Bc = B_carry[l][q4]
                tt("dve", RT, Sv[:, :, 1, :], rs_, ALU.mult, [bs, B_sc], [B_rt[par]])
                tt("dve", V[:, :, 0, :], Sv[:, :, 0, :], rc, ALU.mult, [bs, B_sc], [B_V[par]])
                tt("dve", V[:, :, 0, :], V[:, :, 0, :], RT, ALU.add, [B_V[par], B_rt[par]], [B_V[par]])
                tt("dve", RT, Sv[:, :, 0, :], rs_, ALU.mult, [bs, B_sc], [B_rt[par]])
                tt("dve", V[:, :, 1, :], Sv[:, :, 1, :], rc, ALU.mult, [bs, B_sc], [B_V[par]])
                tt("dve", V[:, :, 1, :], V[:, :, 1, :], RT, ALU.subtract, [B_V[par], B_rt[par]], [B_V[par]])
                if stop_at <= 2:
                    continue
                if seq_start:
                    op("dve", "memset", [], [Bc], carry_q[:, l, :, q4, :], 0.0)
                for b in range(4):
                    for ri in range(2):
                        q = 4 * b + q4
                        op("dve", "tensor_tensor_scan", [B_V[par], B_RR, Bc], [B_W[par]], out=W[:, b, ri, :],
                           data0=RR[:, l * 16 + q:l * 16 + q + 1].to_broadcast([128, NCS]), data1=V[:, b, ri, :],
                           initial=carry_v[:, l, q, ri:ri + 1], op0=ALU.mult, op1=ALU.add)
                if stop_at <= 3:
                    continue
                tt("pool", RT, W[:, :, 1, :], rs_, ALU.mult, [B_W[par], B_sc], [B_rt[par]])
                tt("pool", SF[:, :, 0, 1:NCS + 1], W[:, :, 0, :], rc, ALU.mult, [B_W[par], B_sc], [B_SF[par]])
                tt("pool", SF[:, :, 0, 1:NCS + 1], SF[:, :, 0, 1:NCS + 1], RT, ALU.subtract, [B_SF[par], B_rt[par]], [B_SF[par]])
                tt("pool", RT, W[:, :, 0, :], rs_, ALU.mult, [B_W[par], B_sc], [B_rt[par]])
                tt("pool", SF[:, :, 1, 1:NCS + 1], W[:, :, 1, :], rc, ALU.mult, [B_W[par], B_sc], [B_SF[par]])
                tt("pool", SF[:, :, 1, 1:NCS + 1], SF[:, :, 1, 1:NCS + 1], RT, ALU.add, [B_SF[par], B_rt[par]], [B_SF[par]])
                op("pool", "tensor_copy", [Bc], [B_SF[par]], out=SF[:, :, :, 0], in_=carry_q[:, l, :, q4, :])
                op("pool", "tensor_copy", [B_SF[par]], [Bc], out=carry_q[:, l, :, q4, :], in_=SF[:, :, :, NCS])
                op("act", "activation", [B_SF[par]], [B_sbq[q4]], out=sbq[:, :, q4, :, :], in_=SF[:, :, :, 0:NCS], func=AF.Copy)
                if dd and q4 == 0:
                    dump(V.rearrange("p b r c -> p (b r c)"), 27968, [B_V[par]])
                    dump(W.rearrange("p b r c -> p (b r c)"), 28480, [B_W[par]])
                    dump(SF.rearrange("p b r c -> p (b r c)"), 28992, [B_SF[par]])
            for b in range(4 if (do_ssm and stop_at > 4) else 0):
                par = b % 2
                yps, by = Yb[b]
                Yj = yps.rearrange("p (j c) -> p j c", c=NCS)
                for j in range(TC if stop_at > 5 else 0):
                    for q4 in range(4):
                        for ri in range(2):
                            mm(Yj[32 * q4:32 * q4 + 32, j, :], CSm[:, 4 * b + q4, j, ri, :], sbf_all[:, 4 * b + q4, ri, :], False,
                               (j == TC - 1 and q4 == 3 and ri == 1), [B_sc] + B_sbq, [by], tp=(0, 32 * q4))
                if stop_at <= 6:
                    continue
                Y, G = ys[par], gt[par]
                op("dve", "tensor_scalar", [B_zsf[b], B_gains], [B_gt[par]], out=G, in0=zs_f[:, b, :],
                   scalar1=dsk[:, l * 4 + b:l * 4 + b + 1], scalar2=None, op0=ALU.mult)
                if stop_at <= 7:
                    continue
                tt("dve", Y, yps, G, ALU.add, [by, B_gt[par]], [B_ys[par]])
                if dd and b == 0:
                    dump(Y, 29512, [B_ys[par]])
                if stop_at <= 8:
                    continue
                op("act", "activation", [B_ys[par]], [B_gt[par]], out=G, in_=Y, func=AF.Square, scale=math.sqrt(0.044715))
                if stop_at <= 9:
                    continue
                op("dve", "scalar_tensor_tensor", [B_gt[par], B_ys[par]], [B_gt[par]], out=G, in0=G, scalar=1.0, in1=Y,
                   op0=ALU.add, op1=ALU.mult)
                if stop_at <= 10:
                    continue
                op("act", "activation", [B_gt[par]], [B_gt[par]], out=G, in_=G, func=AF.Sigmoid, scale=1.5957691216057308)
                if stop_at <= 11:
                    continue
                op("dve", "tensor_tensor", [B_gt[par], B_ys[par]], [B_zsf[b]], out=zs_f[:, b, :], in0=G, in1=Y, op=ALU.mult)
                if stop_at <= 12:
                    continue
                op("act", "activation", [B_zsf[b]], [B_gyb[b]], out=gy_bf[:, b, :], in_=zs_f[:, b, :], func=AF.Copy)
            tg, Bg = get_piece(("glu", l))
            tgv = tg[:, 0:2048].rearrange("p (k c) -> p k c", k=4)
            for n_ in range(4 if "nossm" not in flags else 0):
                gps, bg = next_bank()
                for b in range(4):
                    mm(gps, tgv[:, b, n_ * 128:(n_ + 1) * 128], gy_bf[:, b, :], b == 0, b == 3, Bg + [B_gyb[b]], [bg])
                si = sgt_i[0] % 3
                sgt_i[0] += 1
                op("act", "activation", [bg], [B_sgt[si]], out=sgt[si], in_=gps, func=AF.Sigmoid)
                op("dve", "tensor_tensor", [B_sgt[si], B_zsf[n_]], [B_xn[sub]], out=xn[:, n_, ts].rearrange("p (c j) -> p c j", j=TC),
                   in0=sgt[si].rearrange("p (j c) -> p c j", c=NCS), in1=zs_f[:, n_, :].rearrange("p (j c) -> p c j", c=NCS),
                   op=ALU.mult)
            if "nossm" in flags:
                for n_ in range(4):
                    op("dve", "tensor_copy", [B_zsb[n_]], [B_xn[sub]], out=xn[:, n_, ts], in_=zs_bf[:, n_, :])
            tp_, Bp = get_piece(("pool", l))
            tpv = tp_[:, 0:512].rearrange("p (g c) -> p g c", g=4)
            for g, wdw in enumerate(POOL_WINS):
                pps, bp = next_bank()
                mm(pps, tpv[:, g, :], d_bf[:, g, :], True, True, Bp + [B_dbf[g]], [bp])
                op("dve", "tensor_scalar", [bp, B_gains], [B_xn[sub]], out=xn[:, 4 + g, ts], in0=pps,
                   scalar1=psc[:, l * 4 + g:l * 4 + g + 1], scalar2=None, op0=ALU.mult)
            if dbg is not None and l == 0 and sub == 0 and seq_start:
                for k_ in range(8):
                    dump(xn[:, k_, ts], 30024 + 512 * k_, [B_xn[sub]], q="pool")
            for m2 in range(4):
                t, B = get_piece(("sq", "w_out", l, m2))
                tv = t[:, 0:2048].rearrange("p (k c) -> p k c", k=8)
                for mmi in range(2):
                    m = 2 * m2 + mmi
                    ops_, bo = next_bank()
                    for k in range(KD):
                        mm(ops_, tv[:, k, mmi * 128:(mmi + 1) * 128], xn[:, k, ts], k == 0, k == KD - 1, B + [B_xn[sub]], [bo])
                    tt("dve", h[:, m, ts], ops_, h[:, m, ts], ALU.add, [bo, B_h[m][sub]], [B_h[m][sub]])

        def ple(l, tok0):
            norm_to_xn(gains_v[:, 3, l, :])
            dma("pool", pb, dram["pT"][l].rearrange("(k p) n -> p k n", p=128)[:, :, tok0:tok0 + TT], [], [B_pb], "pb")
            for m2 in range(4):
                tpj, Bpj = get_piece(("projm", l, m2))
                tpjv = tpj[:, 0:512].rearrange("p (k c) -> p k c", k=2)
                t, B = get_piece(("sq", "ple_w_gate", l, m2))
                tv = t[:, 0:2048].rearrange("p (k c) -> p k c", k=8)
                for mmi in range(2):
                    m = 2 * m2 + mmi
                    for sub in range(NSUB):
                        ts = slice(sub * SUB, (sub + 1) * SUB)
                        gps, bg = next_bank()
                        for k in range(KD):
                            mm(gps, tv[:, k, mmi * 128:(mmi + 1) * 128], xn[:, k, ts], k == 0, k == KD - 1, B + [B_xn[sub]], [bg])
                        pps, bp = next_bank()
                        for k in range(2):
                            mm(pps, tpjv[:, k, mmi * 128:(mmi + 1) * 128], pb[:, k, ts], k == 0, k == 1, Bpj + [B_pb], [bp])
                        si = sgt_i[0] % 3
                        sgt_i[0] += 1
                        if "ple_mm" in flags:
                            op("dve", "tensor_copy", [bg], [B_sgt[si]], out=sgt[si], in_=gps)
                            op("dve", "tensor_copy", [bp], [B_sgt[si]], out=sgt[si], in_=pps)
                            continue
                        op("act", "activation", [bg], [B_sgt[si]], out=sgt[si], in_=gps, func=AF.Sigmoid)
                        tt("dve", sgt[si], pps, sgt[si], ALU.mult, [B_sgt[si], bp], [B_sgt[si]])
                        tt("dve", h[:, m, ts], sgt[si], h[:, m, ts], ALU.add, [B_sgt[si], B_h[m][sub]], [B_h[m][sub]])

        out_dmas = []
        if "prolog" in flags:
            plan = []
            op("dve", "memset", [], [B_ost], ostage, 0.0)
            out_dmas.append(dma("sp", outT.rearrange("(k p) n -> p k n", p=128)[:, :, 0:SUB], ostage, [B_ost], [B_ost], "ost"))
        for s in range(n_seq if "prolog" not in flags else 0):
            for ti in range(NTILE):
                tok0 = s * seq_len + ti * TT
                for k in range(KD):
                    dma("sp", h[:, k, :], dram["xT"][k * 128:(k + 1) * 128, tok0:tok0 + TT], [], B_h[k], "hx%d" % k)
                for l in range(depth):
                    load_ssm_consts(l)
                    if "noffn" not in flags:
                        ffn(0, l)
                    else:
                        for _ in range(NJ + KD):
                            get_piece(plan[consumed[0]])
                    norm_to_xn(gains_v[:, 1, l, :])
                    guard = act_region_bufs()
                    op("pool", "memset", [], guard + mixer_tmp_bufs, dummies["pool"], 0.0)
                    for sub in range(NSUB):
                        if "nomix" in flags:
                            for _ in range(10):
                                get_piece(plan[consumed[0]])
                        else:
                            mixer(l, ti == 0, sub)
                    op("pool", "memset", [], guard + mixer_tmp_bufs, dummies["pool"], 0.0)
                    if "noffn" not in flags:
                        ffn(1, l)
                    else:
                        for _ in range(NJ + KD):
                            get_piece(plan[consumed[0]])
                    if "nople" in flags:
                        for _ in range(8):
                            get_piece(plan[consumed[0]])
                    else:
                        ple(l, tok0)
                op("pool", "memset", [], act_region_bufs() + mixer_tmp_bufs, dummies["pool"], 0.0)

                def fo(sub, k, hk, gcol, rs, reads):
                    op("dve", "scalar_tensor_tensor", reads + [B_ost], [B_ost], out=ostage[:, k, :], in0=hk, scalar=gcol, in1=rs,
                       op0=ALU.mult, op1=ALU.mult)
                    if k == KD - 1:
                        t0 = tok0 + sub * SUB
                        d_ = dma("sp", outT.rearrange("(k p) n -> p k n", p=128)[:, :, t0:t0 + SUB], ostage, [B_ost], [B_ost], "ost")
                        out_dmas.append(d_)
                rmsnorm(gfin, fo)
        assert consumed[0] == len(plan), (consumed[0], len(plan))
        P.emit(final_waits=out_dmas + dbg_dmas)
    return nc


_CACHE = {}


def _get_nc(n_seq, seq_len, depth, flags=frozenset()):
    key = (n_seq, seq_len, depth, flags)
    if key not in _CACHE:
        _CACHE[key] = build(n_seq, seq_len, depth, flags)
    return _CACHE[key]


def kernel(**inputs):
    x = np.asarray(inputs["x"], dtype=np.float32)
    p = np.asarray(inputs["p"], dtype=np.float32)
    B, L, _ = x.shape
    ncores = 8
    n_seq = B // ncores
    nc = _get_nc(n_seq, L, DEPTH)
    consts = host_consts()
    wmap = {name: np.ascontiguousarray(np.asarray(inputs[name], dtype=np.float32)) for name, _ in WSHAPES}
    in_maps = []
    for c in range(ncores):
        xs = x[c * n_seq:(c + 1) * n_seq].reshape(n_seq * L, D)
        ps_ = p[:, c * n_seq:(c + 1) * n_seq].reshape(DEPTH, n_seq * L, 256)
        m = {"xT": np.ascontiguousarray(xs.T), "pT": np.ascontiguousarray(ps_.transpose(0, 2, 1)), "consts": consts}
        m.update(wmap)
        in_maps.append(m)
    res = run_bass_kernel_spmd(nc, in_maps, core_ids=list(range(ncores)))
    out = np.empty((B, L, D), np.float32)
    for c in range(ncores):
        o = np.asarray(res.results[c]["outT"])
        out[c * n_seq:(c + 1) * n_seq] = o.T.reshape(n_seq, L, D)
    return out
```

```python
import contextlib
import math
import numpy as np
import concourse.bass as bass
import concourse.mybir as mybir
from concourse.bass_utils import run_bass_kernel_spmd

F32 = mybir.dt.float32
BF16 = mybir.dt.bfloat16
AF = mybir.ActivationFunctionType
ALU = mybir.AluOpType

ENGS = ("pe", "act", "dve", "pool", "sp")


class Buf:
    __slots__ = ("w", "rs")

    def __init__(self):
        self.w = None
        self.rs = []


class Op:
    __slots__ = ("eng", "fn", "deps", "sig", "val", "dma", "semkey")

    def __init__(self, eng, fn, dma):
        self.eng = eng
        self.fn = fn
        self.deps = []
        self.sig = False
        self.val = 0
        self.dma = dma
        self.semkey = None


class Prog:
    def __init__(self, nc):
        self.nc = nc
        self.ops = {e: [] for e in ENGS}
        self.last_dma = {}

    def add(self, eng, fn, reads=(), writes=(), dma=False, semkey=None, extra=()):
        op = Op(eng, fn, dma)
        op.semkey = semkey
        deps = op.deps
        for b in reads:
            if b.w is not None:
                deps.append(b.w)
        for b in writes:
            if b.w is not None:
                deps.append(b.w)
            deps.extend(b.rs)
        deps.extend(extra)
        for b in reads:
            if not dma and eng != "pool":
                b.rs = [r for r in b.rs if r.eng != eng or r.dma]
            b.rs.append(op)
        for b in writes:
            b.w = op
            b.rs = []
        self.ops[eng].append(op)
        if dma:
            self.last_dma[semkey] = op
        return op

    def barrier(self, dummies):
        lasts = [self.ops[e][-1] for e in ENGS if self.ops[e]] + list(self.last_dma.values())
        for e in ("act", "dve", "pool"):
            d = dummies[e]
            if e == "act":
                self.add(e, (lambda d: lambda h: h.activation(out=d, in_=d, func=AF.Copy))(d), extra=lasts)
            else:
                self.add(e, (lambda d: lambda h: h.memset(d, 0.0))(d), extra=lasts)
        self.add("sp", lambda h: h.dma_start(out=dummies["spo"], in_=dummies["spi"]), extra=lasts, dma=True, semkey="barrier")

    def emit(self, final_waits=()):
        nc = self.nc
        for e in ENGS:
            for op in self.ops[e]:
                seen = set()
                nd = []
                for d in op.deps:
                    if d is op or id(d) in seen:
                        continue
                    seen.add(id(d))
                    if d.eng == "pe" and op.eng == "pe" and not d.dma and not op.dma:
                        continue
                    nd.append(d)
                    d.sig = True
                op.deps = nd
        for op in final_waits:
            op.sig = True
        cnt = {e: 0 for e in ENGS}
        dma_cnt = {}
        for e in ENGS:
            for op in self.ops[e]:
                if op.dma:
                    k = op.semkey
                    dma_cnt[k] = dma_cnt.get(k, 0) + 16
                    op.val = dma_cnt[k]
                elif op.sig:
                    cnt[e] += 1
                    op.val = cnt[e]
        with contextlib.ExitStack() as st:
            sems = {e: st.enter_context(nc.semaphore("s_" + e)) for e in ENGS}
            dsems = {k: st.enter_context(nc.semaphore("d_%s" % str(k))) for k in dma_cnt}
            block = st.enter_context(nc.Block())

            def sem_of(d):
                return dsems[d.semkey] if d.dma else sems[d.eng]

            def run(e, handle):
                waited = {}
                for op in self.ops[e]:
                    for d in op.deps:
                        s = sem_of(d)
                        key = id(s)
                        if waited.get(key, 0) >= d.val:
                            continue
                        handle.wait_ge(s, d.val)
                        waited[key] = d.val
                    ins = op.fn(handle)
                    if op.dma:
                        ins.then_inc(dsems[op.semkey], 16)
                    elif op.sig:
                        ins.then_inc(sems[e], 1)
                if e == "sp":
                    for d in final_waits:
                        handle.wait_ge(sem_of(d), d.val)

            @block.tensor
            def _(h):
                run("pe", h)

            @block.scalar
            def _(h):
                run("act", h)

            @block.vector
            def _(h):
                run("dve", h)

            @block.gpsimd
            def _(h):
                run("pool", h)

            @block.sync
            def _(h):
                run("sp", h)


D = 1024
KD = 8
DFF = 2816
NJ = 22
TT = 1024
SUB = 512
NSUB = 2
TC = 8
NCS = SUB // TC
DEPTH = 4
EPS = 1e-6
NRING = 5
RINGW = 2816
POOL_WINS = (2, 4, 8, 16)

WSHAPES = [
    ("ffn1_norm", [DEPTH, D]), ("ffn1_wi", [DEPTH, D, 2 * DFF]), ("ffn1_wo", [DEPTH, DFF, D]),
    ("mix_norm", [DEPTH, D]), ("w_in", [DEPTH, D, D]),
    ("ssm_lambda_re", [DEPTH, 32, 64]), ("ssm_lambda_im", [DEPTH, 32, 64]), ("ssm_log_dt", [DEPTH, 32]),
    ("ssm_b_re", [DEPTH, 32, 64, 16]), ("ssm_b_im", [DEPTH, 32, 64, 16]),
    ("ssm_c_re", [DEPTH, 32, 16, 64]), ("ssm_c_im", [DEPTH, 32, 16, 64]),
    ("ssm_d", [DEPTH, 512]), ("ssm_w_glu", [DEPTH, 512, 512]),
    ("pool_w", [DEPTH, 4, 128, 128]), ("pool_scale", [DEPTH, 512]), ("w_out", [DEPTH, D, D]),
    ("ffn2_norm", [DEPTH, D]), ("ffn2_wi", [DEPTH, D, 2 * DFF]), ("ffn2_wo", [DEPTH, DFF, D]),
    ("ple_norm", [DEPTH, D]), ("ple_w_gate", [DEPTH, D, D]), ("ple_w_proj", [DEPTH, 256, D]),
    ("final_norm", [D]),
]
NCONST = 128 + 64 + 64 + 64


def host_consts():
    c = np.zeros((128, NCONST), np.float32)
    c[:, 0:128] = np.eye(128, dtype=np.float32)
    for g2 in range(2):
        for q4 in range(4):
            for h in range(16):
                c[(2 * q4 + g2) * 16 + h, 128 + 64 * g2 + q4 * 16 + h] = 1.0
    for gi, w in enumerate(POOL_WINS):
        for t in range(16):
            c[:, 256 + gi * 16 + t] = 1.0 / min(t + 1, w)
    return c


def build(n_seq, seq_len, depth, flags=frozenset()):
    NTOK = n_seq * seq_len
    NTILE = seq_len // TT
    nc = bass.Bass("TRN2", target_bir_lowering=False)
    dram = {}
    dram["xT"] = nc.dram_tensor("xT", [D, NTOK], F32, kind="ExternalInput").ap()
    dram["pT"] = nc.dram_tensor("pT", [DEPTH, 256, NTOK], F32, kind="ExternalInput").ap()
    dram["consts"] = nc.dram_tensor("consts", [128, NCONST], F32, kind="ExternalInput").ap()
    for name, shp in WSHAPES:
        dram[name] = nc.dram_tensor(name, shp, F32, kind="ExternalInput").ap()
    outT = nc.dram_tensor("outT", [D, NTOK], F32, kind="ExternalOutput").ap()
    kfir_s = nc.dram_tensor("kfir_s", [DEPTH, 128, 4 * 8 * 128], BF16, kind="Internal").ap()
    bst_s = nc.dram_tensor("bst_s", [DEPTH, 128, 4 * 8 * 2 * 128], BF16, kind="Internal").ap()
    cst_s = nc.dram_tensor("cst_s", [DEPTH, 128, 16 * 8 * 2 * 32], BF16, kind="Internal").ap()
    rot_s = nc.dram_tensor("rot_s", [DEPTH, 128, 2 * 16 * 64], F32, kind="Internal").ap()

    dbg = None
    if "dbg" in flags:
        dbg = nc.dram_tensor("dbg", [128, 40960], F32, kind="ExternalOutput").ap()
    dbg_dmas = []
    dbg_i = [0]
    P = Prog(nc)
    NW = 53100

    with contextlib.ExitStack() as st:
        big = st.enter_context(nc.sbuf_tensor("big", [128, NW], F32))[:]
        banks = [st.enter_context(nc.psum_tensor("bank%d" % i, [128, 512], F32))[:] for i in range(8)]
        bank_bufs = [Buf() for _ in range(8)]
        bank_i = [0]

        def next_bank():
            i = bank_i[0] % 8
            bank_i[0] += 1
            return banks[i], bank_bufs[i]

        off = [0]

        def alloc(nelem, dt=F32):
            words = nelem if dt == F32 else (nelem + 1) // 2
            a = off[0]
            off[0] += words
            assert off[0] <= NW, ("SBUF arena overflow", off[0])
            v = big[:, a:a + words]
            return v if dt == F32 else v.bitcast(dt)

        def op(eng, method, reads, writes, *args, **kw):
            return P.add(eng, lambda h: getattr(h, method)(*args, **kw), reads, writes)

        def dma(q, out, in_, reads, writes, semkey):
            return P.add(q, lambda h: h.dma_start(out=out, in_=in_), reads, writes, dma=True, semkey=semkey)

        def dump(ap2d, col, reads, q="sp"):
            if dbg is None:
                return
            n_ = ap2d.shape[1]
            dbg_i[0] += 1
            dbg_dmas.append(dma(q, dbg[:, col:col + n_], ap2d, reads, [Buf()], "dbg%d" % dbg_i[0]))

        def mm(out, lhsT, rhs, start, stop, reads, writes, tp=None):
            kw = dict(start=start, stop=stop, skip_group_check=True)
            if tp is not None:
                kw["tile_position"] = tp
            return P.add("pe", lambda h: h.matmul(out, lhsT=lhsT, rhs=rhs, **kw), reads, writes)

        cst = alloc(NCONST)
        B_cst = Buf()
        ident = cst[:, 0:128]
        sel = [cst[:, 128:192], cst[:, 192:256]]
        invc = cst[:, 256:320].rearrange("p (g t) -> p g t", t=16)
        dma("sp", cst, dram["consts"], [], [B_cst], "cst")
        misc = alloc(16)
        B_misc = Buf()
        op("dve", "memset", [], [B_misc], misc[:, 0:1], math.pi / 2)
        op("dve", "memset", [], [B_misc], misc[:, 1:2], EPS)
        op("dve", "memset", [], [B_misc], misc[:, 2:3], 0.0)
        halfpi = misc[:, 0:1]
        epsc = misc[:, 1:2]
        ones_bf = alloc(128, BF16)
        B_ones = Buf()
        op("dve", "memset", [], [B_ones], ones_bf, 1.0)
        gains = alloc(4 * DEPTH * 8)
        B_gains = Buf()
        gfin = alloc(8)
        dsk = alloc(DEPTH * 4)
        psc = alloc(DEPTH * 4)
        RR = alloc(DEPTH * 16)
        B_RR = Buf()
        carry = alloc(DEPTH * 16 * 2)
        carry_v = carry.rearrange("p (l q r) -> p l q r", l=DEPTH, r=2)
        B_carry = [[Buf() for _ in range(4)] for _ in range(DEPTH)]
        hist = alloc(DEPTH * 4 * 16).rearrange("p (l g t) -> p l g t", l=DEPTH, g=4)
        B_hist = [[Buf() for _ in range(4)] for _ in range(DEPTH)]
        dummies = {e: alloc(2) for e in ("act", "dve", "pool", "spo", "spi")}
        op("dve", "memset", [], [], dummies["spi"], 0.0)
        op("dve", "memset", [], [], dummies["act"], 0.0)
        persist_end = off[0]

        stage = alloc(128)
        B_stage = Buf()

        def load_T(rows_ap, R, dst, B_dst, evac="dve"):
            dma("sp", stage[0:R, :], rows_ap, [], [B_stage], "stage")
            ps, bps = next_bank()
            mm(ps[:, 0:R], stage[0:R, :], ident[0:R, 0:R], True, True, [B_stage, B_cst], [bps])
            op(evac, "tensor_copy", [bps], [B_dst], out=dst, in_=ps[:, 0:R])

        for ki, nm in enumerate(("ffn1_norm", "mix_norm", "ffn2_norm", "ple_norm")):
            load_T(dram[nm].rearrange("l (k p) -> (l k) p", p=128), DEPTH * 8,
                   gains[:, ki * DEPTH * 8:(ki + 1) * DEPTH * 8], B_gains)
        load_T(dram["final_norm"].rearrange("(k p) -> k p", p=128), 8, gfin, B_gains)
        load_T(dram["ssm_d"].rearrange("l (b p) -> (l b) p", p=128), DEPTH * 4, dsk, B_gains)
        load_T(dram["pool_scale"].rearrange("l (b p) -> (l b) p", p=128), DEPTH * 4, psc, B_gains)
        gains_v = gains.rearrange("p (n l k) -> p n l k", n=4, l=DEPTH)

        LQ = DEPTH * 16

        def t64():
            return alloc(LQ), Buf()

        lr, B_lr = t64()
        li, B_li = t64()
        ldt, B_ldt = t64()
        load_T(dram["ssm_lambda_re"].rearrange("l (q g) p -> (l q) (g p)", g=2), LQ, lr, B_lr)
        load_T(dram["ssm_lambda_im"].rearrange("l (q g) p -> (l q) (g p)", g=2), LQ, li, B_li)
        ld2 = alloc(2)
        B_ld2 = Buf()
        dma("sp", ld2[0:LQ, :], dram["ssm_log_dt"].rearrange("l (q g) -> (l q) g", g=2), [], [B_ld2], "ld2")
        stage2 = alloc(128)
        B_stage2 = Buf()
        for g2 in range(2):
            op("dve", "tensor_copy", [B_ld2], [B_stage2], out=stage2[0:LQ, g2 * 64:(g2 + 1) * 64],
               in_=ld2[0:LQ, g2:g2 + 1].to_broadcast([LQ, 64]))
        ps, bps = next_bank()
        mm(ps[:, 0:LQ], stage2[0:LQ, :], ident[0:LQ, 0:LQ], True, True, [B_stage2, B_cst], [bps])
        op("dve", "tensor_copy", [bps], [B_ldt], out=ldt, in_=ps[:, 0:LQ])

        def tt(eng, out, a, b, o, reads, writes):
            return op(eng, "tensor_tensor", reads, writes, out=out, in0=a, in1=b, op=o)

        dt_, B_dt = t64()
        op("act", "activation", [B_ldt], [B_dt], out=dt_, in_=ldt, func=AF.Exp)
        mr, B_mr = t64()
        mi, B_mi = t64()
        tt("dve", mr, lr, dt_, ALU.mult, [B_lr, B_dt], [B_mr])
        tt("dve", mi, li, dt_, ALU.mult, [B_li, B_dt], [B_mi])
        em, B_em = t64()
        op("act", "activation", [B_mr], [B_em], out=em, in_=mr, func=AF.Exp)
        op("act", "activation", [B_mr], [B_RR], out=RR, in_=mr, func=AF.Exp, scale=float(TC))
        cu, B_cu = t64()
        su, B_su = t64()
        op("act", "activation", [B_mi], [B_su], out=su, in_=mi, func=AF.Sin, scale=1.0 / 64)
        op("act", "activation", [B_mi, B_misc], [B_cu], out=cu, in_=mi, func=AF.Sin, scale=1.0 / 64, bias=halfpi)
        ta, B_ta = t64()
        tb, B_tb = t64()

        def csquare(c, Bc, s, Bs):
            tt("dve", ta, c, c, ALU.mult, [Bc], [B_ta])
            tt("dve", tb, s, s, ALU.mult, [Bs], [B_tb])
            op("dve", "scalar_tensor_tensor", [Bc, Bs], [Bs], out=s, in0=c, scalar=2.0, in1=s,
               op0=ALU.mult, op1=ALU.mult)
            tt("dve", c, ta, tb, ALU.subtract, [B_ta, B_tb], [Bc])

        for _ in range(6):
            csquare(cu, B_cu, su, B_su)
        lbr, B_lbr = t64()
        lbi, B_lbi = t64()
        tt("dve", lbr, em, cu, ALU.mult, [B_em, B_cu], [B_lbr])
        tt("dve", lbi, em, su, ALU.mult, [B_em, B_su], [B_lbi])
        a1, B_a1 = t64()
        op("dve", "tensor_scalar", [B_lbr], [B_a1], out=a1, in0=lbr, scalar1=-1.0, scalar2=None, op0=ALU.add)
        inv, B_inv = t64()
        tt("dve", ta, lr, lr, ALU.mult, [B_lr], [B_ta])
        tt("dve", tb, li, li, ALU.mult, [B_li], [B_tb])
        tt("dve", ta, ta, tb, ALU.add, [B_ta, B_tb], [B_ta])
        op("dve", "reciprocal", [B_ta], [B_inv], out=inv, in_=ta)
        cr, B_cr = t64()
        ci, B_ci = t64()
        tt("dve", ta, a1, lr, ALU.mult, [B_a1, B_lr], [B_ta])
        tt("dve", tb, lbi, li, ALU.mult, [B_lbi, B_li], [B_tb])
        tt("dve", ta, ta, tb, ALU.add, [B_ta, B_tb], [B_ta])
        tt("dve", cr, ta, inv, ALU.mult, [B_ta, B_inv], [B_cr])
        tt("dve", ta, lbi, lr, ALU.mult, [B_lbi, B_lr], [B_ta])
        tt("dve", tb, a1, li, ALU.mult, [B_a1, B_li], [B_tb])
        tt("dve", ta, ta, tb, ALU.subtract, [B_ta, B_tb], [B_ta])
        tt("dve", ci, ta, inv, ALU.mult, [B_ta, B_inv], [B_ci])
        Er = alloc(9 * LQ).rearrange("p (k q) -> p k q", k=9)
        Ei = alloc(9 * LQ).rearrange("p (k q) -> p k q", k=9)
        B_E = Buf()
        op("dve", "memset", [], [B_E], Er[:, 0, :], 1.0)
        op("dve", "memset", [], [B_E], Ei[:, 0, :], 0.0)
        op("dve", "tensor_copy", [B_lbr], [B_E], out=Er[:, 1, :], in_=lbr)
        op("dve", "tensor_copy", [B_lbi], [B_E], out=Ei[:, 1, :], in_=lbi)
        for k in range(2, 9):
            tt("dve", ta, Er[:, k - 1, :], lbr, ALU.mult, [B_E, B_lbr], [B_ta])
            tt("dve", tb, Ei[:, k - 1, :], lbi, ALU.mult, [B_E, B_lbi], [B_tb])
            tt("dve", Er[:, k, :], ta, tb, ALU.subtract, [B_ta, B_tb], [B_E])
            tt("dve", ta, Er[:, k - 1, :], lbi, ALU.mult, [B_E, B_lbi], [B_ta])
            tt("dve", tb, Ei[:, k - 1, :], lbr, ALU.mult, [B_E, B_lbr], [B_tb])
            tt("dve", Ei[:, k, :], ta, tb, ALU.add, [B_ta, B_tb], [B_E])
        for _ in range(3):
            csquare(cu, B_cu, su, B_su)
        tabc = alloc(LQ * NCS).rearrange("p (q c) -> p q c", c=NCS)
        tabs = alloc(LQ * NCS).rearrange("p (q c) -> p q c", c=NCS)
        B_tab = Buf()
        op("dve", "tensor_copy", [B_cu], [B_tab], out=tabc[:, :, 0], in_=cu)
        op("dve", "tensor_copy", [B_su], [B_tab], out=tabs[:, :, 0], in_=su)
        tw1 = alloc(LQ * 32).rearrange("p (q c) -> p q c", c=32)
        tw2 = alloc(LQ * 32).rearrange("p (q c) -> p q c", c=32)
        B_tw1, B_tw2 = Buf(), Buf()
        n = 1
        while n < NCS:
            Ac, As = tabc[:, :, 0:n], tabs[:, :, 0:n]
            Bc = tabc[:, :, n - 1:n].to_broadcast([128, LQ, n])
            Bs = tabs[:, :, n - 1:n].to_broadcast([128, LQ, n])
            tt("dve", tw1[:, :, 0:n], Ac, Bc, ALU.mult, [B_tab], [B_tw1])
            tt("dve", tw2[:, :, 0:n], As, Bs, ALU.mult, [B_tab], [B_tw2])
            tt("dve", tabc[:, :, n:2 * n], tw1[:, :, 0:n], tw2[:, :, 0:n], ALU.subtract, [B_tw1, B_tw2], [B_tab])
            tt("dve", tw1[:, :, 0:n], Ac, Bs, ALU.mult, [B_tab], [B_tw1])
            tt("dve", tw2[:, :, 0:n], As, Bc, ALU.mult, [B_tab], [B_tw2])
            tt("dve", tabs[:, :, n:2 * n], tw1[:, :, 0:n], tw2[:, :, 0:n], ALU.add, [B_tw1, B_tw2], [B_tab])
            n *= 2
        for l in range(depth):
            rv = rot_s[l].rearrange("p (r q c) -> p r q c", r=2, q=16)
            dma("sp", rv[:, 0], tabc[:, l * 16:(l + 1) * 16, :], [B_tab], [Buf()], "rot%d" % l)
            dma("sp", rv[:, 1], tabs[:, l * 16:(l + 1) * 16, :], [B_tab], [Buf()], "rot%d" % l)

        def t3(n_, dt=F32):
            return alloc(n_, dt), Buf()

        braw = [t3(256), t3(256)]
        Bm = [t3(512), t3(512)]
        Cm = [t3(512), t3(512)]
        X2 = [t3(256), t3(256)]
        tmpA, B_tmpA = t3(512)
        tmpB, B_tmpB = t3(512)
        EB = [t3(8 * 512), t3(8 * 512)]
        CSt, B_CSt = t3(16 * 8 * 2 * 32, BF16)
        BSt, B_BSt = t3(4 * 8 * 2 * 128, BF16)
        KFt, B_KFt = t3(4 * 8 * 128, BF16)
        Lexp = [t3(16 * 128), t3(16 * 128)]
        Cexp = [t3(16 * 128), t3(16 * 128)]
        for (tl, bl) in Bm + Cm + Lexp + Cexp:
            op("pool", "memset", [], [bl], tl, 0.0)
        CSv = CSt.rearrange("p (q j r c) -> p q j r c", q=16, j=8, r=2)
        BSv = BSt.rearrange("p (b j r c) -> p b j r c", b=4, j=8, r=2)
        KFv = KFt.rearrange("p (b k c) -> p b k c", b=4, k=8)

        def bc32(ap16):
            return ap16.unsqueeze(2).to_broadcast([128, 16, 32])

        for l in range(depth):
            qs = slice(l * 16, (l + 1) * 16)
            for ri, nm in enumerate(("ssm_b_re", "ssm_b_im")):
                dma("sp", braw[ri][0].rearrange("p (q h) -> p q h", h=16),
                    dram[nm][l].rearrange("(q g) p h -> (g p) q h", g=2), [], [braw[ri][1]], "braw%d" % ri)
            crb = cr[:, qs].unsqueeze(2).to_broadcast([128, 16, 16])
            cib = ci[:, qs].unsqueeze(2).to_broadcast([128, 16, 16])
            bre = braw[0][0].rearrange("p (q h) -> p q h", h=16)
            bim = braw[1][0].rearrange("p (q h) -> p q h", h=16)
            tA = tmpA[:, 0:256].rearrange("p (q h) -> p q h", h=16)
            tB = tmpB[:, 0:256].rearrange("p (q h) -> p q h", h=16)
            tC = tmpA[:, 256:512].rearrange("p (q h) -> p q h", h=16)
            for ri in range(2):
                if ri == 0:
                    tt("dve", tA, bre, crb, ALU.mult, [braw[0][1], B_cr], [B_tmpA])
                    tt("dve", tB, bim, cib, ALU.mult, [braw[1][1], B_ci], [B_tmpB])
                    tt("dve", tC, tA, tB, ALU.subtract, [B_tmpA, B_tmpB], [B_tmpA])
                else:
                    tt("dve", tA, bim, crb, ALU.mult, [braw[1][1], B_cr], [B_tmpA])
                    tt("dve", tB, bre, cib, ALU.mult, [braw[0][1], B_ci], [B_tmpB])
                    tt("dve", tC, tA, tB, ALU.add, [B_tmpA, B_tmpB], [B_tmpA])
                bmv = Bm[ri][0].rearrange("p (q g h) -> p q g h", g=2, h=16)
                op("dve", "tensor_copy", [B_tmpA], [Bm[ri][1]], out=bmv[0:64, :, 0, :], in_=tC[0:64])
                op("dve", "tensor_copy", [B_tmpA], [Bm[ri][1]], out=bmv[64:128, :, 1, :], in_=tC[64:128])
            for ri, nm in enumerate(("ssm_c_re", "ssm_c_im")):
                x2v = X2[ri][0].rearrange("p (i c) -> p i c", c=64)
                dma("sp", x2v, dram[nm][l].rearrange("(i g) h p -> (g h) i p", g=8), [], [X2[ri][1]], "x2%d" % ri)
                cmv = Cm[ri][0].rearrange("p (q g h) -> p q g h", g=2, h=16)
                for i in range(4):
                    ps, bps = next_bank()
                    mm(ps[0:64, 0:64], x2v[:, i, :], sel[0], True, True, [X2[ri][1], B_cst], [bps])
                    mm(ps[64:128, 0:64], x2v[:, i, :], sel[1], True, True, [X2[ri][1], B_cst], [bps], tp=(0, 64))
                    pv = ps[:, 0:64].rearrange("p (q h) -> p q h", h=16)
                    op("dve", "tensor_copy", [bps], [Cm[ri][1]], out=cmv[0:64, 4 * i:4 * i + 4, 0, :], in_=pv[0:64])
                    op("dve", "tensor_copy", [bps], [Cm[ri][1]], out=cmv[64:128, 4 * i:4 * i + 4, 1, :], in_=pv[64:128])
            Bmr, Bmi = [Bm[r][0].rearrange("p (q c) -> p q c", c=32) for r in range(2)]
            Cmr, Cmi = [Cm[r][0].rearrange("p (q c) -> p q c", c=32) for r in range(2)]
            tAv = tmpA.rearrange("p (q c) -> p q c", c=32)
            tBv = tmpB.rearrange("p (q c) -> p q c", c=32)
            EBr = EB[0][0].rearrange("p (k q c) -> p k q c", k=8, c=32)
            EBi = EB[1][0].rearrange("p (k q c) -> p k q c", k=8, c=32)
            for k in range(8):
                e = "dve" if k % 2 == 0 else "pool"
                er, ei = bc32(Er[:, k, qs]), bc32(Ei[:, k, qs])
                tt(e, tAv, Bmr, er, ALU.mult, [Bm[0][1], B_E], [B_tmpA])
                tt(e, tBv, Bmi, ei, ALU.mult, [Bm[1][1], B_E], [B_tmpB])
                tt(e, EBr[:, k], tAv, tBv, ALU.subtract, [B_tmpA, B_tmpB], [EB[0][1]])
                tt(e, tAv, Bmi, er, ALU.mult, [Bm[1][1], B_E], [B_tmpA])
                tt(e, tBv, Bmr, ei, ALU.mult, [Bm[0][1], B_E], [B_tmpB])
                tt(e, EBi[:, k], tAv, tBv, ALU.add, [B_tmpA, B_tmpB], [EB[1][1]])
            for j in range(8):
                e = "dve"
                er, ei = bc32(Er[:, j + 1, qs]), bc32(Ei[:, j + 1, qs])
                tt(e, tAv, Cmr, er, ALU.mult, [Cm[0][1], B_E], [B_tmpA])
                tt(e, tBv, Cmi, ei, ALU.mult, [Cm[1][1], B_E], [B_tmpB])
                tt(e, CSv[:, :, j, 0, :], tAv, tBv, ALU.subtract, [B_tmpA, B_tmpB], [B_CSt])
                tt(e, tAv, Cmr, ei, ALU.mult, [Cm[0][1], B_E], [B_tmpA])
                tt(e, tBv, Cmi, er, ALU.mult, [Cm[1][1], B_E], [B_tmpB])
                op(e, "scalar_tensor_tensor", [B_tmpA, B_tmpB], [B_CSt], out=CSv[:, :, j, 1, :], in0=tAv, scalar=-1.0,
                   in1=tBv, op0=ALU.mult, op1=ALU.subtract)
            dma("sp", cst_s[l], CSt, [B_CSt], [Buf()], "cst_s")
            for b in range(4):
                for j0 in range(0, 8, 2):
                    ps, bps = next_bank()
                    for jj in range(2):
                        for ri in range(2):
                            src = EB[ri][0].rearrange("p (k c) -> p k c", k=8)[:, 7 - (j0 + jj), b * 128:(b + 1) * 128]
                            sl = (jj * 2 + ri) * 128
                            mm(ps[:, sl:sl + 128], src, ident, jj == 0 and ri == 0, False, [EB[ri][1], B_cst], [bps])
                    op("act", "activation", [bps], [B_BSt], out=BSv[:, b, j0:j0 + 2].rearrange("p j r c -> p (j r c)"),
                       in_=ps, func=AF.Copy)
            dma("sp", bst_s[l], BSt, [B_BSt], [Buf()], "bst_s")
            for ri in range(2):
                cev = Cexp[ri][0].rearrange("p (b q c) -> p b q c", b=4, q=4)
                cmv4 = Cm[ri][0].rearrange("p (b q c) -> p b q c", b=4, q=4)
                for q4 in range(4):
                    if ri == 0:
                        op("pool", "tensor_copy", [Cm[ri][1]], [Cexp[ri][1]], out=cev[:, :, q4, 32 * q4:32 * q4 + 32],
                           in_=cmv4[:, :, q4, :])
                    else:
                        op("pool", "tensor_scalar", [Cm[ri][1]], [Cexp[ri][1]], out=cev[:, :, q4, 32 * q4:32 * q4 + 32],
                           in0=cmv4[:, :, q4, :], scalar1=-1.0, scalar2=None, op0=ALU.mult)
            for k in range(8):
                for ri in range(2):
                    lev = Lexp[ri][0].rearrange("p (b q c) -> p b q c", b=4, q=4)
                    ebv = EB[ri][0].rearrange("p (k b q c) -> p k b q c", k=8, b=4, q=4)
                    for q4 in range(4):
                        op("pool" if q4 % 2 else "dve", "tensor_copy", [EB[ri][1]], [Lexp[ri][1]],
                           out=lev[:, :, q4, 32 * q4:32 * q4 + 32], in_=ebv[:, k, :, q4, :])
                ps, bps = next_bank()
                first = True
                for b in range(4):
                    for q4 in range(4):
                        for ri in range(2):
                            lq = Lexp[ri][0].rearrange("p (q c) -> p q c", c=128)[:, 4 * b + q4, :]
                            cq = Cexp[ri][0].rearrange("p (q c) -> p q c", c=128)[:, 4 * b + q4, :]
                            mm(ps[:, b * 128:(b + 1) * 128], lq, cq, first, False, [Lexp[ri][1], Cexp[ri][1]], [bps])
                            first = False
                op("act", "activation", [bps], [B_KFt], out=KFv[:, :, k, :], in_=ps.rearrange("p (b c) -> p b c", b=4),
                   func=AF.Copy)
            dma("sp", kfir_s[l], KFt, [B_KFt], [Buf()], "kfir_s")

        dump(RR, 0, [B_RR])
        dump(cr, 64, [B_cr])
        dump(ci, 128, [B_ci])
        dump(Er.rearrange("p k q -> p (k q)"), 192, [B_E])
        dump(Ei.rearrange("p k q -> p (k q)"), 768, [B_E])
        dump(tabc[:, 0:16, :].rearrange("p q c -> p (q c)"), 1344, [B_tab])
        dump(tabs[:, 0:16, :].rearrange("p q c -> p (q c)"), 2368, [B_tab])
        P.barrier(dummies)

        off[0] = persist_end
        h = alloc(KD * TT).rearrange("p (k t) -> p k t", k=KD)
        B_h = [[Buf() for _ in range(NSUB)] for _ in range(KD)]
        xn = alloc(KD * TT, BF16).rearrange("p (k t) -> p k t", k=KD)
        B_xn = [Buf() for _ in range(NSUB)]
        ring = [alloc(RINGW, BF16) for _ in range(NRING)]
        B_ring = [[Buf(), Buf()] for _ in range(NRING)]
        KF = alloc(4 * 8 * 128, BF16)
        BS = alloc(4 * 8 * 2 * 128, BF16)
        CS = alloc(16 * 8 * 2 * 32, BF16)
        ROT = alloc(2 * 16 * 64)
        B_sc = Buf()
        KFm = KF.rearrange("p (b k c) -> p b k c", b=4, k=8)
        BSm = BS.rearrange("p (b j r c) -> p b j r c", b=4, j=8, r=2)
        CSm = CS.rearrange("p (q j r c) -> p q j r c", q=16, j=8, r=2)
        ROTm = ROT.rearrange("p (r q c) -> p r q c", r=2, q=16)
        sq = [alloc(SUB, BF16) for _ in range(4)]
        B_sq = [Buf() for _ in range(4)]
        sq_i = [0]
        rstd = [alloc(SUB) for _ in range(2)]
        B_rstd = [Buf() for _ in range(2)]
        sgt = [alloc(SUB) for _ in range(3)]
        B_sgt = [Buf() for _ in range(3)]
        sgt_i = [0]
        pb = alloc(2 * TT, BF16).rearrange("p (k t) -> p k t", k=2)
        B_pb = Buf()
        u_start = off[0]
        act = alloc(NJ * TT, BF16).rearrange("p (j t) -> p j t", j=NJ)
        B_act = [[Buf() for _ in range(NSUB)] for _ in range(NJ)]
        u_end = off[0]
        off[0] = u_start
        zs_f = alloc(4 * SUB).rearrange("p (b t) -> p b t", b=4)
        B_zsf = [Buf() for _ in range(4)]
        zs_bf = alloc(4 * SUB, BF16).rearrange("p (b t) -> p b t", b=4)
        B_zsb = [Buf() for _ in range(4)]
        zp_f = alloc(4 * (SUB + 16)).rearrange("p (g t) -> p g t", g=4)
        B_zp = [Buf() for _ in range(4)]
        ptmp = [alloc(SUB + 16) for _ in range(3)]
        B_ptmp = [Buf() for _ in range(3)]
        d_bf = alloc(4 * SUB, BF16).rearrange("p (g t) -> p g t", g=4)
        B_dbf = [Buf() for _ in range(4)]
        Vt = [alloc(4 * 2 * NCS).rearrange("p (q r c) -> p q r c", q=4, r=2) for _ in range(2)]
        Wt = [alloc(4 * 2 * NCS).rearrange("p (q r c) -> p q r c", q=4, r=2) for _ in range(2)]
        SFt = [alloc(4 * 2 * (NCS + 1)).rearrange("p (q r c) -> p q r c", q=4, r=2) for _ in range(2)]
        rtmp = [alloc(4 * NCS).rearrange("p (q c) -> p q c", q=4) for _ in range(2)]
        sbf_all = alloc(16 * 2 * NCS, BF16).rearrange("p (q r c) -> p q r c", q=16, r=2)
        B_sbq = [Buf() for _ in range(4)]
        B_V = [Buf() for _ in range(2)]
        B_W = [Buf() for _ in range(2)]
        B_SF = [Buf() for _ in range(2)]
        B_rt = [Buf() for _ in range(2)]
        ys = [alloc(SUB) for _ in range(2)]
        B_ys = [Buf() for _ in range(2)]
        gt = [alloc(SUB) for _ in range(2)]
        B_gt = [Buf() for _ in range(2)]
        gy_bf = alloc(4 * SUB, BF16).rearrange("p (b t) -> p b t", b=4)
        B_gyb = [Buf() for _ in range(4)]
        assert off[0] <= NW
        off[0] = max(off[0], u_end)
        ostage = big[:, u_start:u_start + KD * SUB].rearrange("p (k t) -> p k t", k=KD)
        B_ost = Buf()

        def act_region_bufs():
            r = []
            for row in B_act:
                r.extend(row)
            return r

        mixer_tmp_bufs = (B_zsf + B_zsb + B_zp + B_ptmp + B_dbf + B_V + B_W + B_SF + B_rt + B_sbq + B_ys + B_gt
                          + B_gyb + [B_ost])

        plan = []
        for s in range(n_seq):
            for ti in range(NTILE):
                for l in range(depth):
                    for j in range(NJ):
                        plan.append(("wi", 0, l, j))
                    for m in range(KD):
                        plan.append(("wo", 0, l, m))
                    for sub in range(NSUB):
                        for m2 in range(4):
                            plan.append(("sq", "w_in", l, m2))
                        plan.append(("glu", l))
                        plan.append(("pool", l))
                        for m2 in range(4):
                            plan.append(("sq", "w_out", l, m2))
                    for j in range(NJ):
                        plan.append(("wi", 1, l, j))
                    for m in range(KD):
                        plan.append(("wo", 1, l, m))
                    for m2 in range(4):
                        plan.append(("projm", l, m2))
                        plan.append(("sq", "ple_w_gate", l, m2))
        issued = [0]
        consumed = [0]

        def issue_piece(idx):
            d = plan[idx]
            slot = idx % NRING
            t, B = ring[slot], B_ring[slot]
            key = "ring%d" % slot
            if d[0] == "wi":
                _, f, l, j = d
                w = dram["ffn1_wi" if f == 0 else "ffn2_wi"][l].rearrange("(k p) n -> p k n", p=128)
                tv = t[:, 0:2048].rearrange("p (t k c) -> p t k c", t=2, k=8)
                for tq in range(2):
                    c0 = tq * DFF + j * 128
                    dma("pool", tv[:, tq], w[:, :, c0:c0 + 128], [], [B[tq]], key)
            elif d[0] == "wo":
                _, f, l, m = d
                w = dram["ffn1_wo" if f == 0 else "ffn2_wo"][l].rearrange("(j p) n -> p j n", p=128)
                dma("pool", t[:, 0:NJ * 128].rearrange("p (j c) -> p j c", j=NJ), w[:, :, m * 128:(m + 1) * 128], [], B, key)
            elif d[0] == "sq":
                _, nm, l, m2 = d
                w = dram[nm][l].rearrange("(k p) n -> p k n", p=128)
                dma("pool", t[:, 0:2048].rearrange("p (k c) -> p k c", k=8), w[:, :, m2 * 256:(m2 + 1) * 256], [], B, key)
            elif d[0] == "projm":
                _, l, m2 = d
                w = dram["ple_w_proj"][l].rearrange("(k p) n -> p k n", p=128)
                dma("pool", t[:, 0:512].rearrange("p (k c) -> p k c", k=2), w[:, :, m2 * 256:(m2 + 1) * 256], [], B, key)
            elif d[0] == "glu":
                l = d[1]
                w = dram["ssm_w_glu"][l].rearrange("(k p) n -> p k n", p=128)
                dma("pool", t[:, 0:2048].rearrange("p (k c) -> p k c", k=4), w, [], B, key)
            elif d[0] == "pool":
                l = d[1]
                w = dram["pool_w"][l].rearrange("g p n -> p g n")
                dma("pool", t[:, 0:512].rearrange("p (g c) -> p g c", g=4), w, [], B, key)

        def get_piece(desc):
            idx = consumed[0]
            assert plan[idx] == desc, (plan[idx], desc)
            consumed[0] += 1
            while issued[0] < min(len(plan), idx + NRING - 1):
                issue_piece(issued[0])
                issued[0] += 1
            return ring[idx % NRING], B_ring[idx % NRING]

        def rmsnorm(gain_cols, out_fn):
            for sub in range(NSUB):
                ts = slice(sub * SUB, (sub + 1) * SUB)
                ps, bps = next_bank()
                for k in range(KD):
                    si = sq_i[0] % 4
                    sq_i[0] += 1
                    if k % 3 == 2:
                        tt("pool", sq[si], h[:, k, ts], h[:, k, ts], ALU.mult, [B_h[k][sub]], [B_sq[si]])
                    else:
                        op("act", "activation", [B_h[k][sub]], [B_sq[si]], out=sq[si], in_=h[:, k, ts], func=AF.Square)
                    mm(ps, ones_bf, sq[si], k == 0, k == KD - 1, [B_ones, B_sq[si]], [bps])
                ri = sub
                op("act", "activation", [bps, B_misc], [B_rstd[ri]], out=rstd[ri], in_=ps, func=AF.Ln, scale=1.0 / D,
                   bias=epsc)
                op("act", "activation", [B_rstd[ri]], [B_rstd[ri]], out=rstd[ri], in_=rstd[ri], func=AF.Exp, scale=-0.5)
                for k in range(KD):
                    out_fn(sub, k, h[:, k, ts], gain_cols[:, k:k + 1], rstd[ri], [B_h[k][sub], B_rstd[ri], B_gains])

        def norm_to_xn(gain_cols):
            def f(sub, k, hk, gcol, rs, reads):
                ts = slice(sub * SUB, (sub + 1) * SUB)
                op("dve", "scalar_tensor_tensor", reads, [B_xn[sub]], out=xn[:, k, ts], in0=hk, scalar=gcol, in1=rs,
                   op0=ALU.mult, op1=ALU.mult)
            rmsnorm(gain_cols, f)

        def ffn(f, l):
            norm_to_xn(gains_v[:, 0 if f == 0 else 2, l, :])
            def wi_group(j, sub, tv, B):
                ts = slice(sub * SUB, (sub + 1) * SUB)
                gps, bg = next_bank()
                for k in range(KD):
                    mm(gps, tv[:, 0, k, :], xn[:, k, ts], k == 0, k == KD - 1, B + [B_xn[sub]], [bg])
                ups, bu = next_bank()
                for k in range(KD):
                    mm(ups, tv[:, 1, k, :], xn[:, k, ts], k == 0, k == KD - 1, B + [B_xn[sub]], [bu])
                si = sgt_i[0] % 3
                sgt_i[0] += 1
                op("act", "activation", [bg], [B_sgt[si]], out=sgt[si], in_=gps, func=AF.Silu)
                op("dve", "tensor_tensor", [B_sgt[si], bu], [B_act[j][sub]], out=act[:, j, ts], in0=ups, in1=sgt[si],
                   op=ALU.mult)

            def wi_piece(j):
                t, B = get_piece(("wi", f, l, j))
                return t[:, 0:2048].rearrange("p (t k c) -> p t k c", t=2, k=8), B

            tv0, B0 = wi_piece(0)
            wi_group(0, 0, tv0, B0)
            tv1, B1 = wi_piece(1)
            wi_group(1, 0, tv1, B1)
            wi_group(0, 1, tv0, B0)
            wi_group(1, 1, tv1, B1)
            for j in range(2, NJ):
                tvj, Bj = wi_piece(j)
                for sub in range(NSUB):
                    wi_group(j, sub, tvj, Bj)
            for m in range(KD):
                t, B = get_piece(("wo", f, l, m))
                for sub in range(NSUB):
                    ts = slice(sub * SUB, (sub + 1) * SUB)
                    yps, by = next_bank()
                    for j in range(NJ):
                        mm(yps, t[:, j * 128:(j + 1) * 128], act[:, j, ts], j == 0, j == NJ - 1, B + [B_act[j][sub]], [by])
                    op("dve", "scalar_tensor_tensor", [by, B_h[m][sub]], [B_h[m][sub]], out=h[:, m, ts], in0=yps, scalar=0.5,
                       in1=h[:, m, ts], op0=ALU.mult, op1=ALU.add)

        def load_ssm_consts(l):
            dma("sp", KF, kfir_s[l], [], [B_sc], "sc")
            dma("sp", BS, bst_s[l], [], [B_sc], "sc")
            dma("sp", CS, cst_s[l], [], [B_sc], "sc")
            dma("sp", ROT, rot_s[l], [], [B_sc], "sc")

        def mixer(l, first_of_seq, sub):
            ts = slice(sub * SUB, (sub + 1) * SUB)
            seq_start = first_of_seq and sub == 0
            for m2 in range(4):
                t, B = get_piece(("sq", "w_in", l, m2))
                tv = t[:, 0:2048].rearrange("p (k c) -> p k c", k=8)
                for mmi in range(2):
                    m = 2 * m2 + mmi
                    zps, bz = next_bank()
                    for k in range(KD):
                        mm(zps, tv[:, k, mmi * 128:(mmi + 1) * 128], xn[:, k, ts], k == 0, k == KD - 1, B + [B_xn[sub]], [bz])
                    if m < 4:
                        op("act", "activation", [bz], [B_zsf[m]], out=zs_f[:, m, :].rearrange("p (j c) -> p c j", c=NCS),
                           in_=zps.rearrange("p (c j) -> p c j", j=TC), func=AF.Copy)
                        op("pool", "tensor_copy", [B_zsf[m]], [B_zsb[m]], out=zs_bf[:, m, :], in_=zs_f[:, m, :])
                    else:
                        g = m - 4
                        if seq_start:
                            op("pool", "memset", [], [B_zp[g]], zp_f[:, g, 0:16], 0.0)
                        else:
                            op("pool", "tensor_copy", [B_hist[l][g]], [B_zp[g]], out=zp_f[:, g, 0:16], in_=hist[:, l, g, :])
                        op("act", "activation", [bz], [B_zp[g]], out=zp_f[:, g, 16:16 + SUB], in_=zps, func=AF.Copy)
            for g, wdw in enumerate(POOL_WINS):
                zz = zp_f[:, g, :]
                cur, Bcur = zz, B_zp[g]
                lo = 0
                sh = 1
                ti_ = 0
                while sh < wdw:
                    nxt, Bn = ptmp[ti_ % 3], B_ptmp[ti_ % 3]
                    ti_ += 1
                    nlo = lo + sh
                    tt("pool", nxt[:, nlo:16 + SUB], cur[:, nlo:16 + SUB], cur[:, nlo - sh:16 + SUB - sh], ALU.add, [Bcur], [Bn])
                    cur, Bcur, lo = nxt, Bn, nlo
                    sh *= 2
                op("dve", "scalar_tensor_tensor", [Bcur, B_zp[g]], [B_dbf[g]], out=d_bf[:, g, :], in0=cur[:, 16:16 + SUB],
                   scalar=1.0 / wdw, in1=zz[:, 16:16 + SUB], op0=ALU.mult, op1=ALU.subtract)
                if seq_start:
                    nxt, Bn = ptmp[ti_ % 3], B_ptmp[ti_ % 3]
                    tt("pool", nxt[:, 0:16], cur[:, 16:32], invc[:, g, :], ALU.mult, [Bcur, B_cst], [Bn])
                    tt("pool", d_bf[:, g, 0:16], nxt[:, 0:16], zz[:, 16:32], ALU.subtract, [Bn, B_zp[g]], [B_dbf[g]])
                op("pool", "tensor_copy", [B_zp[g], Bcur, B_dbf[g]], [B_hist[l][g]], out=hist[:, l, g, :], in_=zp_f[:, g, SUB:SUB + 16])
            zj = zs_bf.rearrange("p b (j c) -> p b j c", c=NCS)
            do_ssm = "nossm" not in flags
            stop_at = 99
            for f_ in flags:
                if f_.startswith("stop"):
                    stop_at = int(f_[4:])
            dd = dbg is not None and l == 0 and sub == 0 and seq_start
            if dd:
                dump(KF, 3392, [B_sc], q="pool")
                dump(BS, 7488, [B_sc], q="pool")
                dump(CS, 15680, [B_sc], q="pool")
                dump(zs_f.rearrange("p b t -> p (b t)"), 23872, B_zsf)
                dump(ROT, 25920, [B_sc])
            ROTq = ROTm.rearrange("p r (b q) c -> p r b q c", q=4)
            carry_q = carry_v.rearrange("p l (b q) r -> p l b q r", q=4)
            sbq = sbf_all.rearrange("p (b q) r c -> p b q r c", q=4)
            if do_ssm:
                Sb = [next_bank() for _ in range(4)]
                firsts = [True] * 4
                for b in range(4):
                    for ri in range(2):
                        for j in range(TC):
                            for q4 in range(4):
                                sps, bs = Sb[q4]
                                Sv = sps.rearrange("p (b r c) -> p b r c", b=4, r=2)
                                mm(Sv[:, b, ri, :], BSm[32 * q4:32 * q4 + 32, b, j, ri, :], zj[32 * q4:32 * q4 + 32, b, j, :],
                                   firsts[q4], False, [B_sc, B_zsb[b]], [bs], tp=(32 * q4, 0))
                                firsts[q4] = False
            Yb = []
            for b in range(4 if do_ssm else 0):
                yps, by = next_bank()
                Yb.append((yps, by))
                mm(yps, KFm[:, b, 0, :], zs_bf[:, b, :], True, False, [B_sc, B_zsb[b]], [by])
                for k in range(1, TC):
                    mm(yps[:, k * NCS:SUB], KFm[:, b, k, :], zs_bf[:, b, 0:(TC - k) * NCS], False, False, [B_sc, B_zsb[b]], [by])
            for q4 in range(4 if (do_ssm and stop_at > 1) else 0):
                par = q4 % 2
                V, W, SF, RT = Vt[par], Wt[par], SFt[par], rtmp[par]
                sps, bs = Sb[q4]
                Sv = sps.rearrange("p (b r c) -> p b r c", b=4, r=2)
                rc = ROTq[:, 0, :, q4, :]
                rs_ = ROTq[:, 1, :, q4, :]
                Bc = B_carry[l][q4]
                tt("dve", RT, Sv[:, :, 1, :], rs_, ALU.mult, [bs, B_sc], [B_rt[par]])
                tt("dve", V[:, :, 0, :], Sv[:, :, 0, :], rc, ALU.mult, [bs, B_sc], [B_V[par]])
                tt("dve", V[:, :, 0, :], V[:, :, 0, :], RT, ALU.add, [B_V[par], B_rt[par]], [B_V[par]])
                tt("dve", RT, Sv[:, :, 0, :], rs_, ALU.mult, [bs, B_sc], [B_rt[par]])
                tt("dve", V[:, :, 1, :], Sv[:, :, 1, :], rc, ALU.mult, [bs, B_sc], [B_V[par]])
                tt("dve", V[:, :, 1, :], V[:, :, 1, :], RT, ALU.subtract, [B_V[par], B_rt[par]], [B_V[par]])
                if stop_at <= 2:
                    continue
                if seq_start:
                    op("dve", "memset", [], [Bc], carry_q[:, l, :, q4, :], 0.0)
                for b in range(4):
                    for ri in range(2):
                        q = 4 * b + q4
                        op("dve", "tensor_tensor_scan", [B_V[par], B_RR, Bc], [B_W[par]], out=W[:, b, ri, :],
                           data0=RR[:, l * 16 + q:l * 16 + q + 1].to_broadcast([128, NCS]), data1=V[:, b, ri, :],
                           initial=carry_v[:, l, q, ri:ri + 1], op0=ALU.mult, op1=ALU.add)
                if stop_at <= 3:
                    continue
                beng = "dve" if q4 == 3 else "pool"
                tt(beng, RT, W[:, :, 1, :], rs_, ALU.mult, [B_W[par], B_sc], [B_rt[par]])
                tt(beng, SF[:, :, 0, 1:NCS + 1], W[:, :, 0, :], rc, ALU.mult, [B_W[par], B_sc], [B_SF[par]])
                tt(beng, SF[:, :, 0, 1:NCS + 1], SF[:, :, 0, 1:NCS + 1], RT, ALU.subtract, [B_SF[par], B_rt[par]], [B_SF[par]])
                tt(beng, RT, W[:, :, 0, :], rs_, ALU.mult, [B_W[par], B_sc], [B_rt[par]])
                tt(beng, SF[:, :, 1, 1:NCS + 1], W[:, :, 1, :], rc, ALU.mult, [B_W[par], B_sc], [B_SF[par]])
                tt(beng, SF[:, :, 1, 1:NCS + 1], SF[:, :, 1, 1:NCS + 1], RT, ALU.add, [B_SF[par], B_rt[par]], [B_SF[par]])
                op(beng, "tensor_copy", [Bc], [B_SF[par]], out=SF[:, :, :, 0], in_=carry_q[:, l, :, q4, :])
                op(beng, "tensor_copy", [B_SF[par]], [Bc], out=carry_q[:, l, :, q4, :], in_=SF[:, :, :, NCS])
                op("act", "activation", [B_SF[par]], [B_sbq[q4]], out=sbq[:, :, q4, :, :], in_=SF[:, :, :, 0:NCS], func=AF.Copy)
                if dd and q4 == 0:
                    dump(V.rearrange("p b r c -> p (b r c)"), 27968, [B_V[par]])
                    dump(W.rearrange("p b r c -> p (b r c)"), 28480, [B_W[par]])
                    dump(SF.rearrange("p b r c -> p (b r c)"), 28992, [B_SF[par]])
            for b in range(4 if (do_ssm and stop_at > 4) else 0):
                par = b % 2
                yps, by = Yb[b]
                Yj = yps.rearrange("p (j c) -> p j c", c=NCS)
                for j in range(TC if stop_at > 5 else 0):
                    for q4 in range(4):
                        for ri in range(2):
                            mm(Yj[32 * q4:32 * q4 + 32, j, :], CSm[:, 4 * b + q4, j, ri, :], sbf_all[:, 4 * b + q4, ri, :], False,
                               (j == TC - 1 and q4 == 3 and ri == 1), [B_sc] + B_sbq, [by], tp=(0, 32 * q4))
                if stop_at <= 6:
                    continue
                Y, G = ys[par], gt[par]
                op("dve", "tensor_scalar", [B_zsf[b], B_gains], [B_gt[par]], out=G, in0=zs_f[:, b, :],
                   scalar1=dsk[:, l * 4 + b:l * 4 + b + 1], scalar2=None, op0=ALU.mult)
                if stop_at <= 7:
                    continue
                tt("dve", Y, yps, G, ALU.add, [by, B_gt[par]], [B_ys[par]])
                if dd and b == 0:
                    dump(Y, 29512, [B_ys[par]])
                if stop_at <= 8:
                    continue
                op("act", "activation", [B_ys[par]], [B_gt[par]], out=G, in_=Y, func=AF.Square, scale=math.sqrt(0.044715))
                if stop_at <= 9:
                    continue
                op("dve", "scalar_tensor_tensor", [B_gt[par], B_ys[par]], [B_gt[par]], out=G, in0=G, scalar=1.0, in1=Y,
                   op0=ALU.add, op1=ALU.mult)
                if stop_at <= 10:
                    continue
                op("act", "activation", [B_gt[par]], [B_gt[par]], out=G, in_=G, func=AF.Sigmoid, scale=1.5957691216057308)
                if stop_at <= 11:
                    continue
                op("dve", "tensor_tensor", [B_gt[par], B_ys[par]], [B_zsf[b]], out=zs_f[:, b, :], in0=G, in1=Y, op=ALU.mult)
                if stop_at <= 12:
                    continue
                op("act", "activation", [B_zsf[b]], [B_gyb[b]], out=gy_bf[:, b, :], in_=zs_f[:, b, :], func=AF.Copy)
            tg, Bg = get_piece(("glu", l))
            tgv = tg[:, 0:2048].rearrange("p (k c) -> p k c", k=4)
            for n_ in range(4 if "nossm" not in flags else 0):
                gps, bg = next_bank()
                for b in range(4):
                    mm(gps, tgv[:, b, n_ * 128:(n_ + 1) * 128], gy_bf[:, b, :], b == 0, b == 3, Bg + [B_gyb[b]], [bg])
                si = sgt_i[0] % 3
                sgt_i[0] += 1
                op("act", "activation", [bg], [B_sgt[si]], out=sgt[si], in_=gps, func=AF.Sigmoid)
                op("dve", "tensor_tensor", [B_sgt[si], B_zsf[n_]], [B_xn[sub]], out=xn[:, n_, ts].rearrange("p (c j) -> p c j", j=TC),
                   in0=sgt[si].rearrange("p (j c) -> p c j", c=NCS), in1=zs_f[:, n_, :].rearrange("p (j c) -> p c j", c=NCS),
                   op=ALU.mult)
            if "nossm" in flags:
                for n_ in range(4):
                    op("dve", "tensor_copy", [B_zsb[n_]], [B_xn[sub]], out=xn[:, n_, ts], in_=zs_bf[:, n_, :])
            tp_, Bp = get_piece(("pool", l))
            tpv = tp_[:, 0:512].rearrange("p (g c) -> p g c", g=4)
            for g, wdw in enumerate(POOL_WINS):
                pps, bp = next_bank()
                mm(pps, tpv[:, g, :], d_bf[:, g, :], True, True, Bp + [B_dbf[g]], [bp])
                op("dve", "tensor_scalar", [bp, B_gains], [B_xn[sub]], out=xn[:, 4 + g, ts], in0=pps,
                   scalar1=psc[:, l * 4 + g:l * 4 + g + 1], scalar2=None, op0=ALU.mult)
            if dbg is not None and l == 0 and sub == 0 and seq_start:
                for k_ in range(8):
                    dump(xn[:, k_, ts], 30024 + 512 * k_, [B_xn[sub]], q="pool")
            for m2 in range(4):
                t, B = get_piece(("sq", "w_out", l, m2))
                tv = t[:, 0:2048].rearrange("p (k c) -> p k c", k=8)
                for mmi in range(2):
                    m = 2 * m2 + mmi
                    ops_, bo = next_bank()
                    for k in range(KD):
                        mm(ops_, tv[:, k, mmi * 128:(mmi + 1) * 128], xn[:, k, ts], k == 0, k == KD - 1, B + [B_xn[sub]], [bo])
                    tt("dve", h[:, m, ts], ops_, h[:, m, ts], ALU.add, [bo, B_h[m][sub]], [B_h[m][sub]])

        def ple(l, tok0):
            norm_to_xn(gains_v[:, 3, l, :])
            dma("pool", pb, dram["pT"][l].rearrange("(k p) n -> p k n", p=128)[:, :, tok0:tok0 + TT], [], [B_pb], "pb")
            for m2 in range(4):
                tpj, Bpj = get_piece(("projm", l, m2))
                tpjv = tpj[:, 0:512].rearrange("p (k c) -> p k c", k=2)
                t, B = get_piece(("sq", "ple_w_gate", l, m2))
                tv = t[:, 0:2048].rearrange("p (k c) -> p k c", k=8)
                for sub in range(NSUB):
                    for mmi in range(2):
                        m = 2 * m2 + mmi
                        ts = slice(sub * SUB, (sub + 1) * SUB)
                        gps, bg = next_bank()
                        for k in range(KD):
                            mm(gps, tv[:, k, mmi * 128:(mmi + 1) * 128], xn[:, k, ts], k == 0, k == KD - 1, B + [B_xn[sub]], [bg])
                        pps, bp = next_bank()
                        for k in range(2):
                            mm(pps, tpjv[:, k, mmi * 128:(mmi + 1) * 128], pb[:, k, ts], k == 0, k == 1, Bpj + [B_pb], [bp])
                        si = sgt_i[0] % 3
                        sgt_i[0] += 1
                        if "ple_mm" in flags:
                            op("dve", "tensor_copy", [bg], [B_sgt[si]], out=sgt[si], in_=gps)
                            op("dve", "tensor_copy", [bp], [B_sgt[si]], out=sgt[si], in_=pps)
                            continue
                        op("act", "activation", [bg], [B_sgt[si]], out=sgt[si], in_=gps, func=AF.Sigmoid)
                        tt("dve", sgt[si], pps, sgt[si], ALU.mult, [B_sgt[si], bp], [B_sgt[si]])
                        tt("dve", h[:, m, ts], sgt[si], h[:, m, ts], ALU.add, [B_sgt[si], B_h[m][sub]], [B_h[m][sub]])

        out_dmas = []
        if "prolog" in flags:
            plan = []
            op("dve", "memset", [], [B_ost], ostage, 0.0)
            out_dmas.append(dma("sp", outT.rearrange("(k p) n -> p k n", p=128)[:, :, 0:SUB], ostage, [B_ost], [B_ost], "ost"))
        for s in range(n_seq if "prolog" not in flags else 0):
            for ti in range(NTILE):
                tok0 = s * seq_len + ti * TT
                for k in range(KD):
                    dma("sp", h[:, k, :], dram["xT"][k * 128:(k + 1) * 128, tok0:tok0 + TT], [], B_h[k], "hx%d" % k)
                for l in range(depth):
                    load_ssm_consts(l)
                    if "noffn" not in flags:
                        ffn(0, l)
                    else:
                        for _ in range(NJ + KD):
                            get_piece(plan[consumed[0]])
                    norm_to_xn(gains_v[:, 1, l, :])
                    guard = act_region_bufs()
                    op("pool", "memset", [], guard + mixer_tmp_bufs, dummies["pool"], 0.0)
                    for sub in range(NSUB):
                        if "nomix" in flags:
                            for _ in range(10):
                                get_piece(plan[consumed[0]])
                        else:
                            mixer(l, ti == 0, sub)
                    op("pool", "memset", [], guard + mixer_tmp_bufs, dummies["pool"], 0.0)
                    if "noffn" not in flags:
                        ffn(1, l)
                    else:
                        for _ in range(NJ + KD):
                            get_piece(plan[consumed[0]])
                    if "nople" in flags:
                        for _ in range(8):
                            get_piece(plan[consumed[0]])
                    else:
                        ple(l, tok0)
                op("pool", "memset", [], act_region_bufs() + mixer_tmp_bufs, dummies["pool"], 0.0)

                def fo(sub, k, hk, gcol, rs, reads):
                    op("dve", "scalar_tensor_tensor", reads + [B_ost], [B_ost], out=ostage[:, k, :], in0=hk, scalar=gcol, in1=rs,
                       op0=ALU.mult, op1=ALU.mult)
                    if k == KD - 1:
                        t0 = tok0 + sub * SUB
                        d_ = dma("sp", outT.rearrange("(k p) n -> p k n", p=128)[:, :, t0:t0 + SUB], ostage, [B_ost], [B_ost], "ost")
                        out_dmas.append(d_)
                rmsnorm(gfin, fo)
        assert consumed[0] == len(plan), (consumed[0], len(plan))
        P.emit(final_waits=out_dmas + dbg_dmas)
    return nc


_CACHE = {}


def _get_nc(n_seq, seq_len, depth, flags=frozenset()):
    key = (n_seq, seq_len, depth, flags)
    if key not in _CACHE:
        _CACHE[key] = build(n_seq, seq_len, depth, flags)
    return _CACHE[key]


def kernel(**inputs):
    x = np.asarray(inputs["x"], dtype=np.float32)
    p = np.asarray(inputs["p"], dtype=np.float32)
    B, L, _ = x.shape
    ncores = 8
    n_seq = B // ncores
    nc = _get_nc(n_seq, L, DEPTH)
    consts = host_consts()
    wmap = {name: np.ascontiguousarray(np.asarray(inputs[name], dtype=np.float32)) for name, _ in WSHAPES}
    in_maps = []
    for c in range(ncores):
        xs = x[c * n_seq:(c + 1) * n_seq].reshape(n_seq * L, D)
        ps_ = p[:, c * n_seq:(c + 1) * n_seq].reshape(DEPTH, n_seq * L, 256)
        m = {"xT": np.ascontiguousarray(xs.T), "pT": np.ascontiguousarray(ps_.transpose(0, 2, 1)), "consts": consts}
        m.update(wmap)
        in_maps.append(m)
    res = run_bass_kernel_spmd(nc, in_maps, core_ids=list(range(ncores)))
    out = np.empty((B, L, D), np.float32)
    for c in range(ncores):
        o = np.asarray(res.results[c]["outT"])
        out[c * n_seq:(c + 1) * n_seq] = o.T.reshape(n_seq, L, D)
    return out
```

```python
import contextlib
import math
import numpy as np
import concourse.bass as bass
import concourse.mybir as mybir
from concourse.bass_utils import run_bass_kernel_spmd

F32 = mybir.dt.float32
BF16 = mybir.dt.bfloat16
AF = mybir.ActivationFunctionType
ALU = mybir.AluOpType

ENGS = ("pe", "act", "dve", "pool", "sp")


class Buf:
    __slots__ = ("w", "rs")

    def __init__(self):
        self.w = None
        self.rs = []


class Op:
    __slots__ = ("eng", "fn", "deps", "sig", "val", "dma", "semkey")

    def __init__(self, eng, fn, dma):
        self.eng = eng
        self.fn = fn
        self.deps = []
        self.sig = False
        self.val = 0
        self.dma = dma
        self.semkey = None


class Prog:
    def __init__(self, nc):
        self.nc = nc
        self.ops = {e: [] for e in ENGS}
        self.last_dma = {}

    def add(self, eng, fn, reads=(), writes=(), dma=False, semkey=None, extra=()):
        op = Op(eng, fn, dma)
        op.semkey = semkey
        deps = op.deps
        for b in reads:
            if b.w is not None:
                deps.append(b.w)
        for b in writes:
            if b.w is not None:
                deps.append(b.w)
            deps.extend(b.rs)
        deps.extend(extra)
        for b in reads:
            if not dma and eng != "pool":
                b.rs = [r for r in b.rs if r.eng != eng or r.dma]
            b.rs.append(op)
        for b in writes:
            b.w = op
            b.rs = []
        self.ops[eng].append(op)
        if dma:
            self.last_dma[semkey] = op
        return op

    def barrier(self, dummies):
        lasts = [self.ops[e][-1] for e in ENGS if self.ops[e]] + list(self.last_dma.values())
        for e in ("act", "dve", "pool"):
            d = dummies[e]
            if e == "act":
                self.add(e, (lambda d: lambda h: h.activation(out=d, in_=d, func=AF.Copy))(d), extra=lasts)
            else:
                self.add(e, (lambda d: lambda h: h.memset(d, 0.0))(d), extra=lasts)
        self.add("sp", lambda h: h.dma_start(out=dummies["spo"], in_=dummies["spi"]), extra=lasts, dma=True, semkey="barrier")

    def emit(self, final_waits=()):
        nc = self.nc
        for e in ENGS:
            for op in self.ops[e]:
                seen = set()
                nd = []
                for d in op.deps:
                    if d is op or id(d) in seen:
                        continue
                    seen.add(id(d))
                    if d.eng == "pe" and op.eng == "pe" and not d.dma and not op.dma:
                        continue
                    nd.append(d)
                    d.sig = True
                op.deps = nd
        for op in final_waits:
            op.sig = True
        cnt = {e: 0 for e in ENGS}
        dma_cnt = {}
        for e in ENGS:
            for op in self.ops[e]:
                if op.dma:
                    k = op.semkey
                    dma_cnt[k] = dma_cnt.get(k, 0) + 16
                    op.val = dma_cnt[k]
                elif op.sig:
                    cnt[e] += 1
                    op.val = cnt[e]
        with contextlib.ExitStack() as st:
            sems = {e: st.enter_context(nc.semaphore("s_" + e)) for e in ENGS}
            dsems = {k: st.enter_context(nc.semaphore("d_%s" % str(k))) for k in dma_cnt}
            block = st.enter_context(nc.Block())

            def sem_of(d):
                return dsems[d.semkey] if d.dma else sems[d.eng]

            def run(e, handle):
                waited = {}
                for op in self.ops[e]:
                    for d in op.deps:
                        s = sem_of(d)
                        key = id(s)
                        if waited.get(key, 0) >= d.val:
                            continue
                        handle.wait_ge(s, d.val)
                        waited[key] = d.val
                    ins = op.fn(handle)
                    if op.dma:
                        ins.then_inc(dsems[op.semkey], 16)
                    elif op.sig:
                        ins.then_inc(sems[e], 1)
                if e == "sp":
                    for d in final_waits:
                        handle.wait_ge(sem_of(d), d.val)

            @block.tensor
            def _(h):
                run("pe", h)

            @block.scalar
            def _(h):
                run("act", h)

            @block.vector
            def _(h):
                run("dve", h)

            @block.gpsimd
            def _(h):
                run("pool", h)

            @block.sync
            def _(h):
                run("sp", h)


D = 1024
KD = 8
DFF = 2816
NJ = 22
TT = 1024
SUB = 512
NSUB = 2
TC = 8
NCS = SUB // TC
DEPTH = 4
EPS = 1e-6
NRING = 5
RINGW = 2816
POOL_WINS = (2, 4, 8, 16)

WSHAPES = [
    ("ffn1_norm", [DEPTH, D]), ("ffn1_wi", [DEPTH, D, 2 * DFF]), ("ffn1_wo", [DEPTH, DFF, D]),
    ("mix_norm", [DEPTH, D]), ("w_in", [DEPTH, D, D]),
    ("ssm_lambda_re", [DEPTH, 32, 64]), ("ssm_lambda_im", [DEPTH, 32, 64]), ("ssm_log_dt", [DEPTH, 32]),
    ("ssm_b_re", [DEPTH, 32, 64, 16]), ("ssm_b_im", [DEPTH, 32, 64, 16]),
    ("ssm_c_re", [DEPTH, 32, 16, 64]), ("ssm_c_im", [DEPTH, 32, 16, 64]),
    ("ssm_d", [DEPTH, 512]), ("ssm_w_glu", [DEPTH, 512, 512]),
    ("pool_w", [DEPTH, 4, 128, 128]), ("pool_scale", [DEPTH, 512]), ("w_out", [DEPTH, D, D]),
    ("ffn2_norm", [DEPTH, D]), ("ffn2_wi", [DEPTH, D, 2 * DFF]), ("ffn2_wo", [DEPTH, DFF, D]),
    ("ple_norm", [DEPTH, D]), ("ple_w_gate", [DEPTH, D, D]), ("ple_w_proj", [DEPTH, 256, D]),
    ("final_norm", [D]),
]
NCONST = 128 + 64 + 64 + 64


def host_consts():
    c = np.zeros((128, NCONST), np.float32)
    c[:, 0:128] = np.eye(128, dtype=np.float32)
    for g2 in range(2):
        for q4 in range(4):
            for h in range(16):
                c[(2 * q4 + g2) * 16 + h, 128 + 64 * g2 + q4 * 16 + h] = 1.0
    for gi, w in enumerate(POOL_WINS):
        for t in range(16):
            c[:, 256 + gi * 16 + t] = 1.0 / min(t + 1, w)
    return c


def build(n_seq, seq_len, depth, flags=frozenset()):
    NTOK = n_seq * seq_len
    NTILE = seq_len // TT
    nc = bass.Bass("TRN2", target_bir_lowering=False)
    dram = {}
    dram["xT"] = nc.dram_tensor("xT", [D, NTOK], F32, kind="ExternalInput").ap()
    dram["pT"] = nc.dram_tensor("pT", [DEPTH, 256, NTOK], F32, kind="ExternalInput").ap()
    dram["consts"] = nc.dram_tensor("consts", [128, NCONST], F32, kind="ExternalInput").ap()
    for name, shp in WSHAPES:
        dram[name] = nc.dram_tensor(name, shp, F32, kind="ExternalInput").ap()
    outT = nc.dram_tensor("outT", [D, NTOK], F32, kind="ExternalOutput").ap()
    kfir_s = nc.dram_tensor("kfir_s", [DEPTH, 128, 4 * 8 * 128], BF16, kind="Internal").ap()
    bst_s = nc.dram_tensor("bst_s", [DEPTH, 128, 4 * 8 * 2 * 128], BF16, kind="Internal").ap()
    cst_s = nc.dram_tensor("cst_s", [DEPTH, 128, 16 * 8 * 2 * 32], BF16, kind="Internal").ap()
    rot_s = nc.dram_tensor("rot_s", [DEPTH, 128, 2 * 16 * 64], F32, kind="Internal").ap()

    dbg = None
    if "dbg" in flags:
        dbg = nc.dram_tensor("dbg", [128, 40960], F32, kind="ExternalOutput").ap()
    dbg_dmas = []
    dbg_i = [0]
    P = Prog(nc)
    NW = 53100

    with contextlib.ExitStack() as st:
        big = st.enter_context(nc.sbuf_tensor("big", [128, NW], F32))[:]
        banks = [st.enter_context(nc.psum_tensor("bank%d" % i, [128, 512], F32))[:] for i in range(8)]
        bank_bufs = [Buf() for _ in range(8)]
        bank_i = [0]

        def next_bank():
            i = bank_i[0] % 8
            bank_i[0] += 1
            return banks[i], bank_bufs[i]

        off = [0]

        def alloc(nelem, dt=F32):
            words = nelem if dt == F32 else (nelem + 1) // 2
            a = off[0]
            off[0] += words
            assert off[0] <= NW, ("SBUF arena overflow", off[0])
            v = big[:, a:a + words]
            return v if dt == F32 else v.bitcast(dt)

        def op(eng, method, reads, writes, *args, **kw):
            return P.add(eng, lambda h: getattr(h, method)(*args, **kw), reads, writes)

        def dma(q, out, in_, reads, writes, semkey):
            return P.add(q, lambda h: h.dma_start(out=out, in_=in_), reads, writes, dma=True, semkey=semkey)

        def dump(ap2d, col, reads, q="sp"):
            if dbg is None:
                return
            n_ = ap2d.shape[1]
            dbg_i[0] += 1
            dbg_dmas.append(dma(q, dbg[:, col:col + n_], ap2d, reads, [Buf()], "dbg%d" % dbg_i[0]))

        def mm(out, lhsT, rhs, start, stop, reads, writes, tp=None):
            kw = dict(start=start, stop=stop, skip_group_check=True)
            if tp is not None:
                kw["tile_position"] = tp
            return P.add("pe", lambda h: h.matmul(out, lhsT=lhsT, rhs=rhs, **kw), reads, writes)

        cst = alloc(NCONST)
        B_cst = Buf()
        ident = cst[:, 0:128]
        sel = [cst[:, 128:192], cst[:, 192:256]]
        invc = cst[:, 256:320].rearrange("p (g t) -> p g t", t=16)
        dma("sp", cst, dram["consts"], [], [B_cst], "cst")
        misc = alloc(16)
        B_misc = Buf()
        op("dve", "memset", [], [B_misc], misc[:, 0:1], math.pi / 2)
        op("dve", "memset", [], [B_misc], misc[:, 1:2], EPS)
        op("dve", "memset", [], [B_misc], misc[:, 2:3], 0.0)
        halfpi = misc[:, 0:1]
        epsc = misc[:, 1:2]
        ones_bf = alloc(128, BF16)
        B_ones = Buf()
        op("dve", "memset", [], [B_ones], ones_bf, 1.0)
        gains = alloc(4 * DEPTH * 8)
        B_gains = Buf()
        gfin = alloc(8)
        dsk = alloc(DEPTH * 4)
        psc = alloc(DEPTH * 4)
        RR = alloc(DEPTH * 16)
        B_RR = Buf()
        carry = alloc(DEPTH * 16 * 2)
        carry_v = carry.rearrange("p (l q r) -> p l q r", l=DEPTH, r=2)
        B_carry = [[Buf() for _ in range(4)] for _ in range(DEPTH)]
        hist = alloc(DEPTH * 4 * 16).rearrange("p (l g t) -> p l g t", l=DEPTH, g=4)
        B_hist = [[Buf() for _ in range(4)] for _ in range(DEPTH)]
        dummies = {e: alloc(2) for e in ("act", "dve", "pool", "spo", "spi")}
        op("dve", "memset", [], [], dummies["spi"], 0.0)
        op("dve", "memset", [], [], dummies["act"], 0.0)
        persist_end = off[0]

        stage = alloc(128)
        B_stage = Buf()

        def load_T(rows_ap, R, dst, B_dst, evac="dve"):
            dma("sp", stage[0:R, :], rows_ap, [], [B_stage], "stage")
            ps, bps = next_bank()
            mm(ps[:, 0:R], stage[0:R, :], ident[0:R, 0:R], True, True, [B_stage, B_cst], [bps])
            op(evac, "tensor_copy", [bps], [B_dst], out=dst, in_=ps[:, 0:R])

        for ki, nm in enumerate(("ffn1_norm", "mix_norm", "ffn2_norm", "ple_norm")):
            load_T(dram[nm].rearrange("l (k p) -> (l k) p", p=128), DEPTH * 8,
                   gains[:, ki * DEPTH * 8:(ki + 1) * DEPTH * 8], B_gains)
        load_T(dram["final_norm"].rearrange("(k p) -> k p", p=128), 8, gfin, B_gains)
        load_T(dram["ssm_d"].rearrange("l (b p) -> (l b) p", p=128), DEPTH * 4, dsk, B_gains)
        load_T(dram["pool_scale"].rearrange("l (b p) -> (l b) p", p=128), DEPTH * 4, psc, B_gains)
        gains_v = gains.rearrange("p (n l k) -> p n l k", n=4, l=DEPTH)

        LQ = DEPTH * 16

        def t64():
            return alloc(LQ), Buf()

        lr, B_lr = t64()
        li, B_li = t64()
        ldt, B_ldt = t64()
        load_T(dram["ssm_lambda_re"].rearrange("l (q g) p -> (l q) (g p)", g=2), LQ, lr, B_lr)
        load_T(dram["ssm_lambda_im"].rearrange("l (q g) p -> (l q) (g p)", g=2), LQ, li, B_li)
        ld2 = alloc(2)
        B_ld2 = Buf()
        dma("sp", ld2[0:LQ, :], dram["ssm_log_dt"].rearrange("l (q g) -> (l q) g", g=2), [], [B_ld2], "ld2")
        stage2 = alloc(128)
        B_stage2 = Buf()
        for g2 in range(2):
            op("dve", "tensor_copy", [B_ld2], [B_stage2], out=stage2[0:LQ, g2 * 64:(g2 + 1) * 64],
               in_=ld2[0:LQ, g2:g2 + 1].to_broadcast([LQ, 64]))
        ps, bps = next_bank()
        mm(ps[:, 0:LQ], stage2[0:LQ, :], ident[0:LQ, 0:LQ], True, True, [B_stage2, B_cst], [bps])
        op("dve", "tensor_copy", [bps], [B_ldt], out=ldt, in_=ps[:, 0:LQ])

        def tt(eng, out, a, b, o, reads, writes):
            return op(eng, "tensor_tensor", reads, writes, out=out, in0=a, in1=b, op=o)

        dt_, B_dt = t64()
        op("act", "activation", [B_ldt], [B_dt], out=dt_, in_=ldt, func=AF.Exp)
        mr, B_mr = t64()
        mi, B_mi = t64()
        tt("dve", mr, lr, dt_, ALU.mult, [B_lr, B_dt], [B_mr])
        tt("dve", mi, li, dt_, ALU.mult, [B_li, B_dt], [B_mi])
        em, B_em = t64()
        op("act", "activation", [B_mr], [B_em], out=em, in_=mr, func=AF.Exp)
        op("act", "activation", [B_mr], [B_RR], out=RR, in_=mr, func=AF.Exp, scale=float(TC))
        cu, B_cu = t64()
        su, B_su = t64()
        op("act", "activation", [B_mi], [B_su], out=su, in_=mi, func=AF.Sin, scale=1.0 / 64)
        op("act", "activation", [B_mi, B_misc], [B_cu], out=cu, in_=mi, func=AF.Sin, scale=1.0 / 64, bias=halfpi)
        ta, B_ta = t64()
        tb, B_tb = t64()

        def csquare(c, Bc, s, Bs):
            tt("dve", ta, c, c, ALU.mult, [Bc], [B_ta])
            tt("dve", tb, s, s, ALU.mult, [Bs], [B_tb])
            op("dve", "scalar_tensor_tensor", [Bc, Bs], [Bs], out=s, in0=c, scalar=2.0, in1=s,
               op0=ALU.mult, op1=ALU.mult)
            tt("dve", c, ta, tb, ALU.subtract, [B_ta, B_tb], [Bc])

        for _ in range(6):
            csquare(cu, B_cu, su, B_su)
        lbr, B_lbr = t64()
        lbi, B_lbi = t64()
        tt("dve", lbr, em, cu, ALU.mult, [B_em, B_cu], [B_lbr])
        tt("dve", lbi, em, su, ALU.mult, [B_em, B_su], [B_lbi])
        a1, B_a1 = t64()
        op("dve", "tensor_scalar", [B_lbr], [B_a1], out=a1, in0=lbr, scalar1=-1.0, scalar2=None, op0=ALU.add)
        inv, B_inv = t64()
        tt("dve", ta, lr, lr, ALU.mult, [B_lr], [B_ta])
        tt("dve", tb, li, li, ALU.mult, [B_li], [B_tb])
        tt("dve", ta, ta, tb, ALU.add, [B_ta, B_tb], [B_ta])
        op("dve", "reciprocal", [B_ta], [B_inv], out=inv, in_=ta)
        cr, B_cr = t64()
        ci, B_ci = t64()
        tt("dve", ta, a1, lr, ALU.mult, [B_a1, B_lr], [B_ta])
        tt("dve", tb, lbi, li, ALU.mult, [B_lbi, B_li], [B_tb])
        tt("dve", ta, ta, tb, ALU.add, [B_ta, B_tb], [B_ta])
        tt("dve", cr, ta, inv, ALU.mult, [B_ta, B_inv], [B_cr])
        tt("dve", ta, lbi, lr, ALU.mult, [B_lbi, B_lr], [B_ta])
        tt("dve", tb, a1, li, ALU.mult, [B_a1, B_li], [B_tb])
        tt("dve", ta, ta, tb, ALU.subtract, [B_ta, B_tb], [B_ta])
        tt("dve", ci, ta, inv, ALU.mult, [B_ta, B_inv], [B_ci])
        Er = alloc(9 * LQ).rearrange("p (k q) -> p k q", k=9)
        Ei = alloc(9 * LQ).rearrange("p (k q) -> p k q", k=9)
        B_E = Buf()
        op("dve", "memset", [], [B_E], Er[:, 0, :], 1.0)
        op("dve", "memset", [], [B_E], Ei[:, 0, :], 0.0)
        op("dve", "tensor_copy", [B_lbr], [B_E], out=Er[:, 1, :], in_=lbr)
        op("dve", "tensor_copy", [B_lbi], [B_E], out=Ei[:, 1, :], in_=lbi)
        for k in range(2, 9):
            tt("dve", ta, Er[:, k - 1, :], lbr, ALU.mult, [B_E, B_lbr], [B_ta])
            tt("dve", tb, Ei[:, k - 1, :], lbi, ALU.mult, [B_E, B_lbi], [B_tb])
            tt("dve", Er[:, k, :], ta, tb, ALU.subtract, [B_ta, B_tb], [B_E])
            tt("dve", ta, Er[:, k - 1, :], lbi, ALU.mult, [B_E, B_lbi], [B_ta])
            tt("dve", tb, Ei[:, k - 1, :], lbr, ALU.mult, [B_E, B_lbr], [B_tb])
            tt("dve", Ei[:, k, :], ta, tb, ALU.add, [B_ta, B_tb], [B_E])
        for _ in range(3):
            csquare(cu, B_cu, su, B_su)
        tabc = alloc(LQ * NCS).rearrange("p (q c) -> p q c", c=NCS)
        tabs = alloc(LQ * NCS).rearrange("p (q c) -> p q c", c=NCS)
        B_tab = Buf()
        op("dve", "tensor_copy", [B_cu], [B_tab], out=tabc[:, :, 0], in_=cu)
        op("dve", "tensor_copy", [B_su], [B_tab], out=tabs[:, :, 0], in_=su)
        tw1 = alloc(LQ * 32).rearrange("p (q c) -> p q c", c=32)
        tw2 = alloc(LQ * 32).rearrange("p (q c) -> p q c", c=32)
        B_tw1, B_tw2 = Buf(), Buf()
        n = 1
        while n < NCS:
            Ac, As = tabc[:, :, 0:n], tabs[:, :, 0:n]
            Bc = tabc[:, :, n - 1:n].to_broadcast([128, LQ, n])
            Bs = tabs[:, :, n - 1:n].to_broadcast([128, LQ, n])
            tt("dve", tw1[:, :, 0:n], Ac, Bc, ALU.mult, [B_tab], [B_tw1])
            tt("dve", tw2[:, :, 0:n], As, Bs, ALU.mult, [B_tab], [B_tw2])
            tt("dve", tabc[:, :, n:2 * n], tw1[:, :, 0:n], tw2[:, :, 0:n], ALU.subtract, [B_tw1, B_tw2], [B_tab])
            tt("dve", tw1[:, :, 0:n], Ac, Bs, ALU.mult, [B_tab], [B_tw1])
            tt("dve", tw2[:, :, 0:n], As, Bc, ALU.mult, [B_tab], [B_tw2])
            tt("dve", tabs[:, :, n:2 * n], tw1[:, :, 0:n], tw2[:, :, 0:n], ALU.add, [B_tw1, B_tw2], [B_tab])
            n *= 2
        for l in range(depth):
            rv = rot_s[l].rearrange("p (r q c) -> p r q c", r=2, q=16)
            dma("sp", rv[:, 0], tabc[:, l * 16:(l + 1) * 16, :], [B_tab], [Buf()], "rot%d" % l)
            dma("sp", rv[:, 1], tabs[:, l * 16:(l + 1) * 16, :], [B_tab], [Buf()], "rot%d" % l)

        def t3(n_, dt=F32):
            return alloc(n_, dt), Buf()

        braw = [t3(256), t3(256)]
        Bm = [t3(512), t3(512)]
        Cm = [t3(512), t3(512)]
        X2 = [t3(256), t3(256)]
        tmpA, B_tmpA = t3(512)
        tmpB, B_tmpB = t3(512)
        EB = [t3(8 * 512), t3(8 * 512)]
        CSt, B_CSt = t3(16 * 8 * 2 * 32, BF16)
        BSt, B_BSt = t3(4 * 8 * 2 * 128, BF16)
        KFt, B_KFt = t3(4 * 8 * 128, BF16)
        Lexp = [t3(16 * 128), t3(16 * 128)]
        Cexp = [t3(16 * 128), t3(16 * 128)]
        for (tl, bl) in Bm + Cm + Lexp + Cexp:
            op("pool", "memset", [], [bl], tl, 0.0)
        CSv = CSt.rearrange("p (q j r c) -> p q j r c", q=16, j=8, r=2)
        BSv = BSt.rearrange("p (b j r c) -> p b j r c", b=4, j=8, r=2)
        KFv = KFt.rearrange("p (b k c) -> p b k c", b=4, k=8)

        def bc32(ap16):
            return ap16.unsqueeze(2).to_broadcast([128, 16, 32])

        for l in range(depth):
            qs = slice(l * 16, (l + 1) * 16)
            for ri, nm in enumerate(("ssm_b_re", "ssm_b_im")):
                dma("sp", braw[ri][0].rearrange("p (q h) -> p q h", h=16),
                    dram[nm][l].rearrange("(q g) p h -> (g p) q h", g=2), [], [braw[ri][1]], "braw%d" % ri)
            crb = cr[:, qs].unsqueeze(2).to_broadcast([128, 16, 16])
            cib = ci[:, qs].unsqueeze(2).to_broadcast([128, 16, 16])
            bre = braw[0][0].rearrange("p (q h) -> p q h", h=16)
            bim = braw[1][0].rearrange("p (q h) -> p q h", h=16)
            tA = tmpA[:, 0:256].rearrange("p (q h) -> p q h", h=16)
            tB = tmpB[:, 0:256].rearrange("p (q h) -> p q h", h=16)
            tC = tmpA[:, 256:512].rearrange("p (q h) -> p q h", h=16)
            for ri in range(2):
                if ri == 0:
                    tt("dve", tA, bre, crb, ALU.mult, [braw[0][1], B_cr], [B_tmpA])
                    tt("dve", tB, bim, cib, ALU.mult, [braw[1][1], B_ci], [B_tmpB])
                    tt("dve", tC, tA, tB, ALU.subtract, [B_tmpA, B_tmpB], [B_tmpA])
                else:
                    tt("dve", tA, bim, crb, ALU.mult, [braw[1][1], B_cr], [B_tmpA])
                    tt("dve", tB, bre, cib, ALU.mult, [braw[0][1], B_ci], [B_tmpB])
                    tt("dve", tC, tA, tB, ALU.add, [B_tmpA, B_tmpB], [B_tmpA])
                bmv = Bm[ri][0].rearrange("p (q g h) -> p q g h", g=2, h=16)
                op("dve", "tensor_copy", [B_tmpA], [Bm[ri][1]], out=bmv[0:64, :, 0, :], in_=tC[0:64])
                op("dve", "tensor_copy", [B_tmpA], [Bm[ri][1]], out=bmv[64:128, :, 1, :], in_=tC[64:128])
            for ri, nm in enumerate(("ssm_c_re", "ssm_c_im")):
                x2v = X2[ri][0].rearrange("p (i c) -> p i c", c=64)
                dma("sp", x2v, dram[nm][l].rearrange("(i g) h p -> (g h) i p", g=8), [], [X2[ri][1]], "x2%d" % ri)
                cmv = Cm[ri][0].rearrange("p (q g h) -> p q g h", g=2, h=16)
                for i in range(4):
                    ps, bps = next_bank()
                    mm(ps[0:64, 0:64], x2v[:, i, :], sel[0], True, True, [X2[ri][1], B_cst], [bps])
                    mm(ps[64:128, 0:64], x2v[:, i, :], sel[1], True, True, [X2[ri][1], B_cst], [bps], tp=(0, 64))
                    pv = ps[:, 0:64].rearrange("p (q h) -> p q h", h=16)
                    op("dve", "tensor_copy", [bps], [Cm[ri][1]], out=cmv[0:64, 4 * i:4 * i + 4, 0, :], in_=pv[0:64])
                    op("dve", "tensor_copy", [bps], [Cm[ri][1]], out=cmv[64:128, 4 * i:4 * i + 4, 1, :], in_=pv[64:128])
            Bmr, Bmi = [Bm[r][0].rearrange("p (q c) -> p q c", c=32) for r in range(2)]
            Cmr, Cmi = [Cm[r][0].rearrange("p (q c) -> p q c", c=32) for r in range(2)]
            tAv = tmpA.rearrange("p (q c) -> p q c", c=32)
            tBv = tmpB.rearrange("p (q c) -> p q c", c=32)
            EBr = EB[0][0].rearrange("p (k q c) -> p k q c", k=8, c=32)
            EBi = EB[1][0].rearrange("p (k q c) -> p k q c", k=8, c=32)
            for k in range(8):
                e = "dve" if k % 2 == 0 else "pool"
                er, ei = bc32(Er[:, k, qs]), bc32(Ei[:, k, qs])
                tt(e, tAv, Bmr, er, ALU.mult, [Bm[0][1], B_E], [B_tmpA])
                tt(e, tBv, Bmi, ei, ALU.mult, [Bm[1][1], B_E], [B_tmpB])
                tt(e, EBr[:, k], tAv, tBv, ALU.subtract, [B_tmpA, B_tmpB], [EB[0][1]])
                tt(e, tAv, Bmi, er, ALU.mult, [Bm[1][1], B_E], [B_tmpA])
                tt(e, tBv, Bmr, ei, ALU.mult, [Bm[0][1], B_E], [B_tmpB])
                tt(e, EBi[:, k], tAv, tBv, ALU.add, [B_tmpA, B_tmpB], [EB[1][1]])
            for j in range(8):
                e = "dve"
                er, ei = bc32(Er[:, j + 1, qs]), bc32(Ei[:, j + 1, qs])
                tt(e, tAv, Cmr, er, ALU.mult, [Cm[0][1], B_E], [B_tmpA])
                tt(e, tBv, Cmi, ei, ALU.mult, [Cm[1][1], B_E], [B_tmpB])
                tt(e, CSv[:, :, j, 0, :], tAv, tBv, ALU.subtract, [B_tmpA, B_tmpB], [B_CSt])
                tt(e, tAv, Cmr, ei, ALU.mult, [Cm[0][1], B_E], [B_tmpA])
                tt(e, tBv, Cmi, er, ALU.mult, [Cm[1][1], B_E], [B_tmpB])
                op(e, "scalar_tensor_tensor", [B_tmpA, B_tmpB], [B_CSt], out=CSv[:, :, j, 1, :], in0=tAv, scalar=-1.0,
                   in1=tBv, op0=ALU.mult, op1=ALU.subtract)
            dma("sp", cst_s[l], CSt, [B_CSt], [Buf()], "cst_s")
            for b in range(4):
                for j0 in range(0, 8, 2):
                    ps, bps = next_bank()
                    for jj in range(2):
                        for ri in range(2):
                            src = EB[ri][0].rearrange("p (k c) -> p k c", k=8)[:, 7 - (j0 + jj), b * 128:(b + 1) * 128]
                            sl = (jj * 2 + ri) * 128
                            mm(ps[:, sl:sl + 128], src, ident, jj == 0 and ri == 0, False, [EB[ri][1], B_cst], [bps])
                    op("act", "activation", [bps], [B_BSt], out=BSv[:, b, j0:j0 + 2].rearrange("p j r c -> p (j r c)"),
                       in_=ps, func=AF.Copy)
            dma("sp", bst_s[l], BSt, [B_BSt], [Buf()], "bst_s")
            for ri in range(2):
                cev = Cexp[ri][0].rearrange("p (b q c) -> p b q c", b=4, q=4)
                cmv4 = Cm[ri][0].rearrange("p (b q c) -> p b q c", b=4, q=4)
                for q4 in range(4):
                    if ri == 0:
                        op("pool", "tensor_copy", [Cm[ri][1]], [Cexp[ri][1]], out=cev[:, :, q4, 32 * q4:32 * q4 + 32],
                           in_=cmv4[:, :, q4, :])
                    else:
                        op("pool", "tensor_scalar", [Cm[ri][1]], [Cexp[ri][1]], out=cev[:, :, q4, 32 * q4:32 * q4 + 32],
                           in0=cmv4[:, :, q4, :], scalar1=-1.0, scalar2=None, op0=ALU.mult)
            for k in range(8):
                for ri in range(2):
                    lev = Lexp[ri][0].rearrange("p (b q c) -> p b q c", b=4, q=4)
                    ebv = EB[ri][0].rearrange("p (k b q c) -> p k b q c", k=8, b=4, q=4)
                    for q4 in range(4):
                        op("pool" if q4 % 2 else "dve", "tensor_copy", [EB[ri][1]], [Lexp[ri][1]],
                           out=lev[:, :, q4, 32 * q4:32 * q4 + 32], in_=ebv[:, k, :, q4, :])
                ps, bps = next_bank()
                first = True
                for b in range(4):
                    for q4 in range(4):
                        for ri in range(2):
                            lq = Lexp[ri][0].rearrange("p (q c) -> p q c", c=128)[:, 4 * b + q4, :]
                            cq = Cexp[ri][0].rearrange("p (q c) -> p q c", c=128)[:, 4 * b + q4, :]
                            mm(ps[:, b * 128:(b + 1) * 128], lq, cq, first, False, [Lexp[ri][1], Cexp[ri][1]], [bps])
                            first = False
                op("act", "activation", [bps], [B_KFt], out=KFv[:, :, k, :], in_=ps.rearrange("p (b c) -> p b c", b=4),
                   func=AF.Copy)
            dma("sp", kfir_s[l], KFt, [B_KFt], [Buf()], "kfir_s")

        dump(RR, 0, [B_RR])
        dump(cr, 64, [B_cr])
        dump(ci, 128, [B_ci])
        dump(Er.rearrange("p k q -> p (k q)"), 192, [B_E])
        dump(Ei.rearrange("p k q -> p (k q)"), 768, [B_E])
        dump(tabc[:, 0:16, :].rearrange("p q c -> p (q c)"), 1344, [B_tab])
        dump(tabs[:, 0:16, :].rearrange("p q c -> p (q c)"), 2368, [B_tab])
        P.barrier(dummies)

        off[0] = persist_end
        h = alloc(KD * TT).rearrange("p (k t) -> p k t", k=KD)
        B_h = [[Buf() for _ in range(NSUB)] for _ in range(KD)]
        xn = alloc(KD * TT, BF16).rearrange("p (k t) -> p k t", k=KD)
        B_xn = [Buf() for _ in range(NSUB)]
        ring = [alloc(RINGW, BF16) for _ in range(NRING)]
        B_ring = [[Buf(), Buf()] for _ in range(NRING)]
        KF = alloc(4 * 8 * 128, BF16)
        BS = alloc(4 * 8 * 2 * 128, BF16)
        CS = alloc(16 * 8 * 2 * 32, BF16)
        ROT = alloc(2 * 16 * 64)
        B_sc = Buf()
        KFm = KF.rearrange("p (b k c) -> p b k c", b=4, k=8)
        BSm = BS.rearrange("p (b j r c) -> p b j r c", b=4, j=8, r=2)
        CSm = CS.rearrange("p (q j r c) -> p q j r c", q=16, j=8, r=2)
        ROTm = ROT.rearrange("p (r q c) -> p r q c", r=2, q=16)
        sq = [alloc(SUB, BF16) for _ in range(4)]
        B_sq = [Buf() for _ in range(4)]
        sq_i = [0]
        rstd = [alloc(SUB) for _ in range(2)]
        B_rstd = [Buf() for _ in range(2)]
        sgt = [alloc(SUB) for _ in range(3)]
        B_sgt = [Buf() for _ in range(3)]
        sgt_i = [0]
        pb = alloc(2 * TT, BF16).rearrange("p (k t) -> p k t", k=2)
        B_pb = Buf()
        u_start = off[0]
        act = alloc(NJ * TT, BF16).rearrange("p (j t) -> p j t", j=NJ)
        B_act = [[Buf() for _ in range(NSUB)] for _ in range(NJ)]
        u_end = off[0]
        off[0] = u_start
        zs_f = alloc(4 * SUB).rearrange("p (b t) -> p b t", b=4)
        B_zsf = [Buf() for _ in range(4)]
        zs_bf = alloc(4 * SUB, BF16).rearrange("p (b t) -> p b t", b=4)
        B_zsb = [Buf() for _ in range(4)]
        zp_f = alloc(4 * (SUB + 16)).rearrange("p (g t) -> p g t", g=4)
        B_zp = [Buf() for _ in range(4)]
        ptmp = [alloc(SUB + 16) for _ in range(3)]
        B_ptmp = [Buf() for _ in range(3)]
        d_bf = alloc(4 * SUB, BF16).rearrange("p (g t) -> p g t", g=4)
        B_dbf = [Buf() for _ in range(4)]
        Vt = [alloc(4 * 2 * NCS).rearrange("p (q r c) -> p q r c", q=4, r=2) for _ in range(2)]
        Wt = [alloc(4 * 2 * NCS).rearrange("p (q r c) -> p q r c", q=4, r=2) for _ in range(2)]
        SFt = [alloc(4 * 2 * (NCS + 1)).rearrange("p (q r c) -> p q r c", q=4, r=2) for _ in range(2)]
        rtmp = [alloc(4 * NCS).rearrange("p (q c) -> p q c", q=4) for _ in range(2)]
        sbf_all = alloc(16 * 2 * NCS, BF16).rearrange("p (q r c) -> p q r c", q=16, r=2)
        B_sbq = [Buf() for _ in range(4)]
        B_V = [Buf() for _ in range(2)]
        B_W = [Buf() for _ in range(2)]
        B_SF = [Buf() for _ in range(2)]
        B_rt = [Buf() for _ in range(2)]
        ys = [alloc(SUB) for _ in range(2)]
        B_ys = [Buf() for _ in range(2)]
        gt = [alloc(SUB) for _ in range(2)]
        B_gt = [Buf() for _ in range(2)]
        gy_bf = alloc(4 * SUB, BF16).rearrange("p (b t) -> p b t", b=4)
        B_gyb = [Buf() for _ in range(4)]
        assert off[0] <= NW
        off[0] = max(off[0], u_end)
        ostage = big[:, u_start:u_start + KD * SUB].rearrange("p (k t) -> p k t", k=KD)
        B_ost = Buf()

        def act_region_bufs():
            r = []
            for row in B_act:
                r.extend(row)
            return r

        mixer_tmp_bufs = (B_zsf + B_zsb + B_zp + B_ptmp + B_dbf + B_V + B_W + B_SF + B_rt + B_sbq + B_ys + B_gt
                          + B_gyb + [B_ost])

        plan = []
        for s in range(n_seq):
            for ti in range(NTILE):
                for l in range(depth):
                    for j in range(NJ):
                        plan.append(("wi", 0, l, j))
                    for m in range(KD):
                        plan.append(("wo", 0, l, m))
                    for sub in range(NSUB):
                        for m2 in range(4):
                            plan.append(("sq", "w_in", l, m2))
                        plan.append(("glu", l))
                        plan.append(("pool", l))
                        for m2 in range(4):
                            plan.append(("sq", "w_out", l, m2))
                    for j in range(NJ):
                        plan.append(("wi", 1, l, j))
                    for m in range(KD):
                        plan.append(("wo", 1, l, m))
                    for m2 in range(4):
                        plan.append(("projm", l, m2))
                        plan.append(("sq", "ple_w_gate", l, m2))
        issued = [0]
        consumed = [0]

        def issue_piece(idx):
            d = plan[idx]
            slot = idx % NRING
            t, B = ring[slot], B_ring[slot]
            key = "ring%d" % slot
            if d[0] == "wi":
                _, f, l, j = d
                w = dram["ffn1_wi" if f == 0 else "ffn2_wi"][l].rearrange("(k p) n -> p k n", p=128)
                tv = t[:, 0:2048].rearrange("p (t k c) -> p t k c", t=2, k=8)
                for tq in range(2):
                    c0 = tq * DFF + j * 128
                    dma("pool", tv[:, tq], w[:, :, c0:c0 + 128], [], [B[tq]], key)
            elif d[0] == "wo":
                _, f, l, m = d
                w = dram["ffn1_wo" if f == 0 else "ffn2_wo"][l].rearrange("(j p) n -> p j n", p=128)
                dma("pool", t[:, 0:NJ * 128].rearrange("p (j c) -> p j c", j=NJ), w[:, :, m * 128:(m + 1) * 128], [], B, key)
            elif d[0] == "sq":
                _, nm, l, m2 = d
                w = dram[nm][l].rearrange("(k p) n -> p k n", p=128)
                dma("pool", t[:, 0:2048].rearrange("p (k c) -> p k c", k=8), w[:, :, m2 * 256:(m2 + 1) * 256], [], B, key)
            elif d[0] == "projm":
                _, l, m2 = d
                w = dram["ple_w_proj"][l].rearrange("(k p) n -> p k n", p=128)
                dma("pool", t[:, 0:512].rearrange("p (k c) -> p k c", k=2), w[:, :, m2 * 256:(m2 + 1) * 256], [], B, key)
            elif d[0] == "glu":
                l = d[1]
                w = dram["ssm_w_glu"][l].rearrange("(k p) n -> p k n", p=128)
                dma("pool", t[:, 0:2048].rearrange("p (k c) -> p k c", k=4), w, [], B, key)
            elif d[0] == "pool":
                l = d[1]
                w = dram["pool_w"][l].rearrange("g p n -> p g n")
                dma("pool", t[:, 0:512].rearrange("p (g c) -> p g c", g=4), w, [], B, key)

        def get_piece(desc):
            idx = consumed[0]
            assert plan[idx] == desc, (plan[idx], desc)
            consumed[0] += 1
            while issued[0] < min(len(plan), idx + NRING - 1):
                issue_piece(issued[0])
                issued[0] += 1
            return ring[idx % NRING], B_ring[idx % NRING]

        def rmsnorm(gain_cols, out_fn):
            for sub in range(NSUB):
                ts = slice(sub * SUB, (sub + 1) * SUB)
                ps, bps = next_bank()
                for k in range(KD):
                    si = sq_i[0] % 4
                    sq_i[0] += 1
                    op("act", "activation", [B_h[k][sub]], [B_sq[si]], out=sq[si], in_=h[:, k, ts], func=AF.Square)
                    mm(ps, ones_bf, sq[si], k == 0, k == KD - 1, [B_ones, B_sq[si]], [bps])
                ri = sub
                op("act", "activation", [bps, B_misc], [B_rstd[ri]], out=rstd[ri], in_=ps, func=AF.Ln, scale=1.0 / D,
                   bias=epsc)
                op("act", "activation", [B_rstd[ri]], [B_rstd[ri]], out=rstd[ri], in_=rstd[ri], func=AF.Exp, scale=-0.5)
                for k in range(KD):
                    out_fn(sub, k, h[:, k, ts], gain_cols[:, k:k + 1], rstd[ri], [B_h[k][sub], B_rstd[ri], B_gains])

        def norm_to_xn(gain_cols):
            def f(sub, k, hk, gcol, rs, reads):
                ts = slice(sub * SUB, (sub + 1) * SUB)
                op("dve", "scalar_tensor_tensor", reads, [B_xn[sub]], out=xn[:, k, ts], in0=hk, scalar=gcol, in1=rs,
                   op0=ALU.mult, op1=ALU.mult)
            rmsnorm(gain_cols, f)

        def ffn(f, l):
            norm_to_xn(gains_v[:, 0 if f == 0 else 2, l, :])
            def wi_group(j, sub, tv, B):
                ts = slice(sub * SUB, (sub + 1) * SUB)
                gps, bg = next_bank()
                for k in range(KD):
                    mm(gps, tv[:, 0, k, :], xn[:, k, ts], k == 0, k == KD - 1, B + [B_xn[sub]], [bg])
                ups, bu = next_bank()
                for k in range(KD):
                    mm(ups, tv[:, 1, k, :], xn[:, k, ts], k == 0, k == KD - 1, B + [B_xn[sub]], [bu])
                si = sgt_i[0] % 3
                sgt_i[0] += 1
                op("act", "activation", [bg], [B_sgt[si]], out=sgt[si], in_=gps, func=AF.Silu)
                op("dve", "tensor_tensor", [B_sgt[si], bu], [B_act[j][sub]], out=act[:, j, ts], in0=ups, in1=sgt[si],
                   op=ALU.mult)

            def wi_piece(j):
                t, B = get_piece(("wi", f, l, j))
                return t[:, 0:2048].rearrange("p (t k c) -> p t k c", t=2, k=8), B

            tv0, B0 = wi_piece(0)
            wi_group(0, 0, tv0, B0)
            tv1, B1 = wi_piece(1)
            wi_group(1, 0, tv1, B1)
            wi_group(0, 1, tv0, B0)
            wi_group(1, 1, tv1, B1)
            for j in range(2, NJ):
                tvj, Bj = wi_piece(j)
                for sub in range(NSUB):
                    wi_group(j, sub, tvj, Bj)
            for m in range(KD):
                t, B = get_piece(("wo", f, l, m))
                for sub in range(NSUB):
                    ts = slice(sub * SUB, (sub + 1) * SUB)
                    yps, by = next_bank()
                    for j in range(NJ):
                        mm(yps, t[:, j * 128:(j + 1) * 128], act[:, j, ts], j == 0, j == NJ - 1, B + [B_act[j][sub]], [by])
                    op("dve", "scalar_tensor_tensor", [by, B_h[m][sub]], [B_h[m][sub]], out=h[:, m, ts], in0=yps, scalar=0.5,
                       in1=h[:, m, ts], op0=ALU.mult, op1=ALU.add)

        def load_ssm_consts(l):
            dma("sp", KF, kfir_s[l], [], [B_sc], "sc")
            dma("sp", BS, bst_s[l], [], [B_sc], "sc")
            dma("sp", CS, cst_s[l], [], [B_sc], "sc")
            dma("sp", ROT, rot_s[l], [], [B_sc], "sc")

        def mixer(l, first_of_seq, sub):
            ts = slice(sub * SUB, (sub + 1) * SUB)
            seq_start = first_of_seq and sub == 0
            for m2 in range(4):
                t, B = get_piece(("sq", "w_in", l, m2))
                tv = t[:, 0:2048].rearrange("p (k c) -> p k c", k=8)
                for mmi in range(2):
                    m = 2 * m2 + mmi
                    zps, bz = next_bank()
                    for k in range(KD):
                        mm(zps, tv[:, k, mmi * 128:(mmi + 1) * 128], xn[:, k, ts], k == 0, k == KD - 1, B + [B_xn[sub]], [bz])
                    if m < 4:
                        op("act", "activation", [bz], [B_zsf[m]], out=zs_f[:, m, :].rearrange("p (j c) -> p c j", c=NCS),
                           in_=zps.rearrange("p (c j) -> p c j", j=TC), func=AF.Copy)
                        op("pool", "tensor_copy", [B_zsf[m]], [B_zsb[m]], out=zs_bf[:, m, :], in_=zs_f[:, m, :])
                    else:
                        g = m - 4
                        if seq_start:
                            op("pool", "memset", [], [B_zp[g]], zp_f[:, g, 0:16], 0.0)
                        else:
                            op("pool", "tensor_copy", [B_hist[l][g]], [B_zp[g]], out=zp_f[:, g, 0:16], in_=hist[:, l, g, :])
                        op("act", "activation", [bz], [B_zp[g]], out=zp_f[:, g, 16:16 + SUB], in_=zps, func=AF.Copy)
            for g, wdw in enumerate(POOL_WINS):
                zz = zp_f[:, g, :]
                cur, Bcur = zz, B_zp[g]
                lo = 0
                sh = 1
                ti_ = 0
                while sh < wdw:
                    nxt, Bn = ptmp[ti_ % 3], B_ptmp[ti_ % 3]
                    ti_ += 1
                    nlo = lo + sh
                    tt("pool", nxt[:, nlo:16 + SUB], cur[:, nlo:16 + SUB], cur[:, nlo - sh:16 + SUB - sh], ALU.add, [Bcur], [Bn])
                    cur, Bcur, lo = nxt, Bn, nlo
                    sh *= 2
                op("dve", "scalar_tensor_tensor", [Bcur, B_zp[g]], [B_dbf[g]], out=d_bf[:, g, :], in0=cur[:, 16:16 + SUB],
                   scalar=1.0 / wdw, in1=zz[:, 16:16 + SUB], op0=ALU.mult, op1=ALU.subtract)
                if seq_start:
                    nxt, Bn = ptmp[ti_ % 3], B_ptmp[ti_ % 3]
                    tt("pool", nxt[:, 0:16], cur[:, 16:32], invc[:, g, :], ALU.mult, [Bcur, B_cst], [Bn])
                    tt("pool", d_bf[:, g, 0:16], nxt[:, 0:16], zz[:, 16:32], ALU.subtract, [Bn, B_zp[g]], [B_dbf[g]])
                op("pool", "tensor_copy", [B_zp[g], Bcur, B_dbf[g]], [B_hist[l][g]], out=hist[:, l, g, :], in_=zp_f[:, g, SUB:SUB + 16])
            zj = zs_bf.rearrange("p b (j c) -> p b j c", c=NCS)
            do_ssm = "nossm" not in flags
            stop_at = 99
            for f_ in flags:
                if f_.startswith("stop"):
                    stop_at = int(f_[4:])
            dd = dbg is not None and l == 0 and sub == 0 and seq_start
            if dd:
                dump(KF, 3392, [B_sc], q="pool")
                dump(BS, 7488, [B_sc], q="pool")
                dump(CS, 15680, [B_sc], q="pool")
                dump(zs_f.rearrange("p b t -> p (b t)"), 23872, B_zsf)
                dump(ROT, 25920, [B_sc])
            ROTq = ROTm.rearrange("p r (b q) c -> p r b q c", q=4)
            carry_q = carry_v.rearrange("p l (b q) r -> p l b q r", q=4)
            sbq = sbf_all.rearrange("p (b q) r c -> p b q r c", q=4)
            if do_ssm:
                Sb = [next_bank() for _ in range(4)]
                firsts = [True] * 4
                for b in range(4):
                    for ri in range(2):
                        for j in range(TC):
                            for q4 in range(4):
                                sps, bs = Sb[q4]
                                Sv = sps.rearrange("p (b r c) -> p b r c", b=4, r=2)
                                mm(Sv[:, b, ri, :], BSm[32 * q4:32 * q4 + 32, b, j, ri, :], zj[32 * q4:32 * q4 + 32, b, j, :],
                                   firsts[q4], False, [B_sc, B_zsb[b]], [bs], tp=(32 * q4, 0))
                                firsts[q4] = False
            Yb = []
            for b in range(4 if do_ssm else 0):
                yps, by = next_bank()
                Yb.append((yps, by))
                mm(yps, KFm[:, b, 0, :], zs_bf[:, b, :], True, False, [B_sc, B_zsb[b]], [by])
                for k in range(1, TC):
                    mm(yps[:, k * NCS:SUB], KFm[:, b, k, :], zs_bf[:, b, 0:(TC - k) * NCS], False, False, [B_sc, B_zsb[b]], [by])
            for q4 in range(4 if (do_ssm and stop_at > 1) else 0):
                par = q4 % 2
                V, W, SF, RT = Vt[par], Wt[par], SFt[par], rtmp[par]
                sps, bs = Sb[q4]
                Sv = sps.rearrange("p (b r c) -> p b r c", b=4, r=2)
                rc = ROTq[:, 0, :, q4, :]
                rs_ = ROTq[:, 1, :, q4, :]
                Bc = B_carry[l][q4]
                tt("dve", RT, Sv[:, :, 1, :], rs_, ALU.mult, [bs, B_sc], [B_rt[par]])
                tt("dve", V[:, :, 0, :], Sv[:, :, 0, :], rc, ALU.mult, [bs, B_sc], [B_V[par]])
                tt("dve", V[:, :, 0, :], V[:, :, 0, :], RT, ALU.add, [B_V[par], B_rt[par]], [B_V[par]])
                tt("dve", RT, Sv[:, :, 0, :], rs_, ALU.mult, [bs, B_sc], [B_rt[par]])
                tt("dve", V[:, :, 1, :], Sv[:, :, 1, :], rc, ALU.mult, [bs, B_sc], [B_V[par]])
                tt("dve", V[:, :, 1, :], V[:, :, 1, :], RT, ALU.subtract, [B_V[par], B_rt[par]], [B_V[par]])
                if stop_at <= 2:
                    continue
                if seq_start:
                    op("dve", "memset", [], [Bc], carry_q[:, l, :, q4, :], 0.0)
                for b in range(4):
                    for ri in range(2):
                        q = 4 * b + q4
                        op("dve", "tensor_tensor_scan", [B_V[par], B_RR, Bc], [B_W[par]], out=W[:, b, ri, :],
                           data0=RR[:, l * 16 + q:l * 16 + q + 1].to_broadcast([128, NCS]), data1=V[:, b, ri, :],
                           initial=carry_v[:, l, q, ri:ri + 1], op0=ALU.mult, op1=ALU.add)
                if stop_at <= 3:
                    continue
                tt("pool", RT, W[:, :, 1, :], rs_, ALU.mult, [B_W[par], B_sc], [B_rt[par]])
                tt("pool", SF[:, :, 0, 1:NCS + 1], W[:, :, 0, :], rc, ALU.mult, [B_W[par], B_sc], [B_SF[par]])
                tt("pool", SF[:, :, 0, 1:NCS + 1], SF[:, :, 0, 1:NCS + 1], RT, ALU.subtract, [B_SF[par], B_rt[par]], [B_SF[par]])
                tt("pool", RT, W[:, :, 0, :], rs_, ALU.mult, [B_W[par], B_sc], [B_rt[par]])
                tt("pool", SF[:, :, 1, 1:NCS + 1], W[:, :, 1, :], rc, ALU.mult, [B_W[par], B_sc], [B_SF[par]])
                tt("pool", SF[:, :, 1, 1:NCS + 1], SF[:, :, 1, 1:NCS + 1], RT, ALU.add, [B_SF[par], B_rt[par]], [B_SF[par]])
                op("pool", "tensor_copy", [Bc], [B_SF[par]], out=SF[:, :, :, 0], in_=carry_q[:, l, :, q4, :])
                op("pool", "tensor_copy", [B_SF[par]], [Bc], out=carry_q[:, l, :, q4, :], in_=SF[:, :, :, NCS])
                op("act", "activation", [B_SF[par]], [B_sbq[q4]], out=sbq[:, :, q4, :, :], in_=SF[:, :, :, 0:NCS], func=AF.Copy)
                if dd and q4 == 0:
                    dump(V.rearrange("p b r c -> p (b r c)"), 27968, [B_V[par]])
                    dump(W.rearrange("p b r c -> p (b r c)"), 28480, [B_W[par]])
                    dump(SF.rearrange("p b r c -> p (b r c)"), 28992, [B_SF[par]])
            for b in range(4 if (do_ssm and stop_at > 4) else 0):
                par = b % 2
                yps, by = Yb[b]
                Yj = yps.rearrange("p (j c) -> p j c", c=NCS)
                for j in range(TC if stop_at > 5 else 0):
                    for q4 in range(4):
                        for ri in range(2):
                            mm(Yj[32 * q4:32 * q4 + 32, j, :], CSm[:, 4 * b + q4, j, ri, :], sbf_all[:, 4 * b + q4, ri, :], False,
                               (j == TC - 1 and q4 == 3 and ri == 1), [B_sc] + B_sbq, [by], tp=(0, 32 * q4))
                if stop_at <= 6:
                    continue
                Y, G = ys[par], gt[par]
                op("dve", "tensor_scalar", [B_zsf[b], B_gains], [B_gt[par]], out=G, in0=zs_f[:, b, :],
                   scalar1=dsk[:, l * 4 + b:l * 4 + b + 1], scalar2=None, op0=ALU.mult)
                if stop_at <= 7:
                    continue
                tt("dve", Y, yps, G, ALU.add, [by, B_gt[par]], [B_ys[par]])
                if dd and b == 0:
                    dump(Y, 29512, [B_ys[par]])
                if stop_at <= 8:
                    continue
                op("act", "activation", [B_ys[par]], [B_gt[par]], out=G, in_=Y, func=AF.Square, scale=math.sqrt(0.044715))
                if stop_at <= 9:
                    continue
                op("dve", "scalar_tensor_tensor", [B_gt[par], B_ys[par]], [B_gt[par]], out=G, in0=G, scalar=1.0, in1=Y,
                   op0=ALU.add, op1=ALU.mult)
                if stop_at <= 10:
                    continue
                op("act", "activation", [B_gt[par]], [B_gt[par]], out=G, in_=G, func=AF.Sigmoid, scale=1.5957691216057308)
                if stop_at <= 11:
                    continue
                op("dve", "tensor_tensor", [B_gt[par], B_ys[par]], [B_zsf[b]], out=zs_f[:, b, :], in0=G, in1=Y, op=ALU.mult)
                if stop_at <= 12:
                    continue
                op("act", "activation", [B_zsf[b]], [B_gyb[b]], out=gy_bf[:, b, :], in_=zs_f[:, b, :], func=AF.Copy)
            tg, Bg = get_piece(("glu", l))
            tgv = tg[:, 0:2048].rearrange("p (k c) -> p k c", k=4)
            for n_ in range(4 if "nossm" not in flags else 0):
                gps, bg = next_bank()
                for b in range(4):
                    mm(gps, tgv[:, b, n_ * 128:(n_ + 1) * 128], gy_bf[:, b, :], b == 0, b == 3, Bg + [B_gyb[b]], [bg])
                si = sgt_i[0] % 3
                sgt_i[0] += 1
                op("act", "activation", [bg], [B_sgt[si]], out=sgt[si], in_=gps, func=AF.Sigmoid)
                op("dve", "tensor_tensor", [B_sgt[si], B_zsf[n_]], [B_xn[sub]], out=xn[:, n_, ts].rearrange("p (c j) -> p c j", j=TC),
                   in0=sgt[si].rearrange("p (j c) -> p c j", c=NCS), in1=zs_f[:, n_, :].rearrange("p (j c) -> p c j", c=NCS),
                   op=ALU.mult)
            if "nossm" in flags:
                for n_ in range(4):
                    op("dve", "tensor_copy", [B_zsb[n_]], [B_xn[sub]], out=xn[:, n_, ts], in_=zs_bf[:, n_, :])
            tp_, Bp = get_piece(("pool", l))
            tpv = tp_[:, 0:512].rearrange("p (g c) -> p g c", g=4)
            for g, wdw in enumerate(POOL_WINS):
                pps, bp = next_bank()
                mm(pps, tpv[:, g, :], d_bf[:, g, :], True, True, Bp + [B_dbf[g]], [bp])
                op("dve", "tensor_scalar", [bp, B_gains], [B_xn[sub]], out=xn[:, 4 + g, ts], in0=pps,
                   scalar1=psc[:, l * 4 + g:l * 4 + g + 1], scalar2=None, op0=ALU.mult)
            if dbg is not None and l == 0 and sub == 0 and seq_start:
                for k_ in range(8):
                    dump(xn[:, k_, ts], 30024 + 512 * k_, [B_xn[sub]], q="pool")
            for m2 in range(4):
                t, B = get_piece(("sq", "w_out", l, m2))
                tv = t[:, 0:2048].rearrange("p (k c) -> p k c", k=8)
                for mmi in range(2):
                    m = 2 * m2 + mmi
                    ops_, bo = next_bank()
                    for k in range(KD):
                        mm(ops_, tv[:, k, mmi * 128:(mmi + 1) * 128], xn[:, k, ts], k == 0, k == KD - 1, B + [B_xn[sub]], [bo])
                    tt("dve", h[:, m, ts], ops_, h[:, m, ts], ALU.add, [bo, B_h[m][sub]], [B_h[m][sub]])

        def ple(l, tok0):
            norm_to_xn(gains_v[:, 3, l, :])
            dma("pool", pb, dram["pT"][l].rearrange("(k p) n -> p k n", p=128)[:, :, tok0:tok0 + TT], [], [B_pb], "pb")
            for m2 in range(4):
                tpj, Bpj = get_piece(("projm", l, m2))
                tpjv = tpj[:, 0:512].rearrange("p (k c) -> p k c", k=2)
                t, B = get_piece(("sq", "ple_w_gate", l, m2))
                tv = t[:, 0:2048].rearrange("p (k c) -> p k c", k=8)
                for sub in range(NSUB):
                    for mmi in range(2):
                        m = 2 * m2 + mmi
                        ts = slice(sub * SUB, (sub + 1) * SUB)
                        gps, bg = next_bank()
                        for k in range(KD):
                            mm(gps, tv[:, k, mmi * 128:(mmi + 1) * 128], xn[:, k, ts], k == 0, k == KD - 1, B + [B_xn[sub]], [bg])
                        pps, bp = next_bank()
                        for k in range(2):
                            mm(pps, tpjv[:, k, mmi * 128:(mmi + 1) * 128], pb[:, k, ts], k == 0, k == 1, Bpj + [B_pb], [bp])
                        si = sgt_i[0] % 3
                        sgt_i[0] += 1
                        if "ple_mm" in flags:
                            op("dve", "tensor_copy", [bg], [B_sgt[si]], out=sgt[si], in_=gps)
                            op("dve", "tensor_copy", [bp], [B_sgt[si]], out=sgt[si], in_=pps)
                            continue
                        op("act", "activation", [bg], [B_sgt[si]], out=sgt[si], in_=gps, func=AF.Sigmoid)
                        tt("dve", sgt[si], pps, sgt[si], ALU.mult, [B_sgt[si], bp], [B_sgt[si]])
                        tt("dve", h[:, m, ts], sgt[si], h[:, m, ts], ALU.add, [B_sgt[si], B_h[m][sub]], [B_h[m][sub]])

        out_dmas = []
        if "prolog" in flags:
            plan = []
            op("dve", "memset", [], [B_ost], ostage, 0.0)
            out_dmas.append(dma("sp", outT.rearrange("(k p) n -> p k n", p=128)[:, :, 0:SUB], ostage, [B_ost], [B_ost], "ost"))
        for s in range(n_seq if "prolog" not in flags else 0):
            for ti in range(NTILE):
                tok0 = s * seq_len + ti * TT
                for k in range(KD):
                    dma("sp", h[:, k, :], dram["xT"][k * 128:(k + 1) * 128, tok0:tok0 + TT], [], B_h[k], "hx%d" % k)
                for l in range(depth):
                    load_ssm_consts(l)
                    if "noffn" not in flags:
                        ffn(0, l)
                    else:
                        for _ in range(NJ + KD):
                            get_piece(plan[consumed[0]])
                    norm_to_xn(gains_v[:, 1, l, :])
                    guard = act_region_bufs()
                    op("pool", "memset", [], guard + mixer_tmp_bufs, dummies["pool"], 0.0)
                    for sub in range(NSUB):
                        if "nomix" in flags:
                            for _ in range(10):
                                get_piece(plan[consumed[0]])
                        else:
                            mixer(l, ti == 0, sub)
                    op("pool", "memset", [], guard + mixer_tmp_bufs, dummies["pool"], 0.0)
                    if "noffn" not in flags:
                        ffn(1, l)
                    else:
                        for _ in range(NJ + KD):
                            get_piece(plan[consumed[0]])
                    if "nople" in flags:
                        for _ in range(8):
                            get_piece(plan[consumed[0]])
                    else:
                        ple(l, tok0)
                op("pool", "memset", [], act_region_bufs() + mixer_tmp_bufs, dummies["pool"], 0.0)

                def fo(sub, k, hk, gcol, rs, reads):
                    op("dve", "scalar_tensor_tensor", reads + [B_ost], [B_ost], out=ostage[:, k, :], in0=hk, scalar=gcol, in1=rs,
                       op0=ALU.mult, op1=ALU.mult)
                    if k == KD - 1:
                        t0 = tok0 + sub * SUB
                        d_ = dma("sp", outT.rearrange("(k p) n -> p k n", p=128)[:, :, t0:t0 + SUB], ostage, [B_ost], [B_ost], "ost")
                        out_dmas.append(d_)
                rmsnorm(gfin, fo)
        assert consumed[0] == len(plan), (consumed[0], len(plan))
        P.emit(final_waits=out_dmas + dbg_dmas)
    return nc


_CACHE = {}


def _get_nc(n_seq, seq_len, depth, flags=frozenset()):
    key = (n_seq, seq_len, depth, flags)
    if key not in _CACHE:
        _CACHE[key] = build(n_seq, seq_len, depth, flags)
    return _CACHE[key]


def kernel(**inputs):
    x = np.asarray(inputs["x"], dtype=np.float32)
    p = np.asarray(inputs["p"], dtype=np.float32)
    B, L, _ = x.shape
    ncores = 8
    n_seq = B // ncores
    nc = _get_nc(n_seq, L, DEPTH)
    consts = host_consts()
    wmap = {name: np.ascontiguousarray(np.asarray(inputs[name], dtype=np.float32)) for name, _ in WSHAPES}
    in_maps = []
    for c in range(ncores):
        xs = x[c * n_seq:(c + 1) * n_seq].reshape(n_seq * L, D)
        ps_ = p[:, c * n_seq:(c + 1) * n_seq].reshape(DEPTH, n_seq * L, 256)
        m = {"xT": np.ascontiguousarray(xs.T), "pT": np.ascontiguousarray(ps_.transpose(0, 2, 1)), "consts": consts}
        m.update(wmap)
        in_maps.append(m)
    res = run_bass_kernel_spmd(nc, in_maps, core_ids=list(range(ncores)))
    out = np.empty((B, L, D), np.float32)
    for c in range(ncores):
        o = np.asarray(res.results[c]["outT"])
        out[c * n_seq:(c + 1) * n_seq] = o.T.reshape(n_seq, L, D)
    return out
```
